# Optimizing a Trainium2 kernel written in Bass

```python
import jax
import jax.numpy as jnp
from jax import lax
import numpy as np

D_MODEL = 1024
BATCH = 2
SEQ = 8192
DEPTH = 2

CTX_LEN = 256
GRID_W = 64
HEAD_DIM = 64
FOURIER_W = 256
FOURIER_GROUPS = 4
ATTN_HEADS = 6
ATTN_KV_HEADS = 2
ATTN_GROUP = ATTN_HEADS // ATTN_KV_HEADS
ATTN_W = ATTN_HEADS * HEAD_DIM
KV_W = ATTN_KV_HEADS * HEAD_DIM
RWKV_HEADS = 6
RWKV_W = RWKV_HEADS * HEAD_DIM
MIX_W = FOURIER_W + ATTN_W + RWKV_W
WINDOW = 128
BLOCK = 128
ROPE_BASE = 10000.0
DECAY_LORA = 64
ICLR_LORA = 64
GATE_LORA = 128
CONV_W = 3
FFN_DIM = 2752
N_MOD = 9
NORM_EPS = 1e-6
GN_EPS = 64e-5
IN_WIDTHS = (FOURIER_W, ATTN_W, KV_W, KV_W, RWKV_W, RWKV_W, RWKV_W, DECAY_LORA, DECAY_LORA, ICLR_LORA, ICLR_LORA, GATE_LORA)
IN_W = FOURIER_W + ATTN_W + 2 * KV_W + 3 * RWKV_W + 2 * DECAY_LORA + 2 * ICLR_LORA + GATE_LORA

kernel_name = 'hybrid_fourier_swa_rwkv7_prefix_dit'


def rms_norm(x, g):
    xf = x.astype(jnp.float32)
    y = xf * lax.rsqrt(jnp.mean(xf * xf, axis=-1, keepdims=True) + NORM_EPS)
    return (y * g.astype(jnp.float32)).astype(x.dtype)


def ada_mod(cond, w, b):
    m = jax.nn.silu(cond) @ w + b
    return m.reshape(cond.shape[:-1] + (N_MOD, D_MODEL))


def swiglu(h, wi, wo):
    gate, up = jnp.split(h @ wi, 2, axis=-1)
    return (jax.nn.silu(gate) * up) @ wo


def ffn_half(x, m, g_pre, g_post, wi, wo):
    h = rms_norm(x, g_pre) * (1.0 + m[:, 1][:, None]) + m[:, 0][:, None]
    return x + 0.5 * m[:, 2][:, None] * rms_norm(swiglu(h, wi, wo), g_post)


def split_cols(z):
    parts = []
    off = 0
    for w in IN_WIDTHS:
        parts.append(z[..., off:off + w])
        off += w
    return parts


def axial_rope(n_tokens):
    rows_n = n_tokens // GRID_W
    row = jnp.repeat(jnp.arange(rows_n), GRID_W).astype(jnp.float32)
    col = jnp.tile(jnp.arange(GRID_W), rows_n).astype(jnp.float32)
    n_freq = HEAD_DIM // 4
    inv = ROPE_BASE ** (-jnp.arange(n_freq, dtype=jnp.float32) / n_freq)
    ang = jnp.concatenate([row[:, None] * inv, col[:, None] * inv], axis=-1)
    return jnp.cos(ang), jnp.sin(ang)


def apply_rope(x, cos, sin):
    xf = x.astype(jnp.float32)
    half = HEAD_DIM // 2
    x1, x2 = xf[..., :half], xf[..., half:]
    c = cos[None, :, None, :]
    s = sin[None, :, None, :]
    return jnp.concatenate([x1 * c - x2 * s, x2 * c + x1 * s], axis=-1).astype(x.dtype)


def fourier_mix(z):
    b, n = z.shape[:2]
    zf = z.astype(jnp.float32).reshape(b, n, FOURIER_GROUPS, FOURIER_W // FOURIER_GROUPS)
    y = jnp.fft.fft2(zf, axes=(1, 3), norm='ortho').real
    return y.reshape(b, n, FOURIER_W).astype(z.dtype)


def window_attention(q, k, v, kc, vc, sink):
    b, s = q.shape[:2]
    nb = s // BLOCK
    f32 = jnp.float32
    qb = q.astype(f32).reshape(b, nb, BLOCK, ATTN_KV_HEADS, ATTN_GROUP, HEAD_DIM) * (HEAD_DIM ** -0.5)
    pad = ((0, 0), (BLOCK, BLOCK), (0, 0), (0, 0))
    kp = jnp.pad(k.astype(f32), pad).reshape(b, nb + 2, BLOCK, ATTN_KV_HEADS, HEAD_DIM)
    vp = jnp.pad(v.astype(f32), pad).reshape(b, nb + 2, BLOCK, ATTN_KV_HEADS, HEAD_DIM)
    kw = jnp.concatenate([kp[:, :-2], kp[:, 1:-1], kp[:, 2:]], axis=2)
    vw = jnp.concatenate([vp[:, :-2], vp[:, 1:-1], vp[:, 2:]], axis=2)
    s_w = jnp.einsum('bnqhgd,bnkhd->bnhgqk', qb, kw)
    s_c = jnp.einsum('bnqhgd,bchd->bnhgqc', qb, kc.astype(f32))
    qpos = jnp.arange(BLOCK)[:, None]
    kpos = jnp.arange(3 * BLOCK)[None, :] - BLOCK
    abs_k = jnp.arange(nb)[:, None, None] * BLOCK + kpos[None]
    valid = (jnp.abs(kpos - qpos) <= WINDOW)[None] & (abs_k >= 0) & (abs_k < s)
    s_w = jnp.where(valid[None, :, None, None], s_w, -jnp.inf)
    sk = sink.astype(f32).reshape(ATTN_KV_HEADS, ATTN_GROUP)[None, None, :, :, None]
    m = jnp.maximum(jnp.maximum(s_w.max(-1), s_c.max(-1)), sk)
    e_w = jnp.exp(s_w - m[..., None])
    e_c = jnp.exp(s_c - m[..., None])
    den = e_w.sum(-1) + e_c.sum(-1) + jnp.exp(sk - m)
    o = jnp.einsum('bnhgqk,bnkhd->bnqhgd', e_w, vw) + jnp.einsum('bnhgqc,bchd->bnqhgd', e_c, vc.astype(f32))
    o = o / jnp.moveaxis(den, -1, 2)[..., None]
    return o.reshape(b, s, ATTN_W).astype(q.dtype)


def context_attention(qc, kc, vc, sink):
    b, n = qc.shape[:2]
    f32 = jnp.float32
    q = qc.astype(f32).reshape(b, n, ATTN_KV_HEADS, ATTN_GROUP, HEAD_DIM) * (HEAD_DIM ** -0.5)
    s = jnp.einsum('bqhgd,bkhd->bhgqk', q, kc.astype(f32))
    sk = sink.astype(f32).reshape(ATTN_KV_HEADS, ATTN_GROUP)[None, :, :, None]
    m = jnp.maximum(s.max(-1), sk)
    e = jnp.exp(s - m[..., None])
    den = e.sum(-1) + jnp.exp(sk - m)
    o = jnp.einsum('bhgqk,bkhd->bqhgd', e, vc.astype(f32)) / jnp.moveaxis(den, -1, 1)[..., None]
    return o.reshape(b, n, ATTN_W).astype(qc.dtype)


def short_conv(x, w):
    n = x.shape[1]
    half = CONV_W // 2
    xp = jnp.pad(x, ((0, 0), (half, half), (0, 0)))
    y = xp[:, 0:n] * w[0]
    for j in range(1, CONV_W):
        y = y + xp[:, j:j + n] * w[j]
    return y


def wkv_step(state, inp):
    r, w, k, v, a, b = inp
    sa = jnp.einsum('zbhij,zbhj->zbhi', state, a)
    state = state * w[..., None, :] + sa[..., None] * b[..., None, :] + v[..., None] * k[..., None, :]
    y = jnp.einsum('zbhij,zbhj->zbhi', state, r)
    return state, y


def rwkv_scan(r, k, v, zw, za, state0, conv_w, w0, w2, a0, a2, k_k, k_a):
    b, n = r.shape[:2]
    f32 = jnp.float32
    rkv = short_conv(jnp.concatenate([r, k, v], axis=-1), conv_w).astype(f32)
    r, k, v = jnp.split(rkv, 3, axis=-1)
    w = -jax.nn.softplus(-(w0.astype(f32)[:, None, None, :] + jnp.einsum('zbnr,zrc->zbnc', jnp.tanh(zw.astype(f32)), w2.astype(f32)))) - 0.5
    decay = jnp.exp(-jnp.exp(w))
    a = jax.nn.sigmoid(a0.astype(f32)[:, None, None, :] + jnp.einsum('zbnr,zrc->zbnc', za.astype(f32), a2.astype(f32)))
    kk = (k * k_k.astype(f32)).reshape(b, n, RWKV_HEADS, HEAD_DIM)
    kk = (kk * lax.rsqrt(jnp.maximum(jnp.sum(kk * kk, -1, keepdims=True), 1e-24))).reshape(b, n, RWKV_W)
    k_dir = k * (1.0 + (a - 1.0) * k_a.astype(f32))
    seqs = [jnp.broadcast_to(r, a.shape), decay, k_dir, jnp.broadcast_to(v, a.shape), jnp.broadcast_to(-kk, a.shape), kk * a]

    def time_major(t):
        t = jnp.stack([t[0], jnp.flip(t[1], axis=1)])
        return jnp.moveaxis(t.reshape(2, b, n, RWKV_HEADS, HEAD_DIM), 2, 0)

    s_fin, y = lax.scan(wkv_step, state0, [time_major(t) for t in seqs])
    y = jnp.moveaxis(y, 0, 2)
    y = y[0] + jnp.flip(y[1], axis=1)
    return y.reshape(b, n, RWKV_W), s_fin, r, k, v


def rwkv_readout(y, r, k, v, zg, g2, r_k, ln_g, ln_b, out_dtype):
    b, n = y.shape[:2]
    f32 = jnp.float32
    yh = y.reshape(b, n, RWKV_HEADS, HEAD_DIM)
    mu = yh.mean(-1, keepdims=True)
    var = jnp.mean(jnp.square(yh - mu), -1, keepdims=True)
    yh = (yh - mu) * lax.rsqrt(var + GN_EPS)
    yh = yh * ln_g.astype(f32).reshape(RWKV_HEADS, HEAD_DIM) + ln_b.astype(f32).reshape(RWKV_HEADS, HEAD_DIM)
    rh = r.reshape(b, n, RWKV_HEADS, HEAD_DIM)
    kh = k.reshape(b, n, RWKV_HEADS, HEAD_DIM)
    bonus = jnp.sum(rh * kh * r_k.astype(f32), -1, keepdims=True) * v.reshape(b, n, RWKV_HEADS, HEAD_DIM)
    gate = jax.nn.sigmoid(zg.astype(f32)) @ g2.astype(f32)
    return ((yh + bonus).reshape(b, n, RWKV_W) * gate).astype(out_dtype)


def token_mixer(hl, hc, w_in, w_out, sink, conv_w, w0, w2, a0, a2, g2, k_k, k_a, r_k, ln_g, ln_b, need_ctx_out):
    b, s = hl.shape[:2]
    n_c = hc.shape[1]
    f_l, q_l, k_l, v_l, rr_l, rk_l, rv_l, wf_l, wb_l, af_l, ab_l, g_l = split_cols(hl @ w_in)
    f_c, q_c, k_c, v_c, rr_c, rk_c, rv_c, wf_c, wb_c, af_c, ab_c, g_c = split_cols(hc @ w_in)
    cos, sin = axial_rope(s)
    q = apply_rope(q_l.reshape(b, s, ATTN_HEADS, HEAD_DIM), cos, sin)
    k = apply_rope(k_l.reshape(b, s, ATTN_KV_HEADS, HEAD_DIM), cos, sin)
    v = v_l.reshape(b, s, ATTN_KV_HEADS, HEAD_DIM)
    kc = k_c.reshape(b, n_c, ATTN_KV_HEADS, HEAD_DIM)
    vc = v_c.reshape(b, n_c, ATTN_KV_HEADS, HEAD_DIM)
    attn_l = window_attention(q, k, v, kc, vc, sink)
    state0 = jnp.zeros((2, b, RWKV_HEADS, HEAD_DIM, HEAD_DIM), jnp.float32)
    y_c, state_c, r_c, k_c2, v_c2 = rwkv_scan(rr_c, rk_c, rv_c, jnp.stack([wf_c, wb_c]), jnp.stack([af_c, ab_c]), state0, conv_w, w0, w2, a0, a2, k_k, k_a)
    y_l, _, r_l, k_l2, v_l2 = rwkv_scan(rr_l, rk_l, rv_l, jnp.stack([wf_l, wb_l]), jnp.stack([af_l, ab_l]), state_c, conv_w, w0, w2, a0, a2, k_k, k_a)
    rwkv_l = rwkv_readout(y_l, r_l, k_l2, v_l2, g_l, g2, r_k, ln_g, ln_b, hl.dtype)
    out_l = jnp.concatenate([fourier_mix(f_l), attn_l, rwkv_l], axis=-1) @ w_out
    if not need_ctx_out:
        return out_l, None
    attn_c = context_attention(q_c.reshape(b, n_c, ATTN_HEADS, HEAD_DIM), kc, vc, sink)
    rwkv_c = rwkv_readout(y_c, r_c, k_c2, v_c2, g_c, g2, r_k, ln_g, ln_b, hc.dtype)
    out_c = jnp.concatenate([fourier_mix(f_c), attn_c, rwkv_c], axis=-1) @ w_out
    return out_l, out_c


def setup_inputs(seed: int = 0) -> dict:
    key = jax.random.key(seed)
    ks = jax.random.split(key, 32)
    L, D, F = DEPTH, D_MODEL, FFN_DIM
    f32 = jnp.float32

    def nrm(i, shape, scale):
        return jax.random.normal(ks[i], shape, f32) * scale

    return {
        'x': nrm(0, (BATCH, SEQ, D), 1.0),
        'c': nrm(1, (BATCH, D), 1.0),
        'ctx': nrm(2, (BATCH, CTX_LEN, D), 1.0),
        'c_ctx': nrm(3, (D,), 1.0),
        'mod_w': nrm(4, (L, D, N_MOD * D), 0.5 * D ** -0.5),
        'mod_b': nrm(5, (L, N_MOD * D), 0.02),
        'norm_g': 1.0 + nrm(6, (L, 6, D), 0.05),
        'ffn1_wi': nrm(7, (L, D, 2 * F), D ** -0.5),
        'ffn1_wo': nrm(8, (L, F, D), F ** -0.5),
        'mix_w_in': nrm(9, (L, D, IN_W), D ** -0.5),
        'mix_w_out': nrm(10, (L, MIX_W, D), MIX_W ** -0.5),
        'attn_sink': nrm(11, (L, ATTN_HEADS), 0.5),
        'rwkv_conv': nrm(12, (L, CONV_W, 3 * RWKV_W), 0.2).at[:, CONV_W // 2].add(1.0),
        'rwkv_w0': jax.random.uniform(ks[13], (L, 2, RWKV_W), f32, -6.0, 1.0),
        'rwkv_w2': nrm(14, (L, 2, DECAY_LORA, RWKV_W), 0.1),
        'rwkv_a0': nrm(15, (L, 2, RWKV_W), 0.5),
        'rwkv_a2': nrm(16, (L, 2, ICLR_LORA, RWKV_W), 0.5 * ICLR_LORA ** -0.5),
        'rwkv_g2': nrm(17, (L, GATE_LORA, RWKV_W), GATE_LORA ** -0.5),
        'rwkv_k_k': 0.85 + nrm(18, (L, RWKV_W), 0.05),
        'rwkv_k_a': 1.0 + nrm(19, (L, RWKV_W), 0.05),
        'rwkv_r_k': nrm(20, (L, RWKV_HEADS, HEAD_DIM), 0.1),
        'rwkv_ln_g': 1.0 + nrm(21, (L, RWKV_W), 0.05),
        'rwkv_ln_b': nrm(22, (L, RWKV_W), 0.02),
        'ffn2_wi': nrm(23, (L, D, 2 * F), D ** -0.5),
        'ffn2_wo': nrm(24, (L, F, D), F ** -0.5),
    }


def reference(x, c, ctx, c_ctx, mod_w, mod_b, norm_g, ffn1_wi, ffn1_wo, mix_w_in, mix_w_out, attn_sink,
              rwkv_conv, rwkv_w0, rwkv_w2, rwkv_a0, rwkv_a2, rwkv_g2, rwkv_k_k, rwkv_k_a, rwkv_r_k,
              rwkv_ln_g, rwkv_ln_b, ffn2_wi, ffn2_wo):
    xl, xc = x, ctx
    for li in range(DEPTH):
        need_ctx_out = li < DEPTH - 1
        ml = ada_mod(c, mod_w[li], mod_b[li])
        mc = ada_mod(c_ctx, mod_w[li], mod_b[li])[None]
        g = norm_g[li]
        xl = ffn_half(xl, ml[:, 0:3], g[0], g[1], ffn1_wi[li], ffn1_wo[li])
        xc = ffn_half(xc, mc[:, 0:3], g[0], g[1], ffn1_wi[li], ffn1_wo[li])
        hl = rms_norm(xl, g[2]) * (1.0 + ml[:, 4][:, None]) + ml[:, 3][:, None]
        hc = rms_norm(xc, g[2]) * (1.0 + mc[:, 4][:, None]) + mc[:, 3][:, None]
        out_l, out_c = token_mixer(hl, hc, mix_w_in[li], mix_w_out[li], attn_sink[li], rwkv_conv[li],
                                   rwkv_w0[li], rwkv_w2[li], rwkv_a0[li], rwkv_a2[li], rwkv_g2[li],
                                   rwkv_k_k[li], rwkv_k_a[li], rwkv_r_k[li], rwkv_ln_g[li], rwkv_ln_b[li],
                                   need_ctx_out)
        xl = xl + ml[:, 5][:, None] * rms_norm(out_l, g[3])
        if need_ctx_out:
            xc = xc + mc[:, 5][:, None] * rms_norm(out_c, g[3])
            xc = ffn_half(xc, mc[:, 6:9], g[4], g[5], ffn2_wi[li], ffn2_wo[li])
        xl = ffn_half(xl, ml[:, 6:9], g[4], g[5], ffn2_wi[li], ffn2_wo[li])
    return xl
```

```python
import numpy as np
import concourse.bass as bass
import concourse.mybir as mybir
from concourse.bass_utils import run_bass_kernel_spmd
from contextlib import ExitStack

F32 = mybir.dt.float32
BF16 = mybir.dt.bfloat16
AF = mybir.ActivationFunctionType
ALU = mybir.AluOpType
AX = mybir.AxisListType

D = 1024
FF = 2752
NF = 22
NCORE = 8
TL = 2048
TC = 64
EPS = 1e-6

ENGS = ['pe', 'act', 'dve', 'pool', 'sp']


class Buf:
    __slots__ = ('name', 'w', 'r', 'sem', 'semval')

    def __init__(self, name):
        self.name = name
        self.w = None
        self.r = []
        self.sem = None
        self.semval = 0


class Op:
    __slots__ = ('eng', 'fn', 'deps', 'is_dma', 'signal', 'val', 'sem', 'idx')

    def __init__(self, eng, fn, is_dma=False):
        self.eng = eng
        self.fn = fn
        self.deps = []
        self.is_dma = is_dma
        self.signal = False
        self.val = 0
        self.sem = None
        self.idx = 0


class Sched:
    def __init__(self, nc, stack):
        self.nc = nc
        self.stack = stack
        self.ops = {e: [] for e in ENGS}
        self.esem = {}
        for e in ['pe', 'act', 'dve', 'pool']:
            self.esem[e] = stack.enter_context(nc.semaphore('es_' + e))
        self.dma_bufs = []
        self.nbuf = 0

    def sb(self, name, shape, dtype):
        return self.stack.enter_context(self.nc.sbuf_tensor(name, list(shape), dtype))

    def buf(self, name=None):
        self.nbuf += 1
        return Buf(name or f'b{self.nbuf}')

    def _track(self, o, reads, writes):
        deps = []
        for b in reads:
            if b.w is not None:
                deps.append(b.w)
        for b in writes:
            if b.w is not None:
                deps.append(b.w)
            deps.extend(b.r)
        seen = set()
        for d in deps:
            if d is o or id(d) in seen:
                continue
            seen.add(id(d))
            o.deps.append(d)
        for b in reads:
            b.r.append(o)
        for b in writes:
            b.w = o
            b.r = []

    def op(self, eng, fn, reads=(), writes=()):
        o = Op(eng, fn)
        self._track(o, reads, writes)
        o.idx = len(self.ops[eng])
        self.ops[eng].append(o)
        return o

    def dma(self, eng, out, in_, reads=(), writes=(), sembuf=None, **kw):
        o = Op(eng, None, is_dma=True)
        self._track(o, reads, writes)
        sb_ = sembuf or (writes[0] if writes else reads[0])
        if sb_.sem is None:
            sb_.sem = self.stack.enter_context(self.nc.semaphore('ds_' + sb_.name))
            self.dma_bufs.append(sb_)
        sb_.semval += 16
        o.sem = sb_.sem
        o.val = sb_.semval
        o.fn = lambda e: e.dma_start(out=out, in_=in_, **kw)
        o.idx = len(self.ops[eng])
        self.ops[eng].append(o)
        return o

    def _needs_sync(self, o, d):
        if d.is_dma:
            return True
        if d.eng != o.eng:
            return True
        return (o.idx - d.idx) <= 1 and d.eng != 'pe'

    def emit(self):
        for e in ENGS:
            for o in self.ops[e]:
                for d in o.deps:
                    if (not d.is_dma) and self._needs_sync(o, d):
                        d.signal = True
        EPOCH = 4000
        for e in ENGS:
            c = 0
            sems = [self.esem[e]] if e in self.esem else []
            for o in self.ops[e]:
                if o.is_dma:
                    continue
                if o.signal:
                    ep, v = divmod(c, EPOCH)
                    if ep >= len(sems):
                        sems.append(self.stack.enter_context(self.nc.semaphore(f'es_{e}_{ep}')))
                    c += 1
                    o.val = v + 1
                    o.sem = sems[ep]
        finals = [(b.sem, b.semval) for b in self.dma_bufs]

        def run(e, h):
            seen = {}
            for o in self.ops[e]:
                for d in o.deps:
                    if not self._needs_sync(o, d):
                        continue
                    k = id(d.sem)
                    if seen.get(k, 0) >= d.val:
                        continue
                    seen[k] = d.val
                    h.wait_ge(d.sem, d.val)
                ins = o.fn(h)
                if o.is_dma:
                    ins.then_inc(o.sem, 16)
                elif o.signal:
                    ins.then_inc(o.sem, 1)
            if e == 'sp':
                for s, v in finals:
                    h.wait_ge(s, v)

        with self.nc.Block() as block:
            @block.tensor
            def _(h):
                run('pe', h)

            @block.scalar
            def _(h):
                run('act', h)

            @block.vector
            def _(h):
                run('dve', h)

            @block.gpsimd
            def _(h):
                run('pool', h)

            @block.sync
            def _(h):
                run('sp', h)


def mkap(t, offset, pat):
    return bass.AP(t.tensor, offset, [list(p) for p in pat])


class Ctx:
    pass


def setup_common(S, nc):
    C = Ctx()
    C.P = S.stack.enter_context(nc.psum_tensor("P", [128, 8, 512], F32))
    C.PB = [S.buf(f'pb{i}') for i in range(8)]
    C.ident_f = S.sb("ident_f", [128, 128], F32)
    C.ident_b = S.sb("ident_b", [128, 128], BF16)
    C.B_ident = S.buf('ident')
    return C


def load_ident(S, C, ident_dram):
    S.dma('sp', C.ident_f[:], ident_dram, writes=[C.B_ident])
    S.op('dve', lambda e: e.tensor_copy(out=C.ident_b[:], in_=C.ident_f[:]), reads=[C.B_ident], writes=[C.B_ident])


def rstd_from_ss(S, ss, rstd, B_ss, B_rstd, n, eps):
    S.op('dve', lambda e: e.tensor_scalar(out=rstd, in0=ss, scalar1=1.0 / n, scalar2=eps, op0=ALU.mult, op1=ALU.add),
         reads=[B_ss], writes=[B_rstd])
    S.op('act', lambda e: e.activation(out=rstd, in_=rstd, func=AF.Sqrt), reads=[B_rstd], writes=[B_rstd])
    S.op('dve', lambda e: e.reciprocal(out=rstd, in_=rstd), reads=[B_rstd], writes=[B_rstd])


def rows_to_cols(S, C, rows_sb, B_rows, off, out_cols, B_out, bank):
    for dc in range(8):
        S.op('pe', lambda e, dc=dc: e.transpose(out=C.P[:, bank, dc * 2:dc * 2 + 2], in_=rows_sb[0:2, off + dc * 128:off + (dc + 1) * 128],
                                                identity=C.ident_f[0:2, 0:2]),
             reads=[B_rows, C.B_ident], writes=[C.PB[bank]])
    S.op('dve', lambda e: e.tensor_copy(out=out_cols[:].rearrange("p a b -> p (a b)"), in_=C.P[:, bank, 0:16]),
         reads=[C.PB[bank]], writes=[B_out])


def rows_bcast(S, C, rows_sb, B_rows, off, sel, B_sel, cond, bank0):
    for hh in range(2):
        S.op('pe', lambda e, hh=hh: e.matmul(C.P[:, bank0 + hh, :], lhsT=sel[0:2, cond, :], rhs=rows_sb[0:2, off + hh * 512:off + (hh + 1) * 512],
                                             start=True, stop=True),
             reads=[B_rows, B_sel], writes=[C.PB[bank0 + hh]])


INW = 2432
GN_EPS = 64e-5


def build_dense(mode, ntiles_l=16, has_ctx=True):
    nc = bass.Bass("TRN2", target_bir_lowering=False)
    T = ntiles_l * 128 + (TC if has_ctx else 0)
    nmod = {'ffn': 3, 'inproj': 2, 'outproj': 1}[mode]
    x = nc.dram_tensor("x", [T, D], F32, kind="ExternalInput").ap()
    condT_d = nc.dram_tensor("condT", [128, 8, 2], F32, kind="ExternalInput").ap()
    mw = nc.dram_tensor("mw", [D, nmod * D], F32, kind="ExternalInput").ap()
    mb = nc.dram_tensor("mb", [2, nmod * D], F32, kind="ExternalInput").ap()
    ident_d = nc.dram_tensor("ident", [128, 128], F32, kind="ExternalInput").ap()
    sel_d = nc.dram_tensor("sel", [2, 2, 128], F32, kind="ExternalInput").ap()
    if mode != 'outproj':
        gpreT_d = nc.dram_tensor("gpreT", [128, 8], F32, kind="ExternalInput").ap()
    if mode != 'inproj':
        gpost_d = nc.dram_tensor("gpost", [1, D], F32, kind="ExternalInput").ap()
        y = nc.dram_tensor("y", [T, D], F32, kind="ExternalOutput").ap()
    if mode == 'ffn':
        wi = nc.dram_tensor("wi", [D, 2 * FF], F32, kind="ExternalInput").ap()
        wo = nc.dram_tensor("wo", [FF, D], F32, kind="ExternalInput").ap()
    elif mode == 'inproj':
        w_d = nc.dram_tensor("w", [D, INW], F32, kind="ExternalInput").ap()
        z_d = nc.dram_tensor("z", [T, INW], F32, kind="ExternalOutput").ap()
    else:
        w_d = nc.dram_tensor("w", [D, D], F32, kind="ExternalInput").ap()
        fa_d = nc.dram_tensor("fa", [T, 640], F32, kind="ExternalInput").ap()
        yf_d = nc.dram_tensor("yf", [T, 384], F32, kind="ExternalInput").ap()
        yb_d = nc.dram_tensor("yb", [T, 384], F32, kind="ExternalInput").ap()
        rkv_d = nc.dram_tensor("rkv", [T, 1152], F32, kind="ExternalInput").ap()
        zg_d = nc.dram_tensor("zg", [T, 128], F32, kind="ExternalInput").ap()
        g2_d = nc.dram_tensor("g2", [128, 384], F32, kind="ExternalInput").ap()
        vec_d = nc.dram_tensor("vecs", [3, 384], F32, kind="ExternalInput").ap()

    with ExitStack() as st:
        S = Sched(nc, st)
        C = setup_common(S, nc)
        P = C.P
        load_ident(S, C, ident_d)
        xg = [S.sb(f"xg{i}", [128, 2, D], F32) for i in range(2)]
        B_xg = [[S.buf(f'xg{i}_{j}') for j in range(2)] for i in range(2)]
        stg_ap = [xg[0][:].rearrange("p a b -> p (a b)"), xg[1][:].rearrange("p a b -> p (a b)")]
        B_stg = [B_xg[0][0], B_xg[1][0]]
        condT = S.sb("condT_sb", [128, 8, 2], F32)
        C.B_cond = S.buf('cond')
        rows_sb = S.sb("rows_sb", [2, nmod * D], F32)
        B_rows = S.buf('rows')
        sel = S.sb("sel_sb", [2, 2, 128], F32)
        B_sel = S.buf('sel')
        B_g = S.buf('g')
        B_modT = S.buf('modT')
        B_GG = S.buf('GG')

        S.dma('sp', condT[:], condT_d, writes=[C.B_cond])
        S.op('act', lambda e: e.activation(out=condT[:], in_=condT[:], func=AF.Silu), reads=[C.B_cond], writes=[C.B_cond])
        S.dma('sp', rows_sb[:], mb, writes=[B_rows])
        S.dma('sp', sel[:], sel_d, writes=[B_sel])
        nchunk = nmod * 2
        for k in range(8):
            ncol = nmod * D
            S.dma('sp', stg_ap[0][:, 0:min(2048, ncol)], mw[k * 128:(k + 1) * 128, 0:min(2048, ncol)], writes=[B_stg[0]])
            if ncol > 2048:
                S.dma('sp', stg_ap[1][:, 0:ncol - 2048], mw[k * 128:(k + 1) * 128, 2048:ncol], writes=[B_stg[1]])
            for n in range(nchunk):
                sa = stg_ap[0] if n < 4 else stg_ap[1]
                cc = n * 512 if n < 4 else (n - 4) * 512
                S.op('pe', lambda e, sa=sa, n=n, k=k, cc=cc: e.matmul(P[0:2, n, :], lhsT=condT[:, k, :], rhs=sa[:, cc:cc + 512],
                                                                     start=(k == 0), stop=(k == 7)),
                     reads=[B_stg[0] if n < 4 else B_stg[1], C.B_cond], writes=[C.PB[n]])
        for n in range(nchunk):
            S.op('dve', lambda e, n=n: e.tensor_tensor(out=rows_sb[0:2, n * 512:(n + 1) * 512], in0=P[0:2, n, :],
                                                      in1=rows_sb[0:2, n * 512:(n + 1) * 512], op=ALU.add),
                 reads=[C.PB[n], B_rows], writes=[B_rows])
        if mode != 'outproj':
            gpreT = S.sb("gpreT_sb", [128, 8], F32)
            S.dma('sp', gpreT[:], gpreT_d, writes=[B_g])
            modT = [S.sb(f"modT{i}", [128, 8, 2], F32) for i in range(2)]
            G1T = S.sb("G1T", [128, 8, 2], F32)
            rows_to_cols(S, C, rows_sb, B_rows, 0, modT[0], B_modT, 6)
            rows_to_cols(S, C, rows_sb, B_rows, D, modT[1], B_modT, 7)
            S.op('dve', lambda e: e.tensor_scalar(out=G1T[:], in0=modT[1][:], scalar1=1.0, scalar2=None, op0=ALU.add),
                 reads=[B_modT], writes=[B_modT])
            S.op('pool', lambda e: e.tensor_tensor(out=G1T[:], in0=G1T[:], in1=gpreT[:].unsqueeze(2).to_broadcast([128, 8, 2]), op=ALU.mult),
                 reads=[B_modT, B_g], writes=[B_modT])
        if mode != 'inproj':
            gpost_bc = S.sb("gpost_bc", [128, D], F32)
            S.dma('sp', gpost_bc[:], mkap(gpost_d, 0, [(0, 128), (1, D)]), writes=[B_g])
            GG = S.sb("GG", [128, 2, D], F32)
            goff = 2 * D if mode == 'ffn' else 0
            gfac = 0.5 if mode == 'ffn' else 1.0
            for cond in range(2):
                rows_bcast(S, C, rows_sb, B_rows, goff, sel, B_sel, cond, 0 + 2 * cond)
                for hh in range(2):
                    S.op('dve', lambda e, cond=cond, hh=hh: e.scalar_tensor_tensor(
                        out=GG[:, cond, hh * 512:(hh + 1) * 512], in0=P[:, 2 * cond + hh, :], scalar=gfac,
                        in1=gpost_bc[:, hh * 512:(hh + 1) * 512], op0=ALU.mult, op1=ALU.mult),
                        reads=[C.PB[2 * cond + hh], B_g], writes=[B_GG])

        cast_engs = ['act', 'pool', 'dve']
        cnt = [0, 0]

        def cast_in(dst_ap, src_dram, rows, cols, Bdst):
            sa = stg_ap[cnt[0] % 2]
            Bs = B_stg[cnt[0] % 2]
            cnt[0] += 1
            S.dma('sp', sa[0:rows, 0:cols], src_dram, writes=[Bs])
            eng = cast_engs[cnt[1] % 3]
            cnt[1] += 1
            if eng == 'act':
                S.op('act', lambda e: e.activation(func=AF.Copy, out=dst_ap, in_=sa[0:rows, 0:cols]), reads=[Bs], writes=[Bdst])
            else:
                S.op(eng, lambda e: e.tensor_copy(out=dst_ap, in_=sa[0:rows, 0:cols]), reads=[Bs], writes=[Bdst])

        B_w = S.buf('w')
        B_wo = S.buf('wo')
        if mode == 'ffn':
            wi_sb = S.sb("wi_sb", [128, 8, 2 * FF], BF16)
            wo_sb = S.sb("wo_sb", [128, NF, D], BF16)
            for k in range(8):
                for q in range(4):
                    c0 = q * 1376
                    cast_in(wi_sb[:, k, c0:c0 + 1376], wi[k * 128:(k + 1) * 128, c0:c0 + 1376], 128, 1376, B_w)
            for fc in range(NF):
                rows = 128 if fc < NF - 1 else 64
                cast_in(wo_sb[0:rows, fc, :], wo[fc * 128:fc * 128 + rows, :], rows, D, B_wo)
        else:
            wcols = INW if mode == 'inproj' else D
            w_sb = S.sb("w_sb", [128, 8, wcols], BF16)
            for k in range(8):
                for c0 in range(0, wcols, 1216 if mode == 'inproj' else 1024):
                    cw = min(1216 if mode == 'inproj' else 1024, wcols - c0)
                    cast_in(w_sb[:, k, c0:c0 + cw], w_d[k * 128:(k + 1) * 128, c0:c0 + cw], 128, cw, B_w)
        if mode == 'outproj':
            g2_sb = S.sb("g2_sb", [128, 384], BF16)
            cast_in(g2_sb[:], g2_d, 128, 384, B_w)
            vec_bc = S.sb("vec_bc", [128, 3, 384], F32)
            B_vec = S.buf('vec')
            S.dma('sp', vec_bc[:], mkap(vec_d, 0, [(0, 128), (384, 3), (1, 384)]), writes=[B_vec])

        tiles = [(i * 128, 128, 0) for i in range(ntiles_l)]
        if has_ctx:
            tiles.append((ntiles_l * 128, TC, 1))
        groups = [tiles[i:i + 2] for i in range(0, len(tiles), 2)]
        xn = S.sb("xn", [128, 2, D], BF16)
        B_xn = [S.buf('xn0'), S.buf('xn1')]
        junk = S.sb("junk", [128, D], BF16)
        B_junk = S.buf('junk')
        stat = S.sb("stat", [128, 8], F32)
        B_stat = [S.buf('st0'), S.buf('st1'), S.buf('st2'), S.buf('st3')]
        hT = S.sb("hT", [128, 8, 256], BF16)
        B_hT = S.buf('hT')
        PBF = P[:, 6, :].bitcast(BF16)
        if mode == 'ffn':
            hid = S.sb("hid", [128, NF, 256], BF16)
            B_hid = S.buf('hid')
            sg = [S.sb(f"sg{i}", [128, 256], F32) for i in range(2)]
            B_sg = [S.buf('sg0'), S.buf('sg1')]
        if mode != 'inproj':
            tmp = S.sb("tmp", [128, D], F32)
            B_tmp = S.buf('tmp')
        if mode == 'inproj':
            zt = [S.sb(f"zt{i}", [128, INW], F32) for i in range(2)]
            B_zt = [S.buf('zt0'), S.buf('zt1')]
        if mode == 'outproj':
            IN = [dict(fa=S.sb(f"fa{i}", [128, 640], F32), yf=S.sb(f"yf{i}", [128, 384], F32), yb=S.sb(f"yb{i}", [128, 384], F32),
                       rkv=S.sb(f"rkv{i}", [128, 1152], F32), zg=S.sb(f"zg{i}", [128, 128], F32)) for i in range(2)]
            B_IN = [S.buf('in0'), S.buf('in1')]
            w1 = S.sb("w1", [128, 384], F32)
            w2 = S.sb("w2", [128, 384], F32)
            w3 = S.sb("w3", [128, 384], F32)
            B_w1, B_w2, B_w3 = S.buf('w1'), S.buf('w2'), S.buf('w3')
            s6 = S.sb("s6", [128, 4, 6], F32)
            B_s6 = [S.buf(f's6{i}') for i in range(4)]
            sgb = S.sb("sgb", [128, 128], BF16)
            sgT = S.sb("sgT", [128, 128], BF16)
            B_sgb, B_sgT = S.buf('sgb'), S.buf('sgT')
            PBF1 = P[:, 1, :].bitcast(BF16)

        tcount = [0]

        def load_group(gi):
            for j, (t0, n, cond) in enumerate(groups[gi]):
                S.dma('sp', xg[gi % 2][0:n, j, :], x[t0:t0 + n, :], writes=[B_xg[gi % 2][j]])

        def load_tile_inputs(ti):
            t0, n, cond = tiles[ti]
            d = IN[ti % 2]
            Bi = B_IN[ti % 2]
            S.dma('sp', d['fa'][0:n, :], fa_d[t0:t0 + n, :], writes=[Bi])
            S.dma('sp', d['yf'][0:n, :], yf_d[t0:t0 + n, :], writes=[Bi])
            S.dma('sp', d['yb'][0:n, :], yb_d[t0:t0 + n, :], writes=[Bi])
            S.dma('sp', d['rkv'][0:n, :], rkv_d[t0:t0 + n, :], writes=[Bi])
            S.dma('sp', d['zg'][0:n, :], zg_d[t0:t0 + n, :], writes=[Bi])

        def v6(ap, n):
            return ap[0:n, :].rearrange("p (h e) -> p h e", h=6)

        def bc6(ap6, n):
            return ap6.unsqueeze(2).to_broadcast([n, 6, 64])

        def readout_tile(ti, j):
            t0, n, cond = tiles[ti]
            d = IN[ti % 2]
            Bi = B_IN[ti % 2]
            r_ = d['rkv'][0:n, 0:384]
            k_ = d['rkv'][0:n, 384:768]
            v_ = d['rkv'][0:n, 768:1152]
            S.op('act', lambda e: e.activation(func=AF.Copy, out=xn[0:n, j, 0:640], in_=d['fa'][0:n, :]), reads=[Bi], writes=[B_xn[j]])
            S.op('act', lambda e: e.activation(out=sgb[0:n, :], in_=d['zg'][0:n, :], func=AF.Sigmoid), reads=[Bi], writes=[B_sgb])
            S.op('pe', lambda e: e.transpose(out=PBF1[:, 0:n], in_=sgb[0:n, :], identity=C.ident_b[0:n, 0:n]),
                 reads=[B_sgb, C.B_ident], writes=[C.PB[1]])
            S.op('act', lambda e: e.activation(func=AF.Copy, out=sgT[:, 0:n], in_=PBF1[:, 0:n]), reads=[C.PB[1]], writes=[B_sgT])
            S.op('pe', lambda e: e.matmul(P[0:n, 0, 0:384], lhsT=sgT[:, 0:n], rhs=g2_sb[:], start=True, stop=True),
                 reads=[B_sgT, B_w], writes=[C.PB[0]])
            S.op('pool', lambda e: e.tensor_tensor(out=w1[0:n, :], in0=d['yf'][0:n, :], in1=d['yb'][0:n, :], op=ALU.add),
                 reads=[Bi], writes=[B_w1])
            S.op('dve', lambda e: e.tensor_reduce(out=s6[0:n, 0, :], in_=v6(w1, n), axis=AX.X, op=ALU.add), reads=[B_w1], writes=[B_s6[0]])
            S.op('dve', lambda e: e.scalar_tensor_tensor(out=v6(w2, n), in0=bc6(s6[0:n, 0, :], n), scalar=-1.0 / 64, in1=v6(w1, n),
                                                         op0=ALU.mult, op1=ALU.add), reads=[B_s6[0], B_w1], writes=[B_w2])
            S.op('pool', lambda e: e.tensor_tensor(out=w3[0:n, :], in0=w2[0:n, :], in1=w2[0:n, :], op=ALU.mult), reads=[B_w2], writes=[B_w3])
            S.op('dve', lambda e: e.tensor_reduce(out=s6[0:n, 1, :], in_=v6(w3, n), axis=AX.X, op=ALU.add), reads=[B_w3], writes=[B_s6[1]])
            rstd_from_ss(S, s6[0:n, 1, :], s6[0:n, 2, :], B_s6[1], B_s6[2], 64, GN_EPS)
            S.op('dve', lambda e: e.tensor_tensor(out=v6(w2, n), in0=v6(w2, n), in1=bc6(s6[0:n, 2, :], n), op=ALU.mult),
                 reads=[B_s6[2], B_w2], writes=[B_w2])
            S.op('pool', lambda e: e.tensor_tensor(out=w2[0:n, :], in0=w2[0:n, :], in1=vec_bc[0:n, 1, :], op=ALU.mult), reads=[B_vec], writes=[B_w2])
            S.op('pool', lambda e: e.tensor_tensor(out=w3[0:n, :], in0=r_, in1=k_, op=ALU.mult), reads=[Bi, B_w2], writes=[B_w3])
            S.op('pool', lambda e: e.tensor_tensor(out=w2[0:n, :], in0=w2[0:n, :], in1=vec_bc[0:n, 2, :], op=ALU.add), reads=[B_vec, B_w3], writes=[B_w2])
            S.op('pool', lambda e: e.tensor_tensor(out=w3[0:n, :], in0=w3[0:n, :], in1=vec_bc[0:n, 0, :], op=ALU.mult), reads=[B_vec, B_w2], writes=[B_w3])
            S.op('dve', lambda e: e.tensor_reduce(out=s6[0:n, 3, :], in_=v6(w3, n), axis=AX.X, op=ALU.add), reads=[B_w3], writes=[B_s6[3]])
            S.op('dve', lambda e: e.tensor_tensor(out=v6(w1, n), in0=v_.rearrange("p (h e) -> p h e", h=6), in1=bc6(s6[0:n, 3, :], n), op=ALU.mult),
                 reads=[B_s6[3], Bi], writes=[B_w1])
            S.op('pool', lambda e: e.tensor_tensor(out=w1[0:n, :], in0=w1[0:n, :], in1=w2[0:n, :], op=ALU.add), reads=[B_w2], writes=[B_w1])
            S.op('dve', lambda e: e.tensor_tensor(out=xn[0:n, j, 640:1024], in0=P[0:n, 0, 0:384], in1=w1[0:n, :], op=ALU.mult),
                 reads=[C.PB[0], B_w1], writes=[B_xn[j]])

        load_group(0)
        if mode == 'outproj':
            load_tile_inputs(0)
        for gi, grp in enumerate(groups):
            xb = xg[gi % 2]
            Bx = B_xg[gi % 2]
            ntok = sum(t[1] for t in grp)
            if gi + 1 < len(groups):
                load_group(gi + 1)
            for j, (t0, n, cond) in enumerate(grp):
                ti = gi * 2 + j
                if mode == 'outproj':
                    if ti + 1 < len(tiles):
                        load_tile_inputs(ti + 1)
                    readout_tile(ti, j)
                else:
                    ss = stat[0:n, j:j + 1]
                    rs = stat[0:n, 2 + j:3 + j]
                    S.op('act', lambda e, n=n, j=j, ss=ss, xb=xb: e.activation(out=junk[0:n, :], in_=xb[0:n, j, :], func=AF.Square, accum_out=ss),
                         reads=[Bx[j]], writes=[B_junk, B_stat[j]])
                    rstd_from_ss(S, ss, rs, B_stat[j], B_stat[2 + j], D, EPS)
                    S.op('act', lambda e, n=n, j=j, rs=rs, xb=xb: e.activation(out=xn[0:n, j, :], in_=xb[0:n, j, :], func=AF.Copy, scale=rs),
                         reads=[Bx[j], B_stat[2 + j]], writes=[B_xn[j]])
                for kc in range(8):
                    S.op('pe', lambda e, n=n, j=j, kc=kc: e.transpose(out=PBF[:, kc * 128:kc * 128 + n], in_=xn[0:n, j, kc * 128:(kc + 1) * 128],
                                                                      identity=C.ident_b[0:n, 0:n]),
                         reads=[B_xn[j], C.B_ident], writes=[C.PB[6]])
                pv = PBF.rearrange("p (k t) -> p k t", k=8)[:, :, 0:n]
                if mode == 'outproj':
                    S.op('act', lambda e, n=n, j=j, pv=pv: e.activation(func=AF.Copy, out=hT[:, :, j * 128:j * 128 + n], in_=pv), reads=[C.PB[6]], writes=[B_hT])
                else:
                    S.op('dve', lambda e, n=n, j=j, cond=cond, pv=pv: e.tensor_tensor(
                        out=hT[:, :, j * 128:j * 128 + n], in0=pv, in1=G1T[:, :, cond:cond + 1].to_broadcast([128, 8, n]), op=ALU.mult),
                        reads=[C.PB[6], B_modT], writes=[B_hT])
                    S.op('pool', lambda e, n=n, j=j, cond=cond: e.tensor_tensor(
                        out=hT[:, :, j * 128:j * 128 + n], in0=hT[:, :, j * 128:j * 128 + n],
                        in1=modT[0][:, :, cond:cond + 1].to_broadcast([128, 8, n]), op=ALU.add),
                        reads=[B_modT], writes=[B_hT])
            if mode == 'ffn':
                for fc in range(NF):
                    fw = 128 if fc < NF - 1 else 64
                    bg = fc % 2
                    for which in range(2):
                        col0 = which * FF + fc * 128
                        bank = bg * 2 + which
                        for k in range(8):
                            S.op('pe', lambda e, fw=fw, col0=col0, bank=bank, k=k, ntok=ntok: e.matmul(
                                P[0:fw, bank, 0:ntok], lhsT=wi_sb[:, k, col0:col0 + fw], rhs=hT[:, k, 0:ntok], start=(k == 0), stop=(k == 7)),
                                reads=[B_w, B_hT], writes=[C.PB[bank]])
                    S.op('act', lambda e, fw=fw, bg=bg, ntok=ntok: e.activation(out=sg[bg][0:fw, 0:ntok], in_=P[0:fw, bg * 2, 0:ntok], func=AF.Silu),
                         reads=[C.PB[bg * 2]], writes=[B_sg[bg]])
                    S.op('dve', lambda e, fw=fw, bg=bg, fc=fc, ntok=ntok: e.tensor_tensor(out=hid[0:fw, fc, 0:ntok], in0=P[0:fw, bg * 2 + 1, 0:ntok],
                                                                                       in1=sg[bg][0:fw, 0:ntok], op=ALU.mult),
                         reads=[C.PB[bg * 2 + 1], B_sg[bg]], writes=[B_hid])
            if mode == 'inproj':
                for j, (t0, n, cond) in enumerate(grp):
                    zb = zt[tcount[0] % 2]
                    Bz = B_zt[tcount[0] % 2]
                    tcount[0] += 1
                    for ci, c0 in enumerate(range(0, INW, 512)):
                        cw = min(512, INW - c0)
                        for k in range(8):
                            S.op('pe', lambda e, n=n, j=j, ci=ci, c0=c0, cw=cw, k=k: e.matmul(
                                P[0:n, ci, 0:cw], lhsT=hT[:, k, j * 128:j * 128 + n], rhs=w_sb[:, k, c0:c0 + cw], start=(k == 0), stop=(k == 7)),
                                reads=[B_w, B_hT], writes=[C.PB[ci]])
                        if ci % 2 == 0:
                            S.op('act', lambda e, n=n, ci=ci, c0=c0, cw=cw, zb=zb: e.activation(func=AF.Copy, out=zb[0:n, c0:c0 + cw], in_=P[0:n, ci, 0:cw]),
                                 reads=[C.PB[ci]], writes=[Bz])
                        else:
                            S.op('dve', lambda e, n=n, ci=ci, c0=c0, cw=cw, zb=zb: e.tensor_copy(out=zb[0:n, c0:c0 + cw], in_=P[0:n, ci, 0:cw]),
                                 reads=[C.PB[ci]], writes=[Bz])
                    S.dma('sp', z_d[t0:t0 + n, :], zb[0:n, :], reads=[Bz])
                continue
            for j, (t0, n, cond) in enumerate(grp):
                for hh in range(2):
                    bank = 4 + hh
                    if mode == 'ffn':
                        for fc in range(NF):
                            fw = 128 if fc < NF - 1 else 64
                            S.op('pe', lambda e, fw=fw, fc=fc, bank=bank, hh=hh, j=j, n=n: e.matmul(
                                P[0:n, bank, :], lhsT=hid[0:fw, fc, j * 128:j * 128 + n], rhs=wo_sb[0:fw, fc, hh * 512:(hh + 1) * 512],
                                start=(fc == 0), stop=(fc == NF - 1)),
                                reads=[B_hid, B_wo], writes=[C.PB[bank]])
                    else:
                        for k in range(8):
                            S.op('pe', lambda e, k=k, bank=bank, hh=hh, j=j, n=n: e.matmul(
                                P[0:n, bank, :], lhsT=hT[:, k, j * 128:j * 128 + n], rhs=w_sb[:, k, hh * 512:(hh + 1) * 512],
                                start=(k == 0), stop=(k == 7)),
                                reads=[B_hT, B_w], writes=[C.PB[bank]])
                ss = stat[0:n, 4 + j:5 + j]
                rs = stat[0:n, 6 + j:7 + j]
                yv = P[0:n, 4:6, :].rearrange("p a b -> p (a b)")
                S.op('act', lambda e, n=n, ss=ss, yv=yv: e.activation(out=junk[0:n, :], in_=yv, func=AF.Square, accum_out=ss),
                     reads=[C.PB[4], C.PB[5]], writes=[B_junk, B_stat[j]])
                rstd_from_ss(S, ss, rs, B_stat[j], B_stat[2 + j], D, EPS)
                S.op('dve', lambda e, n=n, rs=rs, yv=yv, cond=cond: e.scalar_tensor_tensor(
                    out=tmp[0:n, :], in0=yv, scalar=rs, in1=GG[0:n, cond, :], op0=ALU.mult, op1=ALU.mult),
                    reads=[C.PB[4], C.PB[5], B_stat[2 + j], B_GG], writes=[B_tmp])
                S.op('pool', lambda e, n=n, j=j, xb=xb: e.tensor_tensor(out=xb[0:n, j, :], in0=xb[0:n, j, :], in1=tmp[0:n, :], op=ALU.add),
                     reads=[B_tmp], writes=[Bx[j]])
                S.dma('sp', y[t0:t0 + n, :], xb[0:n, j, :], reads=[Bx[j]])
        S.emit()
    return nc


_cache = {}


def _get(name, builder):
    if name not in _cache:
        _cache[name] = builder()
    return _cache[name]


def _common_maps(c, c_ctx, mod_w, mod_b):
    ident = np.eye(128, dtype=np.float32)
    sel = np.zeros((2, 2, 128), np.float32)
    sel[0, 0, :] = 1.0
    sel[1, 1, :] = 1.0
    maps = []
    mwc = np.ascontiguousarray(mod_w)
    mbc = np.ascontiguousarray(np.stack([mod_b, mod_b], 0))
    for core in range(NCORE):
        b = core // 4
        cond = np.stack([c[b], c_ctx], 0)
        condT = np.ascontiguousarray(cond.reshape(2, 8, 128).transpose(2, 1, 0))
        maps.append(dict(condT=condT, mw=mwc, mb=mbc, ident=ident, sel=sel))
    return maps


def shard_tokens(xl, xc):
    xs = []
    for core in range(NCORE):
        b, q = core // 4, core % 4
        xs.append(np.ascontiguousarray(np.concatenate([xl[b, q * TL:(q + 1) * TL], xc[b, q * TC:(q + 1) * TC]], 0)))
    return xs


def unshard_tokens(ys, width=D):
    xl = np.zeros((2, 8192, width), ys[0].dtype)
    xc = np.zeros((2, 256, width), ys[0].dtype)
    for core in range(NCORE):
        b, q = core // 4, core % 4
        xl[b, q * TL:(q + 1) * TL] = ys[core][:TL]
        xc[b, q * TC:(q + 1) * TC] = ys[core][TL:]
    return xl, xc


def run_ffn(xs, c, c_ctx, mod_w, mod_b, g_pre, g_post, wi, wo):
    nc = _get('ffn', lambda: build_dense('ffn'))
    maps = _common_maps(c, c_ctx, mod_w, mod_b)
    gpreT = np.ascontiguousarray(g_pre.reshape(8, 128).T)
    gpost = np.ascontiguousarray(g_post.reshape(1, D))
    wi = np.ascontiguousarray(wi)
    wo = np.ascontiguousarray(wo)
    for core in range(NCORE):
        maps[core].update(x=xs[core], gpreT=gpreT, gpost=gpost, wi=wi, wo=wo)
    res = run_bass_kernel_spmd(nc, maps, core_ids=list(range(NCORE)))
    return [r['y'] for r in res.results]


def run_inproj(xs, c, c_ctx, mod_w, mod_b, g_pre, w_in):
    nc = _get('inproj', lambda: build_dense('inproj'))
    maps = _common_maps(c, c_ctx, mod_w, mod_b)
    gpreT = np.ascontiguousarray(g_pre.reshape(8, 128).T)
    w_in = np.ascontiguousarray(w_in)
    for core in range(NCORE):
        maps[core].update(x=xs[core], gpreT=gpreT, w=w_in)
    res = run_bass_kernel_spmd(nc, maps, core_ids=list(range(NCORE)))
    return [r['z'] for r in res.results]


def run_outproj(xs, c, c_ctx, mod_w, mod_b, g_post, w_out, fa, yf, yb, rkv, zg, g2, vecs):
    nc = _get('outproj', lambda: build_dense('outproj'))
    maps = _common_maps(c, c_ctx, mod_w, mod_b)
    gpost = np.ascontiguousarray(g_post.reshape(1, D))
    w_out = np.ascontiguousarray(w_out)
    for core in range(NCORE):
        maps[core].update(x=xs[core], gpost=gpost, w=w_out, fa=fa[core], yf=yf[core], yb=yb[core], rkv=rkv[core], zg=zg[core],
                          g2=np.ascontiguousarray(g2), vecs=np.ascontiguousarray(vecs))
    res = run_bass_kernel_spmd(nc, maps, core_ids=list(range(NCORE)))
    return [r['y'] for r in res.results]


def build_prep(ntiles_l=16, has_ctx=True):
    nc = bass.Bass("TRN2", target_bir_lowering=False)
    T = ntiles_l * 128 + (TC if has_ctx else 0)
    zt_d = nc.dram_tensor("zt", [T, INW], F32, kind="ExternalInput").ap()
    zp_d = nc.dram_tensor("zp", [T, 1152], F32, kind="ExternalInput").ap()
    zn_d = nc.dram_tensor("zn", [T, 1152], F32, kind="ExternalInput").ap()
    tab_d = nc.dram_tensor("tab", [T, 64], F32, kind="ExternalInput").ap()
    cw_d = nc.dram_tensor("cw", [1, 3 * 1152], F32, kind="ExternalInput").ap()
    vec_d = nc.dram_tensor("pvecs", [1, 6 * 384], F32, kind="ExternalInput").ap()
    l2_d = nc.dram_tensor("l2", [128, 2, 384], F32, kind="ExternalInput").ap()
    ident_d = nc.dram_tensor("ident", [128, 128], F32, kind="ExternalInput").ap()
    qk_o = nc.dram_tensor("qk", [T, 512], BF16, kind="ExternalOutput").ap()
    v_o = nc.dram_tensor("v", [T, 128], BF16, kind="ExternalOutput").ap()
    rkv_o = nc.dram_tensor("rkv", [T, 1152], F32, kind="ExternalOutput").ap()
    scan_o = nc.dram_tensor("scan", [T, 6, 384], BF16, kind="ExternalOutput").ap()
    dec_o = nc.dram_tensor("dec", [T, 2, 384], F32, kind="ExternalOutput").ap()
    with ExitStack() as st:
        S = Sched(nc, st)
        C = setup_common(S, nc)
        P = C.P
        load_ident(S, C, ident_d)
        cw = S.sb("cw_sb", [128, 3, 1152], F32)
        vec = S.sb("vec_sb", [128, 6, 384], F32)
        l2f = S.sb("l2f", [128, 2, 384], F32)
        l2 = S.sb("l2b", [128, 2, 384], BF16)
        B_c = S.buf('consts')
        S.dma('sp', cw[:].rearrange("p a b -> p (a b)"), mkap(cw_d, 0, [(0, 128), (1, 3 * 1152)]), writes=[B_c])
        S.dma('sp', vec[:].rearrange("p a b -> p (a b)"), mkap(vec_d, 0, [(0, 128), (1, 6 * 384)]), writes=[B_c])
        S.dma('sp', l2f[:], l2_d, writes=[B_c])
        S.op('dve', lambda e: e.tensor_copy(out=l2[:], in_=l2f[:]), reads=[B_c], writes=[B_c])
        tiles = [(i * 128, 128) for i in range(ntiles_l)]
        if has_ctx:
            tiles.append((ntiles_l * 128, TC))
        sets = []
        for i in range(2):
            d = dict(zt=S.sb(f"zt{i}", [128, INW], F32), zp=S.sb(f"zp{i}", [128, 1152], F32), zn=S.sb(f"zn{i}", [128, 1152], F32),
                     tab=S.sb(f"tab{i}", [128, 64], F32), B=S.buf(f'in{i}'),
                     qk=S.sb(f"qk{i}", [128, 512], BF16), v=S.sb(f"v{i}", [128, 128], BF16), rkv=S.sb(f"rkv{i}", [128, 1152], F32),
                     scan=S.sb(f"scan{i}", [128, 6, 384], BF16), dec=S.sb(f"dec{i}", [128, 2, 384], F32),
                     Bqk=S.buf(f'qk{i}'), Bv=S.buf(f'v{i}'), Brkv=S.buf(f'rkv{i}'), Bscan=S.buf(f'scan{i}'), Bdec=S.buf(f'dec{i}'))
            sets.append(d)
        t1 = S.sb("t1", [128, 8, 32], F32)
        t2 = S.sb("t2", [128, 8, 32], F32)
        t3 = S.sb("t3", [128, 8, 32], F32)
        t4 = S.sb("t4", [128, 8, 32], F32)
        Bt = [S.buf(f't{i}') for i in range(4)]
        ca = S.sb("ca", [128, 1152], F32)
        cb = S.sb("cb", [128, 1152], F32)
        Bca, Bcb = S.buf('ca'), S.buf('cb')
        lb = S.sb("lb", [128, 2, 128], BF16)
        lT = S.sb("lT", [128, 2, 128], BF16)
        Blb, BlT = S.buf('lb'), S.buf('lT')
        PBF = P[:, 6, :].bitcast(BF16)
        asig = S.sb("asig", [128, 2, 384], F32)
        Basig = S.buf('asig')
        wt = S.sb("wt", [128, 2, 384], F32)
        Bwt = S.buf('wt')
        kk0 = S.sb("kk0", [128, 384], F32)
        ksq = S.sb("ksq", [128, 384], F32)
        kkf = S.sb("kkf", [128, 384], F32)
        Bkk0, Bksq, Bkkf = S.buf('kk0'), S.buf('ksq'), S.buf('kkf')
        s6 = S.sb("s6", [128, 2, 6], F32)
        Bs6 = [S.buf('s60'), S.buf('s61')]
        tk = S.sb("tk", [128, 2, 384], F32)
        Btk = S.buf('tk')

        def load(ti):
            t0, n = tiles[ti]
            d = sets[ti % 2]
            S.dma('sp', d['zt'][0:n, :], zt_d[t0:t0 + n, :], writes=[d['B']])
            S.dma('sp', d['zp'][0:n, :], zp_d[t0:t0 + n, :], writes=[d['B']])
            S.dma('sp', d['zn'][0:n, :], zn_d[t0:t0 + n, :], writes=[d['B']])
            S.dma('sp', d['tab'][0:n, :], tab_d[t0:t0 + n, :], writes=[d['B']])

        load(0)
        for ti, (t0, n) in enumerate(tiles):
            if ti + 1 < len(tiles):
                load(ti + 1)
            d = sets[ti % 2]
            Bi = d['B']
            zt = d['zt']
            qk = zt[0:n, 256:768].rearrange("p (h two e) -> p h two e", h=8, two=2)
            x1 = qk[:, :, 0, :]
            x2 = qk[:, :, 1, :]
            cosb = d['tab'][0:n, 0:32].unsqueeze(1).to_broadcast([n, 8, 32])
            sinb = d['tab'][0:n, 32:64].unsqueeze(1).to_broadcast([n, 8, 32])
            oq = d['qk'][0:n, :].rearrange("p (h two e) -> p h two e", h=8, two=2)
            S.op('dve', lambda e, x1=x1, cosb=cosb, n=n: e.tensor_tensor(out=t1[0:n], in0=x1, in1=cosb, op=ALU.mult), reads=[Bi], writes=[Bt[0]])
            S.op('pool', lambda e, x2=x2, sinb=sinb, n=n: e.tensor_tensor(out=t2[0:n], in0=x2, in1=sinb, op=ALU.mult), reads=[Bi], writes=[Bt[1]])
            S.op('dve', lambda e, x2=x2, cosb=cosb, n=n: e.tensor_tensor(out=t3[0:n], in0=x2, in1=cosb, op=ALU.mult), reads=[Bi], writes=[Bt[2]])
            S.op('pool', lambda e, x1=x1, sinb=sinb, n=n: e.tensor_tensor(out=t4[0:n], in0=x1, in1=sinb, op=ALU.mult), reads=[Bi], writes=[Bt[3]])
            S.op('dve', lambda e, oq=oq, n=n: e.tensor_tensor(out=oq[:, :, 0, :], in0=t1[0:n], in1=t2[0:n], op=ALU.subtract),
                 reads=[Bt[0], Bt[1]], writes=[d['Bqk']])
            S.op('pool', lambda e, oq=oq, n=n: e.tensor_tensor(out=oq[:, :, 1, :], in0=t3[0:n], in1=t4[0:n], op=ALU.add),
                 reads=[Bt[2], Bt[3]], writes=[d['Bqk']])
            S.dma('sp', qk_o[t0:t0 + n, :], d['qk'][0:n, :], reads=[d['Bqk']])
            S.op('act', lambda e, zt=zt, d=d, n=n: e.activation(func=AF.Copy, out=d['v'][0:n, :], in_=zt[0:n, 768:896]), reads=[Bi], writes=[d['Bv']])
            S.dma('sp', v_o[t0:t0 + n, :], d['v'][0:n, :], reads=[d['Bv']])
            rkv = d['rkv']
            S.op('pool', lambda e, d=d, n=n: e.tensor_tensor(out=ca[0:n, :], in0=d['zp'][0:n, :], in1=cw[0:n, 0, :], op=ALU.mult), reads=[Bi, B_c], writes=[Bca])
            S.op('dve', lambda e, zt=zt, n=n: e.tensor_tensor(out=cb[0:n, :], in0=zt[0:n, 896:2048], in1=cw[0:n, 1, :], op=ALU.mult), reads=[Bi, B_c], writes=[Bcb])
            S.op('pool', lambda e, d=d, n=n, rkv=rkv: e.tensor_tensor(out=rkv[0:n, :], in0=d['zn'][0:n, :], in1=cw[0:n, 2, :], op=ALU.mult), reads=[Bi, B_c], writes=[d['Brkv']])
            S.op('dve', lambda e, n=n: e.tensor_tensor(out=ca[0:n, :], in0=ca[0:n, :], in1=cb[0:n, :], op=ALU.add), reads=[Bcb], writes=[Bca])
            S.op('pool', lambda e, n=n, rkv=rkv: e.tensor_tensor(out=rkv[0:n, :], in0=rkv[0:n, :], in1=ca[0:n, :], op=ALU.add), reads=[Bca], writes=[d['Brkv']])
            S.dma('sp', rkv_o[t0:t0 + n, :], rkv[0:n, :], reads=[d['Brkv']])
            r_ = rkv[0:n, 0:384]
            k_ = rkv[0:n, 384:768]
            S.op('act', lambda e, zt=zt, n=n: e.activation(out=lb[0:n, 0, :], in_=zt[0:n, 2048:2176], func=AF.Tanh), reads=[Bi], writes=[Blb])
            S.op('act', lambda e, zt=zt, n=n: e.activation(func=AF.Copy, out=lb[0:n, 1, :], in_=zt[0:n, 2176:2304]), reads=[Bi], writes=[Blb])
            for q in range(2):
                S.op('pe', lambda e, q=q, n=n: e.transpose(out=PBF[:, q * 128:q * 128 + n], in_=lb[0:n, q, :], identity=C.ident_b[0:n, 0:n]),
                     reads=[Blb, C.B_ident], writes=[C.PB[6]])
            S.op('act', lambda e, n=n: e.activation(func=AF.Copy, out=lT[:, :, 0:n], in_=PBF[:, 0:256].rearrange("p (q t) -> p q t", q=2)[:, :, 0:n]),
                 reads=[C.PB[6]], writes=[BlT])
            for q in range(2):
                for z in range(2):
                    bank = q * 2 + z
                    S.op('pe', lambda e, q=q, z=z, bank=bank, n=n: e.matmul(P[0:n, bank, 0:384], lhsT=lT[z * 64:(z + 1) * 64, q, 0:n],
                                                                          rhs=l2[z * 64:(z + 1) * 64, q, :], start=True, stop=True),
                         reads=[BlT, B_c], writes=[C.PB[bank]])
            for z in range(2):
                S.op('dve', lambda e, z=z, n=n: e.tensor_tensor(out=wt[0:n, z, :], in0=P[0:n, z, 0:384], in1=vec[0:n, z, :], op=ALU.add),
                     reads=[C.PB[z], B_c], writes=[Bwt])
                S.op('dve', lambda e, z=z, n=n: e.tensor_tensor(out=asig[0:n, z, :], in0=P[0:n, 2 + z, 0:384], in1=vec[0:n, 2 + z, :], op=ALU.add),
                     reads=[C.PB[2 + z], B_c], writes=[Basig])
            S.op('act', lambda e, n=n: e.activation(out=wt[0:n], in_=wt[0:n], func=AF.Sigmoid), reads=[Bwt], writes=[Bwt])
            S.op('act', lambda e, n=n: e.activation(out=asig[0:n], in_=asig[0:n], func=AF.Sigmoid), reads=[Basig], writes=[Basig])
            S.op('act', lambda e, n=n, d=d: e.activation(out=d['dec'][0:n], in_=wt[0:n], func=AF.Copy, scale=-0.6065306597126334),
                 reads=[Bwt], writes=[d['Bdec']])
            S.dma('sp', dec_o[t0:t0 + n], d['dec'][0:n], reads=[d['Bdec']])
            S.op('pool', lambda e, n=n, k_=k_: e.tensor_tensor(out=kk0[0:n, :], in0=k_, in1=vec[0:n, 4, :], op=ALU.mult), reads=[d['Brkv'], B_c], writes=[Bkk0])
            S.op('pool', lambda e, n=n: e.tensor_tensor(out=ksq[0:n, :], in0=kk0[0:n, :], in1=kk0[0:n, :], op=ALU.mult), reads=[Bkk0], writes=[Bksq])
            S.op('dve', lambda e, n=n: e.tensor_reduce(out=s6[0:n, 0, :], in_=ksq[0:n, :].rearrange("p (h e) -> p h e", h=6), axis=AX.X, op=ALU.add),
                 reads=[Bksq], writes=[Bs6[0]])
            S.op('dve', lambda e, n=n: e.tensor_scalar(out=s6[0:n, 1, :], in0=s6[0:n, 0, :], scalar1=1e-24, scalar2=None, op0=ALU.max),
                 reads=[Bs6[0]], writes=[Bs6[1]])
            S.op('act', lambda e, n=n: e.activation(out=s6[0:n, 1, :], in_=s6[0:n, 1, :], func=AF.Sqrt), reads=[Bs6[1]], writes=[Bs6[1]])
            S.op('dve', lambda e, n=n: e.reciprocal(out=s6[0:n, 1, :], in_=s6[0:n, 1, :]), reads=[Bs6[1]], writes=[Bs6[1]])
            S.op('dve', lambda e, n=n: e.tensor_tensor(out=kkf[0:n, :].rearrange("p (h e) -> p h e", h=6), in0=kk0[0:n, :].rearrange("p (h e) -> p h e", h=6),
                                                      in1=s6[0:n, 1, :].unsqueeze(2).to_broadcast([n, 6, 64]), op=ALU.mult),
                 reads=[Bkk0, Bs6[1]], writes=[Bkkf])
            sc = d['scan']
            Bsc = d['Bscan']
            S.op('act', lambda e, n=n, sc=sc: e.mul(out=sc[0:n, 0, :], in_=kkf[0:n, :], mul=-1.0), reads=[Bkkf], writes=[Bsc])
            S.op('act', lambda e, n=n, sc=sc, r_=r_: e.activation(func=AF.Copy, out=sc[0:n, 1, :], in_=r_), reads=[d['Brkv']], writes=[Bsc])
            for z in range(2):
                S.op('pool', lambda e, n=n, sc=sc, z=z: e.tensor_tensor(out=sc[0:n, 2 + z, :], in0=kkf[0:n, :], in1=asig[0:n, z, :], op=ALU.mult),
                     reads=[Bkkf, Basig], writes=[Bsc])
                S.op('dve', lambda e, n=n, z=z: e.scalar_tensor_tensor(out=tk[0:n, z, :], in0=asig[0:n, z, :], scalar=-1.0, in1=vec[0:n, 5, :],
                                                                      op0=ALU.add, op1=ALU.mult), reads=[Basig, B_c], writes=[Btk])
            for z in range(2):
                S.op('dve', lambda e, n=n, z=z, sc=sc, k_=k_: e.scalar_tensor_tensor(out=sc[0:n, 4 + z, :], in0=tk[0:n, z, :], scalar=1.0, in1=k_,
                                                                                    op0=ALU.add, op1=ALU.mult), reads=[Btk, d['Brkv']], writes=[Bsc])
            S.dma('sp', scan_o[t0:t0 + n], sc[0:n], reads=[Bsc])
        S.emit()
    return nc


def rope_tables():
    rows = 8192 // 64
    t = np.arange(8192)
    row = (t // 64).astype(np.float32)
    col = (t % 64).astype(np.float32)
    inv = (10000.0 ** (-np.arange(16, dtype=np.float32) / 16)).astype(np.float32)
    ang = np.concatenate([row[:, None] * inv, col[:, None] * inv], -1).astype(np.float32)
    return np.cos(ang).astype(np.float32), np.sin(ang).astype(np.float32)


def run_prep(zl, zc, conv_w, w0, w2, a0, a2, k_k, k_a):
    nc = _get('prep', build_prep)
    cos, sin = rope_tables()
    ident = np.eye(128, dtype=np.float32)
    cwf = np.ascontiguousarray(conv_w.reshape(1, 3 * 1152))
    pvecs = np.ascontiguousarray(np.concatenate([w0[0], w0[1], a0[0], a0[1], k_k, k_a]).reshape(1, 6 * 384))
    l2 = np.ascontiguousarray(np.stack([w2.reshape(128, 384), a2.reshape(128, 384)], 1))

    def shift(zz, d):
        o = np.zeros_like(zz)
        if d == 1:
            o[1:] = zz[:-1]
        else:
            o[:-1] = zz[1:]
        return o
    maps = []
    for core in range(NCORE):
        b, q = core // 4, core % 4
        rl = zl[b][:, 896:2048]
        rc = zc[b][:, 896:2048]
        zt = np.concatenate([zl[b, q * TL:(q + 1) * TL], zc[b, q * TC:(q + 1) * TC]], 0)
        zp = np.concatenate([shift(rl, 1)[q * TL:(q + 1) * TL], shift(rc, 1)[q * TC:(q + 1) * TC]], 0)
        zn = np.concatenate([shift(rl, -1)[q * TL:(q + 1) * TL], shift(rc, -1)[q * TC:(q + 1) * TC]], 0)
        tab = np.zeros((TL + TC, 64), np.float32)
        tab[:TL, 0:32] = cos[q * TL:(q + 1) * TL]
        tab[:TL, 32:64] = sin[q * TL:(q + 1) * TL]
        tab[TL:, 0:32] = 1.0
        maps.append(dict(zt=np.ascontiguousarray(zt), zp=np.ascontiguousarray(zp), zn=np.ascontiguousarray(zn), tab=tab,
                         cw=cwf, pvecs=pvecs, l2=l2, ident=ident))
    res = run_bass_kernel_spmd(nc, maps, core_ids=list(range(NCORE)))
    out = {}
    for key, wdt in [('qk', 512), ('v', 128), ('rkv', 1152)]:
        out[key] = unshard_tokens([r[key] for r in res.results], wdt)
    out['scan'] = unshard_tokens([r['scan'].reshape(TL + TC, 6 * 384) for r in res.results], 6 * 384)
    out['dec'] = unshard_tokens([r['dec'].reshape(TL + TC, 2 * 384) for r in res.results], 2 * 384)
    return out


CH = 64
NSTEP = 8448


def build_scan2(nstep=NSTEP):
    nc = bass.Bass("TRN2", target_bir_lowering=False)
    nch = nstep // CH
    tm_d = nc.dram_tensor("tm", [3, nch, CH, 4, 64], F32, kind="ExternalInput").ap()
    fm_d = nc.dram_tensor("fm", [3, nch, 64, 4, CH], F32, kind="ExternalInput").ap()
    lw_d = nc.dram_tensor("lw", [3, nch, 64, 2, 64], F32, kind="ExternalInput").ap()
    cst_d = nc.dram_tensor("cst", [64, 8, 64], F32, kind="ExternalInput").ap()
    y_o = nc.dram_tensor("y", [3, nch, 64, CH], F32, kind="ExternalOutput").ap()
    NSET = 4
    with ExitStack() as st:
        S = Sched(nc, st)
        P = st.enter_context(nc.psum_tensor("P", [128, 8, 512], F32))
        PB = [S.buf(f'pb{i}') for i in range(8)]
        cst = S.sb("cst_sb", [64, 8, 64], F32)
        B_c = S.buf('cst')
        S.dma('sp', cst[:], cst_d, writes=[B_c])
        tri = cst[:, 0, :]
        maskM = cst[:, 1:6, :].rearrange("p a b -> p (a b)")
        ident = cst[:, 6, :]
        sets = []
        for i in range(NSET):
            d = dict(
                tm=S.sb(f"tm{i}", [64, 4, 64], F32), fm=S.sb(f"fm{i}", [64, 4, 64], F32), lw=S.sb(f"lw{i}", [64, 2, 64], F32),
                Ep=S.sb(f"Ep{i}", [64, 128], F32), En=S.sb(f"En{i}", [64, 128], F32), Ev=S.sb(f"Ev{i}", [64, 128], F32),
                AR=S.sb(f"AR{i}", [64, 128], F32), BKf=S.sb(f"BKf{i}", [64, 2, 64], F32), BKt=S.sb(f"BKt{i}", [64, 2, 64], F32),
                M=S.sb(f"M{i}", [64, 320], F32), X=[S.sb(f"X{i}_{q}", [64, 128], F32) for q in range(2)],
                NN=[S.sb(f"NN{i}_{q}", [64, 128], F32) for q in range(2)],
                Rhat=S.sb(f"Rhat{i}", [64, 64], F32), Y0=S.sb(f"Y0{i}", [64, 64], F32), G0=S.sb(f"G0{i}", [64, 64], F32),
                HT=S.sb(f"HT{i}", [64, 64], F32), ysb=S.sb(f"ysb{i}", [64, 64], F32),
            )
            for k in ['in', 'Ep', 'En', 'Ev', 'AR', 'BKf', 'BKt', 'M', 'X0', 'X1', 'NN0', 'NN1', 'Rhat', 'Y0', 'G0', 'HT', 'ysb']:
                d['B' + k] = S.buf(f'{k}{i}')
            sets.append(d)
        ST = [[S.sb(f"ST{it}_{q}", [64, 64], F32) for q in range(2)] for it in range(3)]
        B_ST = [[S.buf(f'ST{it}_{q}') for q in range(2)] for it in range(3)]
        for it in range(3):
            S.op('pool', lambda e, it=it: e.memset(ST[it][0][:], 0.0), writes=[B_ST[it][0]])

        def load(c, it, d):
            S.dma('sp', d['tm'][:], tm_d[it, c], writes=[d['Bin']])
            S.dma('sp', d['fm'][:], fm_d[it, c], writes=[d['Bin']])
            S.dma('sp', d['lw'][:], lw_d[it, c], writes=[d['Bin']])

        def mm(out, lhsT, rhs, reads, writes, start=True, stop=True):
            S.op('pe', lambda e: e.matmul(out, lhsT=lhsT, rhs=rhs, start=start, stop=stop), reads=reads, writes=writes)

        def act(out, in_, func, reads, writes, scale=None):
            if scale is None:
                S.op('act', lambda e: e.activation(out=out, in_=in_, func=func), reads=reads, writes=writes)
            else:
                S.op('act', lambda e: e.activation(out=out, in_=in_, func=func, scale=scale), reads=reads, writes=writes)

        def tt(eng, out, in0, in1, op, reads, writes):
            S.op(eng, lambda e: e.tensor_tensor(out=out, in0=in0, in1=in1, op=op), reads=reads, writes=writes)

        def mm(out, lhsT, rhs, reads, writes, start=True, stop=True):
            S.op('pe', lambda e: e.matmul(out, lhsT=lhsT, rhs=rhs, start=start, stop=stop), reads=reads, writes=writes)

        def act(out, in_, func, reads, writes, scale=None):
            if scale is None:
                S.op('act', lambda e: e.activation(out=out, in_=in_, func=func), reads=reads, writes=writes)
            else:
                S.op('act', lambda e: e.activation(out=out, in_=in_, func=func, scale=scale), reads=reads, writes=writes)

        def tt(eng, out, in0, in1, op, reads, writes):
            S.op(eng, lambda e: e.tensor_tensor(out=out, in0=in0, in1=in1, op=op), reads=reads, writes=writes)

        order = [(c, it) for c in range(nch) for it in range(3)]
        load(order[0][0], order[0][1], sets[0])
        for n, (c, it) in enumerate(order):
            d = sets[n % NSET]
            if n + 1 < len(order):
                load(order[n + 1][0], order[n + 1][1], sets[(n + 1) % NSET])
            pb = (n % 2) * 3
            bk0, bk1, bk2 = pb, pb + 1, pb + 2
            Bin = d['Bin']
            tm, fm, lw = d['tm'], d['fm'], d['lw']
            Lps = P[0:64, bk0, 0:128]
            BE = d['BEp']
            mm(P[0:64, bk0, 0:64], tri, lw[:, 0, :], [Bin, B_c], [PB[bk0]])
            mm(P[0:64, bk0, 64:128], lw[:, 0, :], tri, [Bin, B_c], [PB[bk0]])
            act(d['Ep'][:], Lps, AF.Exp, [PB[bk0]], [BE])
            act(d['En'][:], Lps, AF.Exp, [PB[bk0]], [BE], scale=-1.0)
            tt('dve', d['Ev'][:], Lps, lw[:].rearrange("p a b -> p (a b)"), ALU.subtract, [PB[bk0], Bin], [BE])
            act(d['Ev'][:], d['Ev'][:], AF.Exp, [BE], [BE])
            X0 = d['X'][0]
            tt('pool', X0[:, 0:64], tm[:, 0, :], d['Ev'][:, 0:64], ALU.mult, [Bin, BE], [d['BX0']])
            tt('dve', d['BKt'][:], tm[:, 1:3, :], d['En'][:, 0:64].unsqueeze(1).to_broadcast([64, 2, 64]), ALU.mult, [Bin, BE], [d['BBKt']])
            tt('pool', d['AR'][:, 0:64], fm[:, 0, :], d['Ev'][:, 64:128], ALU.mult, [Bin, BE], [d['BAR']])
            tt('dve', d['AR'][:, 64:128], fm[:, 1, :], d['Ep'][:, 64:128], ALU.mult, [Bin, BE], [d['BAR']])
            tt('pool', d['BKf'][:], fm[:, 2:4, :], d['En'][:, 64:128].unsqueeze(1).to_broadcast([64, 2, 64]), ALU.mult, [Bin, BE], [d['BBKf']])
            mm(P[0:64, bk1, 0:128], d['BKf'][:, 0, :], d['AR'][:], [d['BBKf'], d['BAR']], [PB[bk1]])
            mm(P[0:64, bk1, 128:256], d['BKf'][:, 1, :], d['AR'][:], [d['BBKf'], d['BAR']], [PB[bk1]])
            mm(P[0:64, bk1, 256:320], d['AR'][:, 0:64], d['BKf'][:, 0, :], [d['BBKf'], d['BAR']], [PB[bk1]])
            tt('dve', d['M'][:], P[0:64, bk1, 0:320], maskM, ALU.mult, [PB[bk1], B_c], [d['BM']])
            M = d['M']
            Mbr, Mka, Mkr = M[:, 64:128], M[:, 128:192], M[:, 192:256]
            mm(P[0:64, bk0, 128:192], Mka, tm[:, 3, :], [d['BM'], Bin], [PB[bk0]])
            act(X0[:, 64:128], P[0:64, bk0, 128:192], AF.Copy, [PB[bk0]], [d['BX0']])
            Nk = M[:, 0:64]
            NkT = M[:, 256:320]
            BNk = d['BM']
            xi = 0
            for k in range(6):
                Xc, Xn = d['X'][xi], d['X'][1 - xi]
                BXc, BXn = d['BX%d' % xi], d['BX%d' % (1 - xi)]
                mm(P[0:64, bk2, 0:128], Nk, Xc[:], [BNk, BXc], [PB[bk2]])
                tt('dve', Xn[:], P[0:64, bk2, 0:128], Xc[:], ALU.add, [PB[bk2], BXc], [BXn])
                xi = 1 - xi
                if k < 5:
                    NNn = d['NN'][k % 2]
                    BNn = d['BNN%d' % (k % 2)]
                    mm(P[0:64, bk2, 128:192], NkT, Nk, [BNk], [PB[bk2]])
                    mm(P[0:64, bk2, 192:256], Nk, NkT, [BNk], [PB[bk2]])
                    act(NNn[:], P[0:64, bk2, 128:256], AF.Copy, [PB[bk2]], [BNn])
                    Nk, NkT, BNk = NNn[:, 0:64], NNn[:, 64:128], BNn
            Xf = d['X'][xi]
            BXf = d['BX%d' % xi]
            At, Wt = Xf[:, 0:64], Xf[:, 64:128]
            Bt, Kt = d['BKt'][:, 0, :], d['BKt'][:, 1, :]
            Vt = tm[:, 3, :]
            mm(P[0:64, bk0, 192:256], At, Mbr, [BXf, d['BM']], [PB[bk0]])
            tt('dve', d['Rhat'][:], P[0:64, bk0, 192:256], d['AR'][:, 64:128], ALU.add, [PB[bk0], d['BAR']], [d['BRhat']])
            mm(P[0:64, bk0, 256:320], At, Bt, [BXf, d['BBKt']], [PB[bk0]])
            tt('dve', d['G0'][:], P[0:64, bk0, 256:320], ident, ALU.add, [PB[bk0], B_c], [d['BG0']])
            mm(P[0:64, bk0, 320:384], Wt, Mbr, [BXf, d['BM']], [PB[bk0]], start=True, stop=False)
            mm(P[0:64, bk0, 320:384], Vt, Mkr, [Bin, d['BM']], [PB[bk0]], start=False, stop=True)
            act(d['Y0'][:], P[0:64, bk0, 320:384], AF.Copy, [PB[bk0]], [d['BY0']])
            mm(P[0:64, bk0, 384:448], Bt, Wt, [BXf, d['BBKt']], [PB[bk0]], start=True, stop=False)
            mm(P[0:64, bk0, 384:448], Kt, Vt, [Bin, d['BBKt']], [PB[bk0]], start=False, stop=True)
            PC = d['Ep'][:, 127:128]
            act(d['HT'][:], P[0:64, bk0, 384:448], AF.Copy, [PB[bk0], BE], [d['BHT']], scale=PC)
            Sc, Sn = ST[it][c % 2], ST[it][(c + 1) % 2]
            BSc, BSn = B_ST[it][c % 2], B_ST[it][(c + 1) % 2]
            sb = 6 + (n % 2)
            mm(P[0:64, sb, 0:64], Sc[:], d['Rhat'][:], [BSc, d['BRhat']], [PB[sb]])
            mm(P[0:64, sb, 64:128], d['G0'][:], Sc[:], [BSc, d['BG0']], [PB[sb]])
            tt('dve', d['ysb'][:], P[0:64, sb, 0:64], d['Y0'][:], ALU.add, [PB[sb], d['BY0']], [d['Bysb']])
            (lambda out, in0, scalar, in1, reads, writes: S.op('dve', lambda e: e.scalar_tensor_tensor(
                out=out, in0=in0, scalar=scalar, in1=in1, op0=ALU.mult, op1=ALU.add), reads=reads, writes=writes))(
                Sn[:], P[0:64, sb, 64:128], PC, d['HT'][:], [PB[sb], BE, d['BHT']], [BSn])
            S.dma('sp', y_o[it, c], d['ysb'][:], reads=[d['Bysb']])
        S.emit()
    return nc


def scan2_host_inputs(prep, nstep=NSTEP):
    scan_l, scan_c = prep['scan']
    lw_l, lw_c = prep['dec']
    rkv_l, rkv_c = prep['rkv']
    nch = nstep // CH
    t = np.arange(64)
    tri = (t[:, None] <= t[None, :]).astype(np.float32)
    su = (t[:, None] < t[None, :]).astype(np.float32)
    sl = (t[:, None] > t[None, :]).astype(np.float32)
    cst = np.ascontiguousarray(np.stack([tri, su, tri, su, tri, sl, np.eye(64, dtype=np.float32), tri.T], 1))

    def seq(lat, ctx, b, z):
        s = np.concatenate([ctx[b], lat[b]], 0) if z == 0 else np.concatenate([ctx[b][::-1], lat[b][::-1]], 0)
        return s[:nstep]
    maps = []
    for core in range(NCORE):
        tm = np.zeros((3, nch, CH, 4, 64), np.float32)
        fm = np.zeros((3, nch, 64, 4, CH), np.float32)
        lw = np.zeros((3, nch, 64, 2, 64), np.float32)
        for it in range(3):
            item = core * 3 + it
            z, b, h = item // 12, (item // 6) % 2, item % 6
            hs = slice(h * 64, (h + 1) * 64)
            sc = seq(scan_l, scan_c, b, z).reshape(nstep, 6, 384).astype(np.float32)
            a_, r_, b_, k_ = sc[:, 0, hs], sc[:, 1, hs], sc[:, 2 + z, hs], sc[:, 4 + z, hs]
            v_ = seq(rkv_l, rkv_c, b, z)[:, 768 + h * 64:768 + (h + 1) * 64]
            l_ = seq(lw_l, lw_c, b, z).reshape(nstep, 2, 384)[:, z, hs]
            tm[it] = np.stack([a_, b_, k_, v_], 1).reshape(nch, CH, 4, 64)
            fm[it] = np.stack([a_, r_, b_, k_], 1).reshape(nch, CH, 4, 64).transpose(0, 3, 2, 1)
            lc = l_.reshape(nch, CH, 64)
            lw[it] = np.stack([lc, lc.transpose(0, 2, 1)], 2)
        maps.append(dict(tm=tm, fm=fm, lw=lw, cst=cst))
    return maps


def scan2_host_outputs(ys, nstep=NSTEP):
    yl = np.zeros((2, 2, 8192, 384), np.float32)
    yc = np.zeros((2, 2, 256, 384), np.float32)
    nch = nstep // CH
    for core in range(NCORE):
        for it in range(3):
            item = core * 3 + it
            z, b, h = item // 12, (item // 6) % 2, item % 6
            y = np.zeros((NSTEP, 64), np.float32)
            y[:nstep] = ys[core][it].transpose(0, 2, 1).reshape(nstep, 64)
            c_, l_ = y[:256], y[256:]
            if z == 1:
                c_, l_ = c_[::-1], l_[::-1]
            yc[z, b, :, h * 64:(h + 1) * 64] = c_
            yl[z, b, :, h * 64:(h + 1) * 64] = l_
    return yl, yc


def run_scan2(prep, nstep=NSTEP):
    nc = _get(f'scan2_{nstep}', lambda: build_scan2(nstep))
    maps = scan2_host_inputs(prep, nstep)
    res = run_bass_kernel_spmd(nc, maps, core_ids=list(range(NCORE)))
    return scan2_host_outputs([r['y'] for r in res.results], nstep)


def build_attn(nblk=16, with_ctx=True):
    nc = bass.Bass("TRN2", target_bir_lowering=False)
    NQ = nblk * 128
    QT_d = nc.dram_tensor("QT", [64, 6, NQ], BF16, kind="ExternalInput").ap()
    KT_d = nc.dram_tensor("KT", [64, 2, NQ + 256], BF16, kind="ExternalInput").ap()
    KcT_d = nc.dram_tensor("KcT", [64, 2, 256], BF16, kind="ExternalInput").ap()
    V_d = nc.dram_tensor("V", [128, nblk + 2, 2, 65], BF16, kind="ExternalInput").ap()
    Vc_d = nc.dram_tensor("Vc", [128, 2, 2, 65], BF16, kind="ExternalInput").ap()
    mask_d = nc.dram_tensor("mask", [128, 2, 128], BF16, kind="ExternalInput").ap()
    sink_d = nc.dram_tensor("sink", [1, 6], F32, kind="ExternalInput").ap()
    QcT_d = nc.dram_tensor("QcT", [64, 6, 64], BF16, kind="ExternalInput").ap()
    o_d = nc.dram_tensor("o", [NQ, 384], F32, kind="ExternalOutput").ap()
    oc_d = nc.dram_tensor("oc", [64, 384], F32, kind="ExternalOutput").ap()
    with ExitStack() as st:
        S = Sched(nc, st)
        P = st.enter_context(nc.psum_tensor("P", [128, 8, 512], F32))
        PB = [S.buf(f'pb{i}') for i in range(8)]
        QT = S.sb("QT_sb", [64, 6, NQ], BF16)
        KT = S.sb("KT_sb", [64, 2, NQ + 256], BF16)
        KcT = S.sb("KcT_sb", [64, 2, 256], BF16)
        V = S.sb("V_sb", [128, nblk + 2, 2, 65], BF16)
        Vc = S.sb("Vc_sb", [128, 2, 2, 65], BF16)
        mask = S.sb("mask_sb", [128, 2, 128], BF16)
        esink = S.sb("esink", [128, 6], F32)
        QcT = S.sb("QcT_sb", [64, 6, 64], BF16)
        B_in = S.buf('in')
        B_es = S.buf('es')
        for t, d in [(QT, QT_d), (KT, KT_d), (KcT, KcT_d), (V, V_d), (Vc, Vc_d), (mask, mask_d), (QcT, QcT_d)]:
            S.dma('sp', t[:], d, writes=[S.buf()])
        S.dma('sp', esink[:], mkap(sink_d, 0, [(0, 128), (1, 6)]), writes=[B_es])
        S.op('act', lambda e: e.activation(out=esink[:], in_=esink[:], func=AF.Exp), reads=[B_es], writes=[B_es])
        PT = [S.sb(f"PT{i}", [128, 384], BF16) for i in range(5)]
        B_PT = [S.buf(f'PT{i}') for i in range(5)]
        ot = [S.sb(f"ot{i}", [128, 384], F32) for i in range(2)]
        B_ot = [S.buf(f'ot{i}') for i in range(2)]
        den = S.sb("den", [128, 2, 3], F32)
        B_den = [S.buf('den0'), S.buf('den1')]
        it = [0]

        def attn_block(q_ap, nq, chunks, kh, otile, Bot):
            i = it[0]
            it[0] += 1
            nch = len(chunks)
            for c, (kT_ap, v_ap, mi) in enumerate(chunks):
                S.op('pe', lambda e, c=c, kT_ap=kT_ap: e.matmul(P[:, c, 0:3 * nq].rearrange("p (g q) -> p g q", g=3), lhsT=kT_ap, rhs=q_ap,
                                                                  start=True, stop=True),
                     reads=[B_in], writes=[PB[c]])
                S.op('act', lambda e, c=c: e.activation(out=PT[c][:, 0:3 * nq], in_=P[:, c, 0:3 * nq], func=AF.Exp, scale=0.125),
                     reads=[PB[c]], writes=[B_PT[c]])
                if mi is not None:
                    eng = 'dve' if mi == 0 else 'pool'
                    S.op(eng, lambda e, c=c, mi=mi: e.tensor_tensor(out=PT[c][:, 0:3 * nq].rearrange("p (g q) -> p g q", g=3),
                                                                     in0=PT[c][:, 0:3 * nq].rearrange("p (g q) -> p g q", g=3),
                                                                     in1=mask[:, mi, 0:nq].unsqueeze(1).to_broadcast([128, 3, nq]), op=ALU.mult),
                         reads=[B_in], writes=[B_PT[c]])
            ob = 5 + (i % 2)
            for g in range(3):
                for c, (kT_ap, v_ap, mi) in enumerate(chunks):
                    S.op('pe', lambda e, c=c, g=g, v_ap=v_ap: e.matmul(P[0:nq, ob, g * 65:(g + 1) * 65], lhsT=PT[c][:, g * nq:(g + 1) * nq], rhs=v_ap,
                                                                     start=(c == 0), stop=(c == nch - 1)),
                         reads=[B_PT[c], B_in], writes=[PB[ob]])
            ov = P[0:nq, ob, 0:195].rearrange("p (g e) -> p g e", g=3)
            dn = den[0:nq, i % 2, :]
            S.op('dve', lambda e: e.tensor_tensor(out=dn, in0=ov[:, :, 64], in1=esink[0:nq, kh * 3:kh * 3 + 3], op=ALU.add),
                 reads=[PB[ob], B_es], writes=[B_den[i % 2]])
            S.op('dve', lambda e: e.reciprocal(out=dn, in_=dn), reads=[B_den[i % 2]], writes=[B_den[i % 2]])
            S.op('dve', lambda e: e.tensor_tensor(out=otile[0:nq, kh * 192:(kh + 1) * 192].rearrange("p (g e) -> p g e", g=3), in0=ov[:, :, 0:64],
                                                  in1=dn.unsqueeze(2).to_broadcast([nq, 3, 64]), op=ALU.mult),
                 reads=[PB[ob], B_den[i % 2]], writes=[Bot])

        all_in = [o for o in S.ops['sp']]
        bo = S.op('pool', lambda e: e.memset(den[:].rearrange("p a b -> p (a b)"), 0.0), writes=[B_in, B_den[0], B_den[1]])
        bo.deps.extend(all_in)
        for n in range(nblk):
            otile = ot[n % 2]
            Bot = B_ot[n % 2]
            for kh in range(2):
                chunks = []
                for j, mi in [(0, 0), (1, None), (2, 1)]:
                    chunks.append((KT[:, kh, (n + j) * 128:(n + j + 1) * 128], V[:, n + j, kh, :], mi))
                for j in range(2):
                    chunks.append((KcT[:, kh, j * 128:(j + 1) * 128], Vc[:, j, kh, :], None))
                attn_block(QT[:, kh * 3:kh * 3 + 3, n * 128:(n + 1) * 128], 128, chunks, kh, otile, Bot)
            S.dma('sp', o_d[n * 128:(n + 1) * 128, :], otile[:], reads=[Bot])
        if with_ctx:
            otile = ot[nblk % 2]
            Bot = B_ot[nblk % 2]
            for kh in range(2):
                chunks = [(KcT[:, kh, j * 128:(j + 1) * 128], Vc[:, j, kh, :], None) for j in range(2)]
                attn_block(QcT[:, kh * 3:kh * 3 + 3, :], 64, chunks, kh, otile, Bot)
            S.dma('sp', oc_d, otile[0:64, :], reads=[Bot])
        S.emit()
    return nc


def run_attn(prep, sink):
    import ml_dtypes
    bf = ml_dtypes.bfloat16
    nc = _get('attn', build_attn)
    qk_l, qk_c = prep['qk']
    v_l, v_c = prep['v']
    kk = np.arange(128)[:, None]
    qq = np.arange(128)[None, :]
    mask = np.stack([(kk >= qq), (kk <= qq)], 1).astype(np.float32).astype(bf)
    maps = []
    for core in range(NCORE):
        b, q = core // 4, core % 4
        s0 = q * TL
        QT = np.ascontiguousarray(qk_l[b, s0:s0 + TL, 0:384].reshape(TL, 6, 64).transpose(2, 1, 0))
        kpad = np.zeros((8192 + 256, 128), bf)
        kpad[128:128 + 8192] = qk_l[b, :, 384:512]
        KT = np.ascontiguousarray(kpad[s0:s0 + TL + 256].reshape(TL + 256, 2, 64).transpose(2, 1, 0))
        KcT = np.ascontiguousarray(qk_c[b, :, 384:512].reshape(256, 2, 64).transpose(2, 1, 0))
        vpad = np.zeros((8192 + 256, 2, 65), bf)
        vpad[128:128 + 8192, :, 0:64] = v_l[b].reshape(8192, 2, 64)
        vpad[128:128 + 8192, :, 64] = 1.0
        V = np.ascontiguousarray(vpad[s0:s0 + TL + 256].reshape(18, 128, 2, 65).transpose(1, 0, 2, 3))
        vc = np.ones((256, 2, 65), bf)
        vc[:, :, 0:64] = v_c[b].reshape(256, 2, 64)
        Vc = np.ascontiguousarray(vc.reshape(2, 128, 2, 65).transpose(1, 0, 2, 3))
        QcT = np.ascontiguousarray(qk_c[b, q * TC:(q + 1) * TC, 0:384].reshape(TC, 6, 64).transpose(2, 1, 0))
        maps.append(dict(QT=QT, KT=KT, KcT=KcT, V=V, Vc=Vc, mask=mask, sink=np.ascontiguousarray(sink.reshape(1, 6).astype(np.float32)), QcT=QcT))
    res = run_bass_kernel_spmd(nc, maps, core_ids=list(range(NCORE)))
    al = np.zeros((2, 8192, 384), np.float32)
    ac = np.zeros((2, 256, 384), np.float32)
    for core in range(NCORE):
        b, q = core // 4, core % 4
        al[b, q * TL:(q + 1) * TL] = res.results[core]['o']
        ac[b, q * TC:(q + 1) * TC] = res.results[core]['oc']
    return al, ac


def build_fourier(with_ctx=True):
    nc = bass.Bass("TRN2", target_bir_lowering=False)
    x0_d = nc.dram_tensor("x0", [64, 64, 128], F32, kind="ExternalInput").ap()
    xc_d = nc.dram_tensor("xc0", [64, 256], F32, kind="ExternalInput").ap()
    cs64_d = nc.dram_tensor("cs64", [64, 128], F32, kind="ExternalInput").ap()
    f128_d = nc.dram_tensor("f128", [128, 3, 128], F32, kind="ExternalInput").ap()
    tw_d = nc.dram_tensor("tw", [128, 2, 64], F32, kind="ExternalInput").ap()
    f64_d = nc.dram_tensor("f64", [64, 2, 64], F32, kind="ExternalInput").ap()
    f256_d = nc.dram_tensor("f256", [128, 2, 2, 256], F32, kind="ExternalInput").ap()
    scr = nc.dram_tensor("scr", [128, 64, 128], F32, kind="ExternalOutput").ap()
    y_d = nc.dram_tensor("y", [64, 128 * 64], F32, kind="ExternalOutput").ap()
    yc_d = nc.dram_tensor("yc", [256, 64], F32, kind="ExternalOutput").ap()
    with ExitStack() as st:
        S = Sched(nc, st)
        P = st.enter_context(nc.psum_tensor("P", [128, 8, 512], F32))
        PB = [S.buf(f'pb{i}') for i in range(8)]
        X0 = S.sb("X0", [64, 64 * 128], F32)
        X1 = S.sb("X1", [128, 64, 128], F32)
        B2 = S.sb("B2", [64, 128, 128], F32)
        cs64 = S.sb("cs64_sb", [64, 128], F32)
        f128 = S.sb("f128_sb", [128, 3, 128], F32)
        tw = S.sb("tw_sb", [128, 2, 64], F32)
        f64 = S.sb("f64_sb", [64, 2, 64], F32)
        f256 = S.sb("f256_sb", [128, 2, 2, 256], F32)
        xc = S.sb("xc_sb", [64, 256], F32)
        zc = S.sb("zc_sb", [128, 2, 128], F32)
        ycs = S.sb("ycs", [128, 2, 64], F32)
        Bt = [S.sb(f"Bt{i}", [128, 8, 2, 64], F32) for i in range(2)]
        tmp = [S.sb(f"ftmp{i}", [128, 8, 64], F32) for i in range(2)]
        B_X0, B_X1, B_B2, B_c, B_scr = S.buf('X0'), S.buf('X1'), S.buf('B2'), S.buf('c'), S.buf('scr')
        B_Bt = [S.buf('Bt0'), S.buf('Bt1')]
        B_tmp = [S.buf('tmp0'), S.buf('tmp1')]
        B_xc, B_zc, B_yc = S.buf('xc'), S.buf('zc'), S.buf('yc')
        S.dma('sp', X0[:], x0_d.rearrange("c a b -> c (a b)"), writes=[B_X0])
        for t, d in [(cs64, cs64_d), (f128, f128_d), (tw, tw_d), (f64, f64_d), (f256, f256_d)]:
            S.dma('sp', t[:], d, writes=[B_c])
        S.dma('sp', xc[:], xc_d, writes=[B_xc])
        for grp in range(16):
            bank = grp % 2
            for j in range(4):
                n2 = grp * 4 + j
                S.op('pe', lambda e, n2=n2, j=j, bank=bank: e.matmul(P[:, bank, j * 128:(j + 1) * 128], lhsT=X0[:, n2 * 128:(n2 + 1) * 128], rhs=cs64[:],
                                                                    start=True, stop=True),
                     reads=[B_X0, B_c], writes=[PB[bank]])
            dst = X1[:, grp * 4:(grp + 1) * 4, :].rearrange("p a b -> p (a b)")
            if grp % 2 == 0:
                S.op('act', lambda e, dst=dst, bank=bank: e.activation(func=AF.Copy, out=dst, in_=P[:, bank, :]), reads=[PB[bank]], writes=[B_X1])
            else:
                S.op('dve', lambda e, dst=dst, bank=bank: e.tensor_copy(out=dst, in_=P[:, bank, :]), reads=[PB[bank]], writes=[B_X1])
        for ch in range(8):
            zr = X1[:, ch * 8:(ch + 1) * 8, 0:64]
            zi = X1[:, ch * 8:(ch + 1) * 8, 64:128]
            ba = 2 + (ch % 2) * 2
            ar = P[:, ba, :].rearrange("p (a b) -> p a b", a=8)
            ai = P[:, ba + 1, :].rearrange("p (a b) -> p a b", a=8)
            S.op('pe', lambda e, ar=ar, zr=zr: e.matmul(ar, lhsT=f128[:, 0, :], rhs=zr, start=True, stop=False), reads=[B_X1, B_c], writes=[PB[ba]])
            S.op('pe', lambda e, ar=ar, zi=zi: e.matmul(ar, lhsT=f128[:, 1, :], rhs=zi, start=False, stop=True), reads=[B_X1, B_c], writes=[PB[ba]])
            S.op('pe', lambda e, ai=ai, zi=zi: e.matmul(ai, lhsT=f128[:, 0, :], rhs=zi, start=True, stop=False), reads=[B_X1, B_c], writes=[PB[ba + 1]])
            S.op('pe', lambda e, ai=ai, zr=zr: e.matmul(ai, lhsT=f128[:, 2, :], rhs=zr, start=False, stop=True), reads=[B_X1, B_c], writes=[PB[ba + 1]])
            tc_ = tw[:, 0, ch * 8:(ch + 1) * 8].unsqueeze(2).to_broadcast([128, 8, 64])
            ts_ = tw[:, 1, ch * 8:(ch + 1) * 8].unsqueeze(2).to_broadcast([128, 8, 64])
            bt = Bt[ch % 2]
            Bb = B_Bt[ch % 2]
            t0_, t1_ = tmp
            S.op('dve', lambda e, bt=bt, ar=ar, tc_=tc_: e.tensor_tensor(out=bt[:, :, 0, :], in0=ar, in1=tc_, op=ALU.mult), reads=[PB[ba], B_c], writes=[Bb])
            S.op('dve', lambda e, ai=ai, ts_=ts_: e.tensor_tensor(out=t0_[:], in0=ai, in1=ts_, op=ALU.mult), reads=[PB[ba + 1], B_c], writes=[B_tmp[0]])
            S.op('dve', lambda e, bt=bt, ai=ai, tc_=tc_: e.tensor_tensor(out=bt[:, :, 1, :], in0=ai, in1=tc_, op=ALU.mult), reads=[PB[ba + 1], B_c], writes=[Bb])
            S.op('dve', lambda e, ar=ar, ts_=ts_: e.tensor_tensor(out=t1_[:], in0=ar, in1=ts_, op=ALU.mult), reads=[PB[ba], B_c], writes=[B_tmp[1]])
            S.op('pool', lambda e, bt=bt: e.tensor_tensor(out=bt[:, :, 0, :], in0=bt[:, :, 0, :], in1=t0_[:], op=ALU.add), reads=[B_tmp[0]], writes=[Bb])
            S.op('pool', lambda e, bt=bt: e.tensor_tensor(out=bt[:, :, 1, :], in0=bt[:, :, 1, :], in1=t1_[:], op=ALU.subtract), reads=[B_tmp[1]], writes=[Bb])
            S.dma('sp', scr[:, ch * 8:(ch + 1) * 8, :], bt[:].rearrange("p a b c -> p a (b c)"), reads=[Bb], writes=[B_scr], sembuf=Bb)
        for q in range(4):
            S.dma('sp', B2[:, q * 32:(q + 1) * 32, :], scr[q * 32:(q + 1) * 32, :, :].rearrange("k n c -> n k c"), reads=[B_scr], writes=[B_B2])
        Y = X0
        for ch in range(16):
            bank = 6 + (ch % 2)
            br = B2[:, ch * 8:(ch + 1) * 8, 0:64]
            bi = B2[:, ch * 8:(ch + 1) * 8, 64:128]
            ov = P[0:64, bank, :].rearrange("p (a b) -> p a b", a=8)
            S.op('pe', lambda e, ov=ov, br=br: e.matmul(ov, lhsT=f64[:, 0, :], rhs=br, start=True, stop=False), reads=[B_B2, B_c], writes=[PB[bank]])
            S.op('pe', lambda e, ov=ov, bi=bi: e.matmul(ov, lhsT=f64[:, 1, :], rhs=bi, start=False, stop=True), reads=[B_B2, B_c], writes=[PB[bank]])
            if ch % 2 == 0:
                S.op('act', lambda e, ch=ch, bank=bank: e.activation(func=AF.Copy, out=Y[:, ch * 512:(ch + 1) * 512], in_=P[0:64, bank, :]), reads=[PB[bank]], writes=[B_X0])
            else:
                S.op('dve', lambda e, ch=ch, bank=bank: e.tensor_copy(out=Y[:, ch * 512:(ch + 1) * 512], in_=P[0:64, bank, :]), reads=[PB[bank]], writes=[B_X0])
        S.dma('sp', y_d, Y[:], reads=[B_X0])
        if with_ctx:
            for nt in range(2):
                S.op('pe', lambda e, nt=nt: e.matmul(P[:, 0, nt * 128:(nt + 1) * 128], lhsT=xc[:, nt * 128:(nt + 1) * 128], rhs=cs64[:], start=True, stop=True),
                     reads=[B_xc, B_c], writes=[PB[0]])
            S.op('act', lambda e: e.activation(func=AF.Copy, out=zc[:].rearrange("p a b -> p (a b)"), in_=P[:, 0, 0:256]), reads=[PB[0]], writes=[B_zc])
            for kt in range(2):
                ov = P[:, 1, kt * 64:(kt + 1) * 64]
                first = True
                for nt in range(2):
                    S.op('pe', lambda e, ov=ov, nt=nt, kt=kt, first=first: e.matmul(ov, lhsT=f256[:, nt, 0, kt * 128:(kt + 1) * 128], rhs=zc[:, nt, 0:64],
                                                                                  start=first, stop=False), reads=[B_zc, B_c], writes=[PB[1]])
                    first = False
                    S.op('pe', lambda e, ov=ov, nt=nt, kt=kt: e.matmul(ov, lhsT=f256[:, nt, 1, kt * 128:(kt + 1) * 128], rhs=zc[:, nt, 64:128],
                                                                     start=False, stop=(nt == 1)), reads=[B_zc, B_c], writes=[PB[1]])
            S.op('act', lambda e: e.activation(func=AF.Copy, out=ycs[:].rearrange("p a b -> p (a b)"), in_=P[:, 1, 0:128]), reads=[PB[1]], writes=[B_yc])
            S.dma('sp', yc_d.rearrange("(kt p) d -> p kt d", p=128), ycs[:], reads=[B_yc])
        S.emit()
    return nc


def fourier_tables():
    f64_ = np.float64

    def cs(n):
        i = np.arange(n)
        ang = 2 * np.pi * np.outer(i, i) / n
        return np.cos(ang), np.sin(ang)
    c64, s64 = cs(64)
    c128, s128 = cs(128)
    c256, s256 = cs(256)
    cs64 = np.concatenate([c64, -s64], 1).astype(np.float32)
    f128 = np.stack([c128, s128, -s128], 1).astype(np.float32)
    k1 = np.arange(128)[:, None]
    n2 = np.arange(64)[None, :]
    ang = 2 * np.pi * k1 * n2 / 8192.0
    tw = np.stack([np.cos(ang), np.sin(ang)], 1).astype(np.float32)
    sc = 1.0 / np.sqrt(8192.0 * 64.0)
    f64t = np.stack([c64 * sc, s64 * sc], 1).astype(np.float32)
    scc = 1.0 / np.sqrt(256.0 * 64.0)
    f256 = np.stack([c256 * scc, s256 * scc], 1).astype(np.float32)
    f256 = np.ascontiguousarray(f256.reshape(2, 128, 2, 256).transpose(1, 0, 2, 3))
    return dict(cs64=cs64, f128=f128, tw=tw, f64=f64t, f256=f256)


def run_fourier(fl, fc):
    nc = _get('fourier', build_fourier)
    tabs = fourier_tables()
    maps = []
    for core in range(NCORE):
        b, g = core // 4, core % 4
        xg = fl[b, :, g * 64:(g + 1) * 64]
        x0 = np.ascontiguousarray(xg.reshape(128, 64, 64).transpose(2, 1, 0))
        xc0 = np.ascontiguousarray(fc[b, :, g * 64:(g + 1) * 64].T)
        m = dict(x0=x0, xc0=xc0)
        m.update(tabs)
        maps.append(m)
    res = run_bass_kernel_spmd(nc, maps, core_ids=list(range(NCORE)))
    ol = np.zeros((2, 8192, 256), np.float32)
    oc = np.zeros((2, 256, 256), np.float32)
    for core in range(NCORE):
        b, g = core // 4, core % 4
        ol[b, :, g * 64:(g + 1) * 64] = res.results[core]['y'].reshape(8192, 64)
        oc[b, :, g * 64:(g + 1) * 64] = res.results[core]['yc']
    return ol, oc


def kernel(x, c, ctx, c_ctx, mod_w, mod_b, norm_g, ffn1_wi, ffn1_wo, mix_w_in, mix_w_out, attn_sink,
           rwkv_conv, rwkv_w0, rwkv_w2, rwkv_a0, rwkv_a2, rwkv_g2, rwkv_k_k, rwkv_k_a, rwkv_r_k,
           rwkv_ln_g, rwkv_ln_b, ffn2_wi, ffn2_wo):
    f = lambda a: np.asarray(a, dtype=np.float32)
    x, c, ctx, c_ctx = f(x), f(c), f(ctx), f(c_ctx)
    xs = shard_tokens(x, ctx)
    for li in range(2):
        mw, mb, g = f(mod_w[li]), f(mod_b[li]), f(norm_g[li])
        xs = run_ffn(xs, c, c_ctx, mw[:, 0:3 * D], mb[0:3 * D], g[0], g[1], f(ffn1_wi[li]), f(ffn1_wo[li]))
        zs = run_inproj(xs, c, c_ctx, mw[:, 3 * D:5 * D], mb[3 * D:5 * D], g[2], f(mix_w_in[li]))
        zl, zc = unshard_tokens(zs, INW)
        prep = run_prep(zl, zc, f(rwkv_conv[li]), f(rwkv_w0[li]), f(rwkv_w2[li]), f(rwkv_a0[li]), f(rwkv_a2[li]),
                        f(rwkv_k_k[li]), f(rwkv_k_a[li]))
        al, ac = run_attn(prep, f(attn_sink[li]))
        fl, fc = run_fourier(np.ascontiguousarray(zl[..., 0:256]), np.ascontiguousarray(zc[..., 0:256]))
        yl, yc = run_scan2(prep)
        fa = shard_tokens(np.concatenate([fl, al], -1), np.concatenate([fc, ac], -1))
        yf = shard_tokens(yl[0], yc[0])
        yb = shard_tokens(yl[1], yc[1])
        rkv = shard_tokens(prep['rkv'][0], prep['rkv'][1])
        zg = shard_tokens(zl[..., 2304:2432], zc[..., 2304:2432])
        vecs = np.stack([f(rwkv_r_k[li]).reshape(384), f(rwkv_ln_g[li]), f(rwkv_ln_b[li])], 0)
        xs = run_outproj(xs, c, c_ctx, mw[:, 5 * D:6 * D], mb[5 * D:6 * D], g[3], f(mix_w_out[li]), fa, yf, yb, rkv, zg,
                         f(rwkv_g2[li]), vecs)
        xs = run_ffn(xs, c, c_ctx, mw[:, 6 * D:9 * D], mb[6 * D:9 * D], g[4], g[5], f(ffn2_wi[li]), f(ffn2_wo[li]))
    xl, _ = unshard_tokens(xs)
    return xl
```

```python
import numpy as np
import concourse.bass as bass
import concourse.mybir as mybir
from concourse.bass_utils import run_bass_kernel_spmd
from contextlib import ExitStack

F32 = mybir.dt.float32
BF16 = mybir.dt.bfloat16
AF = mybir.ActivationFunctionType
ALU = mybir.AluOpType
AX = mybir.AxisListType

D = 1024
FF = 2752
NF = 22
NCORE = 8
TL = 2048
TC = 64
EPS = 1e-6

ENGS = ['pe', 'act', 'dve', 'pool', 'sp']


class Buf:
    __slots__ = ('name', 'w', 'r', 'sem', 'semval')

    def __init__(self, name):
        self.name = name
        self.w = None
        self.r = []
        self.sem = None
        self.semval = 0


class Op:
    __slots__ = ('eng', 'fn', 'deps', 'is_dma', 'signal', 'val', 'sem', 'idx')

    def __init__(self, eng, fn, is_dma=False):
        self.eng = eng
        self.fn = fn
        self.deps = []
        self.is_dma = is_dma
        self.signal = False
        self.val = 0
        self.sem = None
        self.idx = 0


class Sched:
    def __init__(self, nc, stack):
        self.nc = nc
        self.stack = stack
        self.ops = {e: [] for e in ENGS}
        self.esem = {}
        for e in ['pe', 'act', 'dve', 'pool']:
            self.esem[e] = stack.enter_context(nc.semaphore('es_' + e))
        self.dma_bufs = []
        self.nbuf = 0

    def sb(self, name, shape, dtype):
        return self.stack.enter_context(self.nc.sbuf_tensor(name, list(shape), dtype))

    def buf(self, name=None):
        self.nbuf += 1
        return Buf(name or f'b{self.nbuf}')

    def _track(self, o, reads, writes):
        deps = []
        for b in reads:
            if b.w is not None:
                deps.append(b.w)
        for b in writes:
            if b.w is not None:
                deps.append(b.w)
            deps.extend(b.r)
        seen = set()
        for d in deps:
            if d is o or id(d) in seen:
                continue
            seen.add(id(d))
            o.deps.append(d)
        for b in reads:
            b.r.append(o)
        for b in writes:
            b.w = o
            b.r = []

    def op(self, eng, fn, reads=(), writes=()):
        o = Op(eng, fn)
        self._track(o, reads, writes)
        o.idx = len(self.ops[eng])
        self.ops[eng].append(o)
        return o

    def dma(self, eng, out, in_, reads=(), writes=(), sembuf=None, **kw):
        o = Op(eng, None, is_dma=True)
        self._track(o, reads, writes)
        sb_ = sembuf or (writes[0] if writes else reads[0])
        if sb_.sem is None:
            sb_.sem = self.stack.enter_context(self.nc.semaphore('ds_' + sb_.name))
            self.dma_bufs.append(sb_)
        sb_.semval += 16
        o.sem = sb_.sem
        o.val = sb_.semval
        o.fn = lambda e: e.dma_start(out=out, in_=in_, **kw)
        o.idx = len(self.ops[eng])
        self.ops[eng].append(o)
        return o

    def _needs_sync(self, o, d):
        if d.is_dma:
            return True
        if d.eng != o.eng:
            return True
        return (o.idx - d.idx) <= 1 and d.eng != 'pe'

    def emit(self):
        for e in ENGS:
            for o in self.ops[e]:
                for d in o.deps:
                    if (not d.is_dma) and self._needs_sync(o, d):
                        d.signal = True
        EPOCH = 4000
        for e in ENGS:
            c = 0
            sems = [self.esem[e]] if e in self.esem else []
            for o in self.ops[e]:
                if o.is_dma:
                    continue
                if o.signal:
                    ep, v = divmod(c, EPOCH)
                    if ep >= len(sems):
                        sems.append(self.stack.enter_context(self.nc.semaphore(f'es_{e}_{ep}')))
                    c += 1
                    o.val = v + 1
                    o.sem = sems[ep]
        finals = [(b.sem, b.semval) for b in self.dma_bufs]

        def run(e, h):
            seen = {}
            for o in self.ops[e]:
                for d in o.deps:
                    if not self._needs_sync(o, d):
                        continue
                    k = id(d.sem)
                    if seen.get(k, 0) >= d.val:
                        continue
                    seen[k] = d.val
                    h.wait_ge(d.sem, d.val)
                ins = o.fn(h)
                if o.is_dma:
                    ins.then_inc(o.sem, 16)
                elif o.signal:
                    ins.then_inc(o.sem, 1)
            if e == 'sp':
                for s, v in finals:
                    h.wait_ge(s, v)

        with self.nc.Block() as block:
            @block.tensor
            def _(h):
                run('pe', h)

            @block.scalar
            def _(h):
                run('act', h)

            @block.vector
            def _(h):
                run('dve', h)

            @block.gpsimd
            def _(h):
                run('pool', h)

            @block.sync
            def _(h):
                run('sp', h)


def mkap(t, offset, pat):
    return bass.AP(t.tensor, offset, [list(p) for p in pat])


class Ctx:
    pass


def setup_common(S, nc):
    C = Ctx()
    C.P = S.stack.enter_context(nc.psum_tensor("P", [128, 8, 512], F32))
    C.PB = [S.buf(f'pb{i}') for i in range(8)]
    C.ident_f = S.sb("ident_f", [128, 128], F32)
    C.ident_b = S.sb("ident_b", [128, 128], BF16)
    C.B_ident = S.buf('ident')
    return C


def load_ident(S, C, ident_dram):
    S.dma('sp', C.ident_f[:], ident_dram, writes=[C.B_ident])
    S.op('dve', lambda e: e.tensor_copy(out=C.ident_b[:], in_=C.ident_f[:]), reads=[C.B_ident], writes=[C.B_ident])


def rstd_from_ss(S, ss, rstd, B_ss, B_rstd, n, eps):
    S.op('dve', lambda e: e.tensor_scalar(out=rstd, in0=ss, scalar1=1.0 / n, scalar2=eps, op0=ALU.mult, op1=ALU.add),
         reads=[B_ss], writes=[B_rstd])
    S.op('act', lambda e: e.activation(out=rstd, in_=rstd, func=AF.Sqrt), reads=[B_rstd], writes=[B_rstd])
    S.op('dve', lambda e: e.reciprocal(out=rstd, in_=rstd), reads=[B_rstd], writes=[B_rstd])


def rows_to_cols(S, C, rows_sb, B_rows, off, out_cols, B_out, bank):
    for dc in range(8):
        S.op('pe', lambda e, dc=dc: e.transpose(out=C.P[:, bank, dc * 2:dc * 2 + 2], in_=rows_sb[0:2, off + dc * 128:off + (dc + 1) * 128],
                                                identity=C.ident_f[0:2, 0:2]),
             reads=[B_rows, C.B_ident], writes=[C.PB[bank]])
    S.op('dve', lambda e: e.tensor_copy(out=out_cols[:].rearrange("p a b -> p (a b)"), in_=C.P[:, bank, 0:16]),
         reads=[C.PB[bank]], writes=[B_out])


def rows_bcast(S, C, rows_sb, B_rows, off, sel, B_sel, cond, bank0):
    for hh in range(2):
        S.op('pe', lambda e, hh=hh: e.matmul(C.P[:, bank0 + hh, :], lhsT=sel[0:2, cond, :], rhs=rows_sb[0:2, off + hh * 512:off + (hh + 1) * 512],
                                             start=True, stop=True),
             reads=[B_rows, B_sel], writes=[C.PB[bank0 + hh]])


INW = 2432
GN_EPS = 64e-5


def build_dense(mode, ntiles_l=16, has_ctx=True):
    nc = bass.Bass("TRN2", target_bir_lowering=False)
    T = ntiles_l * 128 + (TC if has_ctx else 0)
    nmod = {'ffn': 3, 'inproj': 2, 'outproj': 1}[mode]
    x = nc.dram_tensor("x", [T, D], F32, kind="ExternalInput").ap()
    condT_d = nc.dram_tensor("condT", [128, 8, 2], F32, kind="ExternalInput").ap()
    mw = nc.dram_tensor("mw", [D, nmod * D], F32, kind="ExternalInput").ap()
    mb = nc.dram_tensor("mb", [2, nmod * D], F32, kind="ExternalInput").ap()
    ident_d = nc.dram_tensor("ident", [128, 128], F32, kind="ExternalInput").ap()
    sel_d = nc.dram_tensor("sel", [2, 2, 128], F32, kind="ExternalInput").ap()
    if mode != 'outproj':
        gpreT_d = nc.dram_tensor("gpreT", [128, 8], F32, kind="ExternalInput").ap()
    if mode != 'inproj':
        gpost_d = nc.dram_tensor("gpost", [1, D], F32, kind="ExternalInput").ap()
        y = nc.dram_tensor("y", [T, D], F32, kind="ExternalOutput").ap()
    if mode == 'ffn':
        wi = nc.dram_tensor("wi", [D, 2 * FF], F32, kind="ExternalInput").ap()
        wo = nc.dram_tensor("wo", [FF, D], F32, kind="ExternalInput").ap()
    elif mode == 'inproj':
        w_d = nc.dram_tensor("w", [D, INW], F32, kind="ExternalInput").ap()
        z_d = nc.dram_tensor("z", [T, INW], F32, kind="ExternalOutput").ap()
    else:
        w_d = nc.dram_tensor("w", [D, D], F32, kind="ExternalInput").ap()
        fa_d = nc.dram_tensor("fa", [T, 640], F32, kind="ExternalInput").ap()
        yf_d = nc.dram_tensor("yf", [T, 384], F32, kind="ExternalInput").ap()
        yb_d = nc.dram_tensor("yb", [T, 384], F32, kind="ExternalInput").ap()
        rkv_d = nc.dram_tensor("rkv", [T, 1152], F32, kind="ExternalInput").ap()
        zg_d = nc.dram_tensor("zg", [T, 128], F32, kind="ExternalInput").ap()
        g2_d = nc.dram_tensor("g2", [128, 384], F32, kind="ExternalInput").ap()
        vec_d = nc.dram_tensor("vecs", [3, 384], F32, kind="ExternalInput").ap()

    with ExitStack() as st:
        S = Sched(nc, st)
        C = setup_common(S, nc)
        P = C.P
        load_ident(S, C, ident_d)
        xg = [S.sb(f"xg{i}", [128, 2, D], F32) for i in range(2)]
        B_xg = [[S.buf(f'xg{i}_{j}') for j in range(2)] for i in range(2)]
        stg_ap = [xg[0][:].rearrange("p a b -> p (a b)"), xg[1][:].rearrange("p a b -> p (a b)")]
        B_stg = [B_xg[0][0], B_xg[1][0]]
        condT = S.sb("condT_sb", [128, 8, 2], F32)
        C.B_cond = S.buf('cond')
        rows_sb = S.sb("rows_sb", [2, nmod * D], F32)
        B_rows = S.buf('rows')
        sel = S.sb("sel_sb", [2, 2, 128], F32)
        B_sel = S.buf('sel')
        B_g = S.buf('g')
        B_modT = S.buf('modT')
        B_GG = S.buf('GG')

        S.dma('sp', condT[:], condT_d, writes=[C.B_cond])
        S.op('act', lambda e: e.activation(out=condT[:], in_=condT[:], func=AF.Silu), reads=[C.B_cond], writes=[C.B_cond])
        S.dma('sp', rows_sb[:], mb, writes=[B_rows])
        S.dma('sp', sel[:], sel_d, writes=[B_sel])
        nchunk = nmod * 2
        for k in range(8):
            ncol = nmod * D
            S.dma('sp', stg_ap[0][:, 0:min(2048, ncol)], mw[k * 128:(k + 1) * 128, 0:min(2048, ncol)], writes=[B_stg[0]])
            if ncol > 2048:
                S.dma('sp', stg_ap[1][:, 0:ncol - 2048], mw[k * 128:(k + 1) * 128, 2048:ncol], writes=[B_stg[1]])
            for n in range(nchunk):
                sa = stg_ap[0] if n < 4 else stg_ap[1]
                cc = n * 512 if n < 4 else (n - 4) * 512
                S.op('pe', lambda e, sa=sa, n=n, k=k, cc=cc: e.matmul(P[0:2, n, :], lhsT=condT[:, k, :], rhs=sa[:, cc:cc + 512],
                                                                     start=(k == 0), stop=(k == 7)),
                     reads=[B_stg[0] if n < 4 else B_stg[1], C.B_cond], writes=[C.PB[n]])
        for n in range(nchunk):
            S.op('dve', lambda e, n=n: e.tensor_tensor(out=rows_sb[0:2, n * 512:(n + 1) * 512], in0=P[0:2, n, :],
                                                      in1=rows_sb[0:2, n * 512:(n + 1) * 512], op=ALU.add),
                 reads=[C.PB[n], B_rows], writes=[B_rows])
        if mode != 'outproj':
            gpreT = S.sb("gpreT_sb", [128, 8], F32)
            S.dma('sp', gpreT[:], gpreT_d, writes=[B_g])
            modT = [S.sb(f"modT{i}", [128, 8, 2], F32) for i in range(2)]
            G1T = S.sb("G1T", [128, 8, 2], F32)
            rows_to_cols(S, C, rows_sb, B_rows, 0, modT[0], B_modT, 6)
            rows_to_cols(S, C, rows_sb, B_rows, D, modT[1], B_modT, 7)
            S.op('dve', lambda e: e.tensor_scalar(out=G1T[:], in0=modT[1][:], scalar1=1.0, scalar2=None, op0=ALU.add),
                 reads=[B_modT], writes=[B_modT])
            S.op('pool', lambda e: e.tensor_tensor(out=G1T[:], in0=G1T[:], in1=gpreT[:].unsqueeze(2).to_broadcast([128, 8, 2]), op=ALU.mult),
                 reads=[B_modT, B_g], writes=[B_modT])
        if mode != 'inproj':
            gpost_bc = S.sb("gpost_bc", [128, D], F32)
            S.dma('sp', gpost_bc[:], mkap(gpost_d, 0, [(0, 128), (1, D)]), writes=[B_g])
            GG = S.sb("GG", [128, 2, D], F32)
            goff = 2 * D if mode == 'ffn' else 0
            gfac = 0.5 if mode == 'ffn' else 1.0
            for cond in range(2):
                rows_bcast(S, C, rows_sb, B_rows, goff, sel, B_sel, cond, 0 + 2 * cond)
                for hh in range(2):
                    S.op('dve', lambda e, cond=cond, hh=hh: e.scalar_tensor_tensor(
                        out=GG[:, cond, hh * 512:(hh + 1) * 512], in0=P[:, 2 * cond + hh, :], scalar=gfac,
                        in1=gpost_bc[:, hh * 512:(hh + 1) * 512], op0=ALU.mult, op1=ALU.mult),
                        reads=[C.PB[2 * cond + hh], B_g], writes=[B_GG])

        cast_engs = ['act', 'pool', 'dve']
        cnt = [0, 0]

        def cast_in(dst_ap, src_dram, rows, cols, Bdst):
            sa = stg_ap[cnt[0] % 2]
            Bs = B_stg[cnt[0] % 2]
            cnt[0] += 1
            S.dma('sp', sa[0:rows, 0:cols], src_dram, writes=[Bs])
            eng = cast_engs[cnt[1] % 3]
            cnt[1] += 1
            if eng == 'act':
                S.op('act', lambda e: e.activation(func=AF.Copy, out=dst_ap, in_=sa[0:rows, 0:cols]), reads=[Bs], writes=[Bdst])
            else:
                S.op(eng, lambda e: e.tensor_copy(out=dst_ap, in_=sa[0:rows, 0:cols]), reads=[Bs], writes=[Bdst])

        B_w = S.buf('w')
        B_wo = S.buf('wo')
        if mode == 'ffn':
            wi_sb = S.sb("wi_sb", [128, 8, 2 * FF], BF16)
            wo_sb = S.sb("wo_sb", [128, NF, D], BF16)
            for k in range(8):
                for q in range(4):
                    c0 = q * 1376
                    cast_in(wi_sb[:, k, c0:c0 + 1376], wi[k * 128:(k + 1) * 128, c0:c0 + 1376], 128, 1376, B_w)
            for fc in range(NF):
                rows = 128 if fc < NF - 1 else 64
                cast_in(wo_sb[0:rows, fc, :], wo[fc * 128:fc * 128 + rows, :], rows, D, B_wo)
        else:
            wcols = INW if mode == 'inproj' else D
            w_sb = S.sb("w_sb", [128, 8, wcols], BF16)
            for k in range(8):
                for c0 in range(0, wcols, 1216 if mode == 'inproj' else 1024):
                    cw = min(1216 if mode == 'inproj' else 1024, wcols - c0)
                    cast_in(w_sb[:, k, c0:c0 + cw], w_d[k * 128:(k + 1) * 128, c0:c0 + cw], 128, cw, B_w)
        if mode == 'outproj':
            g2_sb = S.sb("g2_sb", [128, 384], BF16)
            cast_in(g2_sb[:], g2_d, 128, 384, B_w)
            vec_bc = S.sb("vec_bc", [128, 3, 384], F32)
            B_vec = S.buf('vec')
            S.dma('sp', vec_bc[:], mkap(vec_d, 0, [(0, 128), (384, 3), (1, 384)]), writes=[B_vec])

        tiles = [(i * 128, 128, 0) for i in range(ntiles_l)]
        if has_ctx:
            tiles.append((ntiles_l * 128, TC, 1))
        groups = [tiles[i:i + 2] for i in range(0, len(tiles), 2)]
        xn = S.sb("xn", [128, 2, D], BF16)
        B_xn = [S.buf('xn0'), S.buf('xn1')]
        junk = S.sb("junk", [128, D], BF16)
        B_junk = S.buf('junk')
        stat = S.sb("stat", [128, 8], F32)
        B_stat = [S.buf('st0'), S.buf('st1'), S.buf('st2'), S.buf('st3')]
        hT = S.sb("hT", [128, 8, 256], BF16)
        B_hT = S.buf('hT')
        PBF = P[:, 6, :].bitcast(BF16)
        if mode == 'ffn':
            hid = S.sb("hid", [128, NF, 256], BF16)
            B_hid = S.buf('hid')
            sg = [S.sb(f"sg{i}", [128, 256], F32) for i in range(2)]
            B_sg = [S.buf('sg0'), S.buf('sg1')]
        if mode != 'inproj':
            tmp = S.sb("tmp", [128, D], F32)
            B_tmp = S.buf('tmp')
        if mode == 'inproj':
            zt = [S.sb(f"zt{i}", [128, INW], F32) for i in range(2)]
            B_zt = [S.buf('zt0'), S.buf('zt1')]
        if mode == 'outproj':
            IN = [dict(fa=S.sb(f"fa{i}", [128, 640], F32), yf=S.sb(f"yf{i}", [128, 384], F32), yb=S.sb(f"yb{i}", [128, 384], F32),
                       rkv=S.sb(f"rkv{i}", [128, 1152], F32), zg=S.sb(f"zg{i}", [128, 128], F32)) for i in range(2)]
            B_IN = [S.buf('in0'), S.buf('in1')]
            w1 = S.sb("w1", [128, 384], F32)
            w2 = S.sb("w2", [128, 384], F32)
            w3 = S.sb("w3", [128, 384], F32)
            B_w1, B_w2, B_w3 = S.buf('w1'), S.buf('w2'), S.buf('w3')
            s6 = S.sb("s6", [128, 4, 6], F32)
            B_s6 = [S.buf(f's6{i}') for i in range(4)]
            sgb = S.sb("sgb", [128, 128], BF16)
            sgT = S.sb("sgT", [128, 128], BF16)
            B_sgb, B_sgT = S.buf('sgb'), S.buf('sgT')
            PBF1 = P[:, 1, :].bitcast(BF16)

        tcount = [0]

        def load_group(gi):
            for j, (t0, n, cond) in enumerate(groups[gi]):
                S.dma('sp', xg[gi % 2][0:n, j, :], x[t0:t0 + n, :], writes=[B_xg[gi % 2][j]])

        def load_tile_inputs(ti):
            t0, n, cond = tiles[ti]
            d = IN[ti % 2]
            Bi = B_IN[ti % 2]
            S.dma('sp', d['fa'][0:n, :], fa_d[t0:t0 + n, :], writes=[Bi])
            S.dma('sp', d['yf'][0:n, :], yf_d[t0:t0 + n, :], writes=[Bi])
            S.dma('sp', d['yb'][0:n, :], yb_d[t0:t0 + n, :], writes=[Bi])
            S.dma('sp', d['rkv'][0:n, :], rkv_d[t0:t0 + n, :], writes=[Bi])
            S.dma('sp', d['zg'][0:n, :], zg_d[t0:t0 + n, :], writes=[Bi])

        def v6(ap, n):
            return ap[0:n, :].rearrange("p (h e) -> p h e", h=6)

        def bc6(ap6, n):
            return ap6.unsqueeze(2).to_broadcast([n, 6, 64])

        def readout_tile(ti, j):
            t0, n, cond = tiles[ti]
            d = IN[ti % 2]
            Bi = B_IN[ti % 2]
            r_ = d['rkv'][0:n, 0:384]
            k_ = d['rkv'][0:n, 384:768]
            v_ = d['rkv'][0:n, 768:1152]
            S.op('act', lambda e: e.activation(func=AF.Copy, out=xn[0:n, j, 0:640], in_=d['fa'][0:n, :]), reads=[Bi], writes=[B_xn[j]])
            S.op('act', lambda e: e.activation(out=sgb[0:n, :], in_=d['zg'][0:n, :], func=AF.Sigmoid), reads=[Bi], writes=[B_sgb])
            S.op('pe', lambda e: e.transpose(out=PBF1[:, 0:n], in_=sgb[0:n, :], identity=C.ident_b[0:n, 0:n]),
                 reads=[B_sgb, C.B_ident], writes=[C.PB[1]])
            S.op('act', lambda e: e.activation(func=AF.Copy, out=sgT[:, 0:n], in_=PBF1[:, 0:n]), reads=[C.PB[1]], writes=[B_sgT])
            S.op('pe', lambda e: e.matmul(P[0:n, 0, 0:384], lhsT=sgT[:, 0:n], rhs=g2_sb[:], start=True, stop=True),
                 reads=[B_sgT, B_w], writes=[C.PB[0]])
            S.op('pool', lambda e: e.tensor_tensor(out=w1[0:n, :], in0=d['yf'][0:n, :], in1=d['yb'][0:n, :], op=ALU.add),
                 reads=[Bi], writes=[B_w1])
            S.op('dve', lambda e: e.tensor_reduce(out=s6[0:n, 0, :], in_=v6(w1, n), axis=AX.X, op=ALU.add), reads=[B_w1], writes=[B_s6[0]])
            S.op('dve', lambda e: e.scalar_tensor_tensor(out=v6(w2, n), in0=bc6(s6[0:n, 0, :], n), scalar=-1.0 / 64, in1=v6(w1, n),
                                                         op0=ALU.mult, op1=ALU.add), reads=[B_s6[0], B_w1], writes=[B_w2])
            S.op('pool', lambda e: e.tensor_tensor(out=w3[0:n, :], in0=w2[0:n, :], in1=w2[0:n, :], op=ALU.mult), reads=[B_w2], writes=[B_w3])
            S.op('dve', lambda e: e.tensor_reduce(out=s6[0:n, 1, :], in_=v6(w3, n), axis=AX.X, op=ALU.add), reads=[B_w3], writes=[B_s6[1]])
            rstd_from_ss(S, s6[0:n, 1, :], s6[0:n, 2, :], B_s6[1], B_s6[2], 64, GN_EPS)
            S.op('dve', lambda e: e.tensor_tensor(out=v6(w2, n), in0=v6(w2, n), in1=bc6(s6[0:n, 2, :], n), op=ALU.mult),
                 reads=[B_s6[2], B_w2], writes=[B_w2])
            S.op('pool', lambda e: e.tensor_tensor(out=w2[0:n, :], in0=w2[0:n, :], in1=vec_bc[0:n, 1, :], op=ALU.mult), reads=[B_vec], writes=[B_w2])
            S.op('pool', lambda e: e.tensor_tensor(out=w3[0:n, :], in0=r_, in1=k_, op=ALU.mult), reads=[Bi, B_w2], writes=[B_w3])
            S.op('pool', lambda e: e.tensor_tensor(out=w2[0:n, :], in0=w2[0:n, :], in1=vec_bc[0:n, 2, :], op=ALU.add), reads=[B_vec, B_w3], writes=[B_w2])
            S.op('pool', lambda e: e.tensor_tensor(out=w3[0:n, :], in0=w3[0:n, :], in1=vec_bc[0:n, 0, :], op=ALU.mult), reads=[B_vec, B_w2], writes=[B_w3])
            S.op('dve', lambda e: e.tensor_reduce(out=s6[0:n, 3, :], in_=v6(w3, n), axis=AX.X, op=ALU.add), reads=[B_w3], writes=[B_s6[3]])
            S.op('dve', lambda e: e.tensor_tensor(out=v6(w1, n), in0=v_.rearrange("p (h e) -> p h e", h=6), in1=bc6(s6[0:n, 3, :], n), op=ALU.mult),
                 reads=[B_s6[3], Bi], writes=[B_w1])
            S.op('pool', lambda e: e.tensor_tensor(out=w1[0:n, :], in0=w1[0:n, :], in1=w2[0:n, :], op=ALU.add), reads=[B_w2], writes=[B_w1])
            S.op('dve', lambda e: e.tensor_tensor(out=xn[0:n, j, 640:1024], in0=P[0:n, 0, 0:384], in1=w1[0:n, :], op=ALU.mult),
                 reads=[C.PB[0], B_w1], writes=[B_xn[j]])

        load_group(0)
        if mode == 'outproj':
            load_tile_inputs(0)
        for gi, grp in enumerate(groups):
            xb = xg[gi % 2]
            Bx = B_xg[gi % 2]
            ntok = sum(t[1] for t in grp)
            if gi + 1 < len(groups):
                load_group(gi + 1)
            for j, (t0, n, cond) in enumerate(grp):
                ti = gi * 2 + j
                if mode == 'outproj':
                    if ti + 1 < len(tiles):
                        load_tile_inputs(ti + 1)
                    readout_tile(ti, j)
                else:
                    ss = stat[0:n, j:j + 1]
                    rs = stat[0:n, 2 + j:3 + j]
                    S.op('act', lambda e, n=n, j=j, ss=ss, xb=xb: e.activation(out=junk[0:n, :], in_=xb[0:n, j, :], func=AF.Square, accum_out=ss),
                         reads=[Bx[j]], writes=[B_junk, B_stat[j]])
                    rstd_from_ss(S, ss, rs, B_stat[j], B_stat[2 + j], D, EPS)
                    S.op('act', lambda e, n=n, j=j, rs=rs, xb=xb: e.activation(out=xn[0:n, j, :], in_=xb[0:n, j, :], func=AF.Copy, scale=rs),
                         reads=[Bx[j], B_stat[2 + j]], writes=[B_xn[j]])
                for kc in range(8):
                    S.op('pe', lambda e, n=n, j=j, kc=kc: e.transpose(out=PBF[:, kc * 128:kc * 128 + n], in_=xn[0:n, j, kc * 128:(kc + 1) * 128],
                                                                      identity=C.ident_b[0:n, 0:n]),
                         reads=[B_xn[j], C.B_ident], writes=[C.PB[6]])
                pv = PBF.rearrange("p (k t) -> p k t", k=8)[:, :, 0:n]
                if mode == 'outproj':
                    S.op('act', lambda e, n=n, j=j, pv=pv: e.activation(func=AF.Copy, out=hT[:, :, j * 128:j * 128 + n], in_=pv), reads=[C.PB[6]], writes=[B_hT])
                else:
                    S.op('dve', lambda e, n=n, j=j, cond=cond, pv=pv: e.tensor_tensor(
                        out=hT[:, :, j * 128:j * 128 + n], in0=pv, in1=G1T[:, :, cond:cond + 1].to_broadcast([128, 8, n]), op=ALU.mult),
                        reads=[C.PB[6], B_modT], writes=[B_hT])
                    S.op('pool', lambda e, n=n, j=j, cond=cond: e.tensor_tensor(
                        out=hT[:, :, j * 128:j * 128 + n], in0=hT[:, :, j * 128:j * 128 + n],
                        in1=modT[0][:, :, cond:cond + 1].to_broadcast([128, 8, n]), op=ALU.add),
                        reads=[B_modT], writes=[B_hT])
            if mode == 'ffn':
                for fc in range(NF):
                    fw = 128 if fc < NF - 1 else 64
                    bg = fc % 2
                    for which in range(2):
                        col0 = which * FF + fc * 128
                        bank = bg * 2 + which
                        for k in range(8):
                            S.op('pe', lambda e, fw=fw, col0=col0, bank=bank, k=k, ntok=ntok: e.matmul(
                                P[0:fw, bank, 0:ntok], lhsT=wi_sb[:, k, col0:col0 + fw], rhs=hT[:, k, 0:ntok], start=(k == 0), stop=(k == 7)),
                                reads=[B_w, B_hT], writes=[C.PB[bank]])
                    S.op('act', lambda e, fw=fw, bg=bg, ntok=ntok: e.activation(out=sg[bg][0:fw, 0:ntok], in_=P[0:fw, bg * 2, 0:ntok], func=AF.Silu),
                         reads=[C.PB[bg * 2]], writes=[B_sg[bg]])
                    S.op('dve', lambda e, fw=fw, bg=bg, fc=fc, ntok=ntok: e.tensor_tensor(out=hid[0:fw, fc, 0:ntok], in0=P[0:fw, bg * 2 + 1, 0:ntok],
                                                                                       in1=sg[bg][0:fw, 0:ntok], op=ALU.mult),
                         reads=[C.PB[bg * 2 + 1], B_sg[bg]], writes=[B_hid])
            if mode == 'inproj':
                for j, (t0, n, cond) in enumerate(grp):
                    zb = zt[tcount[0] % 2]
                    Bz = B_zt[tcount[0] % 2]
                    tcount[0] += 1
                    for ci, c0 in enumerate(range(0, INW, 512)):
                        cw = min(512, INW - c0)
                        for k in range(8):
                            S.op('pe', lambda e, n=n, j=j, ci=ci, c0=c0, cw=cw, k=k: e.matmul(
                                P[0:n, ci, 0:cw], lhsT=hT[:, k, j * 128:j * 128 + n], rhs=w_sb[:, k, c0:c0 + cw], start=(k == 0), stop=(k == 7)),
                                reads=[B_w, B_hT], writes=[C.PB[ci]])
                        if ci % 2 == 0:
                            S.op('act', lambda e, n=n, ci=ci, c0=c0, cw=cw, zb=zb: e.activation(func=AF.Copy, out=zb[0:n, c0:c0 + cw], in_=P[0:n, ci, 0:cw]),
                                 reads=[C.PB[ci]], writes=[Bz])
                        else:
                            S.op('dve', lambda e, n=n, ci=ci, c0=c0, cw=cw, zb=zb: e.tensor_copy(out=zb[0:n, c0:c0 + cw], in_=P[0:n, ci, 0:cw]),
                                 reads=[C.PB[ci]], writes=[Bz])
                    S.dma('sp', z_d[t0:t0 + n, :], zb[0:n, :], reads=[Bz])
                continue
            for j, (t0, n, cond) in enumerate(grp):
                for hh in range(2):
                    bank = 4 + hh
                    if mode == 'ffn':
                        for fc in range(NF):
                            fw = 128 if fc < NF - 1 else 64
                            S.op('pe', lambda e, fw=fw, fc=fc, bank=bank, hh=hh, j=j, n=n: e.matmul(
                                P[0:n, bank, :], lhsT=hid[0:fw, fc, j * 128:j * 128 + n], rhs=wo_sb[0:fw, fc, hh * 512:(hh + 1) * 512],
                                start=(fc == 0), stop=(fc == NF - 1)),
                                reads=[B_hid, B_wo], writes=[C.PB[bank]])
                    else:
                        for k in range(8):
                            S.op('pe', lambda e, k=k, bank=bank, hh=hh, j=j, n=n: e.matmul(
                                P[0:n, bank, :], lhsT=hT[:, k, j * 128:j * 128 + n], rhs=w_sb[:, k, hh * 512:(hh + 1) * 512],
                                start=(k == 0), stop=(k == 7)),
                                reads=[B_hT, B_w], writes=[C.PB[bank]])
                ss = stat[0:n, 4 + j:5 + j]
                rs = stat[0:n, 6 + j:7 + j]
                yv = P[0:n, 4:6, :].rearrange("p a b -> p (a b)")
                S.op('act', lambda e, n=n, ss=ss, yv=yv: e.activation(out=junk[0:n, :], in_=yv, func=AF.Square, accum_out=ss),
                     reads=[C.PB[4], C.PB[5]], writes=[B_junk, B_stat[j]])
                rstd_from_ss(S, ss, rs, B_stat[j], B_stat[2 + j], D, EPS)
                S.op('dve', lambda e, n=n, rs=rs, yv=yv, cond=cond: e.scalar_tensor_tensor(
                    out=tmp[0:n, :], in0=yv, scalar=rs, in1=GG[0:n, cond, :], op0=ALU.mult, op1=ALU.mult),
                    reads=[C.PB[4], C.PB[5], B_stat[2 + j], B_GG], writes=[B_tmp])
                S.op('pool', lambda e, n=n, j=j, xb=xb: e.tensor_tensor(out=xb[0:n, j, :], in0=xb[0:n, j, :], in1=tmp[0:n, :], op=ALU.add),
                     reads=[B_tmp], writes=[Bx[j]])
                S.dma('sp', y[t0:t0 + n, :], xb[0:n, j, :], reads=[Bx[j]])
        S.emit()
    return nc


_cache = {}


def _get(name, builder):
    if name not in _cache:
        _cache[name] = builder()
    return _cache[name]


def _common_maps(c, c_ctx, mod_w, mod_b):
    ident = np.eye(128, dtype=np.float32)
    sel = np.zeros((2, 2, 128), np.float32)
    sel[0, 0, :] = 1.0
    sel[1, 1, :] = 1.0
    maps = []
    mwc = np.ascontiguousarray(mod_w)
    mbc = np.ascontiguousarray(np.stack([mod_b, mod_b], 0))
    for core in range(NCORE):
        b = core // 4
        cond = np.stack([c[b], c_ctx], 0)
        condT = np.ascontiguousarray(cond.reshape(2, 8, 128).transpose(2, 1, 0))
        maps.append(dict(condT=condT, mw=mwc, mb=mbc, ident=ident, sel=sel))
    return maps


def shard_tokens(xl, xc):
    xs = []
    for core in range(NCORE):
        b, q = core // 4, core % 4
        xs.append(np.ascontiguousarray(np.concatenate([xl[b, q * TL:(q + 1) * TL], xc[b, q * TC:(q + 1) * TC]], 0)))
    return xs


def unshard_tokens(ys, width=D):
    xl = np.zeros((2, 8192, width), ys[0].dtype)
    xc = np.zeros((2, 256, width), ys[0].dtype)
    for core in range(NCORE):
        b, q = core // 4, core % 4
        xl[b, q * TL:(q + 1) * TL] = ys[core][:TL]
        xc[b, q * TC:(q + 1) * TC] = ys[core][TL:]
    return xl, xc


def run_ffn(xs, c, c_ctx, mod_w, mod_b, g_pre, g_post, wi, wo):
    nc = _get('ffn', lambda: build_dense('ffn'))
    maps = _common_maps(c, c_ctx, mod_w, mod_b)
    gpreT = np.ascontiguousarray(g_pre.reshape(8, 128).T)
    gpost = np.ascontiguousarray(g_post.reshape(1, D))
    wi = np.ascontiguousarray(wi)
    wo = np.ascontiguousarray(wo)
    for core in range(NCORE):
        maps[core].update(x=xs[core], gpreT=gpreT, gpost=gpost, wi=wi, wo=wo)
    res = run_bass_kernel_spmd(nc, maps, core_ids=list(range(NCORE)))
    return [r['y'] for r in res.results]


def run_inproj(xs, c, c_ctx, mod_w, mod_b, g_pre, w_in):
    nc = _get('inproj', lambda: build_dense('inproj'))
    maps = _common_maps(c, c_ctx, mod_w, mod_b)
    gpreT = np.ascontiguousarray(g_pre.reshape(8, 128).T)
    w_in = np.ascontiguousarray(w_in)
    for core in range(NCORE):
        maps[core].update(x=xs[core], gpreT=gpreT, w=w_in)
    res = run_bass_kernel_spmd(nc, maps, core_ids=list(range(NCORE)))
    return [r['z'] for r in res.results]


def run_outproj(xs, c, c_ctx, mod_w, mod_b, g_post, w_out, fa, yf, yb, rkv, zg, g2, vecs):
    nc = _get('outproj', lambda: build_dense('outproj'))
    maps = _common_maps(c, c_ctx, mod_w, mod_b)
    gpost = np.ascontiguousarray(g_post.reshape(1, D))
    w_out = np.ascontiguousarray(w_out)
    for core in range(NCORE):
        maps[core].update(x=xs[core], gpost=gpost, w=w_out, fa=fa[core], yf=yf[core], yb=yb[core], rkv=rkv[core], zg=zg[core],
                          g2=np.ascontiguousarray(g2), vecs=np.ascontiguousarray(vecs))
    res = run_bass_kernel_spmd(nc, maps, core_ids=list(range(NCORE)))
    return [r['y'] for r in res.results]


def build_prep(ntiles_l=16, has_ctx=True):
    nc = bass.Bass("TRN2", target_bir_lowering=False)
    T = ntiles_l * 128 + (TC if has_ctx else 0)
    zt_d = nc.dram_tensor("zt", [T, INW], F32, kind="ExternalInput").ap()
    zp_d = nc.dram_tensor("zp", [T, 1152], F32, kind="ExternalInput").ap()
    zn_d = nc.dram_tensor("zn", [T, 1152], F32, kind="ExternalInput").ap()
    tab_d = nc.dram_tensor("tab", [T, 64], F32, kind="ExternalInput").ap()
    cw_d = nc.dram_tensor("cw", [1, 3 * 1152], F32, kind="ExternalInput").ap()
    vec_d = nc.dram_tensor("pvecs", [1, 6 * 384], F32, kind="ExternalInput").ap()
    l2_d = nc.dram_tensor("l2", [128, 2, 384], F32, kind="ExternalInput").ap()
    ident_d = nc.dram_tensor("ident", [128, 128], F32, kind="ExternalInput").ap()
    qk_o = nc.dram_tensor("qk", [T, 512], BF16, kind="ExternalOutput").ap()
    v_o = nc.dram_tensor("v", [T, 128], BF16, kind="ExternalOutput").ap()
    rkv_o = nc.dram_tensor("rkv", [T, 1152], F32, kind="ExternalOutput").ap()
    scan_o = nc.dram_tensor("scan", [T, 6, 384], BF16, kind="ExternalOutput").ap()
    dec_o = nc.dram_tensor("dec", [T, 2, 384], F32, kind="ExternalOutput").ap()
    with ExitStack() as st:
        S = Sched(nc, st)
        C = setup_common(S, nc)
        P = C.P
        load_ident(S, C, ident_d)
        cw = S.sb("cw_sb", [128, 3, 1152], F32)
        vec = S.sb("vec_sb", [128, 6, 384], F32)
        l2f = S.sb("l2f", [128, 2, 384], F32)
        l2 = S.sb("l2b", [128, 2, 384], BF16)
        B_c = S.buf('consts')
        S.dma('sp', cw[:].rearrange("p a b -> p (a b)"), mkap(cw_d, 0, [(0, 128), (1, 3 * 1152)]), writes=[B_c])
        S.dma('sp', vec[:].rearrange("p a b -> p (a b)"), mkap(vec_d, 0, [(0, 128), (1, 6 * 384)]), writes=[B_c])
        S.dma('sp', l2f[:], l2_d, writes=[B_c])
        S.op('dve', lambda e: e.tensor_copy(out=l2[:], in_=l2f[:]), reads=[B_c], writes=[B_c])
        tiles = [(i * 128, 128) for i in range(ntiles_l)]
        if has_ctx:
            tiles.append((ntiles_l * 128, TC))
        sets = []
        for i in range(2):
            d = dict(zt=S.sb(f"zt{i}", [128, INW], F32), zp=S.sb(f"zp{i}", [128, 1152], F32), zn=S.sb(f"zn{i}", [128, 1152], F32),
                     tab=S.sb(f"tab{i}", [128, 64], F32), B=S.buf(f'in{i}'),
                     qk=S.sb(f"qk{i}", [128, 512], BF16), v=S.sb(f"v{i}", [128, 128], BF16), rkv=S.sb(f"rkv{i}", [128, 1152], F32),
                     scan=S.sb(f"scan{i}", [128, 6, 384], BF16), dec=S.sb(f"dec{i}", [128, 2, 384], F32),
                     Bqk=S.buf(f'qk{i}'), Bv=S.buf(f'v{i}'), Brkv=S.buf(f'rkv{i}'), Bscan=S.buf(f'scan{i}'), Bdec=S.buf(f'dec{i}'))
            sets.append(d)
        t1 = S.sb("t1", [128, 8, 32], F32)
        t2 = S.sb("t2", [128, 8, 32], F32)
        t3 = S.sb("t3", [128, 8, 32], F32)
        t4 = S.sb("t4", [128, 8, 32], F32)
        Bt = [S.buf(f't{i}') for i in range(4)]
        ca = S.sb("ca", [128, 1152], F32)
        cb = S.sb("cb", [128, 1152], F32)
        Bca, Bcb = S.buf('ca'), S.buf('cb')
        lb = S.sb("lb", [128, 2, 128], BF16)
        lT = S.sb("lT", [128, 2, 128], BF16)
        Blb, BlT = S.buf('lb'), S.buf('lT')
        PBF = P[:, 6, :].bitcast(BF16)
        asig = S.sb("asig", [128, 2, 384], F32)
        Basig = S.buf('asig')
        wt = S.sb("wt", [128, 2, 384], F32)
        Bwt = S.buf('wt')
        kk0 = S.sb("kk0", [128, 384], F32)
        ksq = S.sb("ksq", [128, 384], F32)
        kkf = S.sb("kkf", [128, 384], F32)
        Bkk0, Bksq, Bkkf = S.buf('kk0'), S.buf('ksq'), S.buf('kkf')
        s6 = S.sb("s6", [128, 2, 6], F32)
        Bs6 = [S.buf('s60'), S.buf('s61')]
        tk = S.sb("tk", [128, 2, 384], F32)
        Btk = S.buf('tk')

        def load(ti):
            t0, n = tiles[ti]
            d = sets[ti % 2]
            S.dma('sp', d['zt'][0:n, :], zt_d[t0:t0 + n, :], writes=[d['B']])
            S.dma('sp', d['zp'][0:n, :], zp_d[t0:t0 + n, :], writes=[d['B']])
            S.dma('sp', d['zn'][0:n, :], zn_d[t0:t0 + n, :], writes=[d['B']])
            S.dma('sp', d['tab'][0:n, :], tab_d[t0:t0 + n, :], writes=[d['B']])

        load(0)
        for ti, (t0, n) in enumerate(tiles):
            if ti + 1 < len(tiles):
                load(ti + 1)
            d = sets[ti % 2]
            Bi = d['B']
            zt = d['zt']
            qk = zt[0:n, 256:768].rearrange("p (h two e) -> p h two e", h=8, two=2)
            x1 = qk[:, :, 0, :]
            x2 = qk[:, :, 1, :]
            cosb = d['tab'][0:n, 0:32].unsqueeze(1).to_broadcast([n, 8, 32])
            sinb = d['tab'][0:n, 32:64].unsqueeze(1).to_broadcast([n, 8, 32])
            oq = d['qk'][0:n, :].rearrange("p (h two e) -> p h two e", h=8, two=2)
            S.op('dve', lambda e, x1=x1, cosb=cosb, n=n: e.tensor_tensor(out=t1[0:n], in0=x1, in1=cosb, op=ALU.mult), reads=[Bi], writes=[Bt[0]])
            S.op('pool', lambda e, x2=x2, sinb=sinb, n=n: e.tensor_tensor(out=t2[0:n], in0=x2, in1=sinb, op=ALU.mult), reads=[Bi], writes=[Bt[1]])
            S.op('dve', lambda e, x2=x2, cosb=cosb, n=n: e.tensor_tensor(out=t3[0:n], in0=x2, in1=cosb, op=ALU.mult), reads=[Bi], writes=[Bt[2]])
            S.op('pool', lambda e, x1=x1, sinb=sinb, n=n: e.tensor_tensor(out=t4[0:n], in0=x1, in1=sinb, op=ALU.mult), reads=[Bi], writes=[Bt[3]])
            S.op('dve', lambda e, oq=oq, n=n: e.tensor_tensor(out=oq[:, :, 0, :], in0=t1[0:n], in1=t2[0:n], op=ALU.subtract),
                 reads=[Bt[0], Bt[1]], writes=[d['Bqk']])
            S.op('pool', lambda e, oq=oq, n=n: e.tensor_tensor(out=oq[:, :, 1, :], in0=t3[0:n], in1=t4[0:n], op=ALU.add),
                 reads=[Bt[2], Bt[3]], writes=[d['Bqk']])
            S.dma('sp', qk_o[t0:t0 + n, :], d['qk'][0:n, :], reads=[d['Bqk']])
            S.op('act', lambda e, zt=zt, d=d, n=n: e.activation(func=AF.Copy, out=d['v'][0:n, :], in_=zt[0:n, 768:896]), reads=[Bi], writes=[d['Bv']])
            S.dma('sp', v_o[t0:t0 + n, :], d['v'][0:n, :], reads=[d['Bv']])
            rkv = d['rkv']
            S.op('pool', lambda e, d=d, n=n: e.tensor_tensor(out=ca[0:n, :], in0=d['zp'][0:n, :], in1=cw[0:n, 0, :], op=ALU.mult), reads=[Bi, B_c], writes=[Bca])
            S.op('dve', lambda e, zt=zt, n=n: e.tensor_tensor(out=cb[0:n, :], in0=zt[0:n, 896:2048], in1=cw[0:n, 1, :], op=ALU.mult), reads=[Bi, B_c], writes=[Bcb])
            S.op('pool', lambda e, d=d, n=n, rkv=rkv: e.tensor_tensor(out=rkv[0:n, :], in0=d['zn'][0:n, :], in1=cw[0:n, 2, :], op=ALU.mult), reads=[Bi, B_c], writes=[d['Brkv']])
            S.op('dve', lambda e, n=n: e.tensor_tensor(out=ca[0:n, :], in0=ca[0:n, :], in1=cb[0:n, :], op=ALU.add), reads=[Bcb], writes=[Bca])
            S.op('pool', lambda e, n=n, rkv=rkv: e.tensor_tensor(out=rkv[0:n, :], in0=rkv[0:n, :], in1=ca[0:n, :], op=ALU.add), reads=[Bca], writes=[d['Brkv']])
            S.dma('sp', rkv_o[t0:t0 + n, :], rkv[0:n, :], reads=[d['Brkv']])
            r_ = rkv[0:n, 0:384]
            k_ = rkv[0:n, 384:768]
            S.op('act', lambda e, zt=zt, n=n: e.activation(out=lb[0:n, 0, :], in_=zt[0:n, 2048:2176], func=AF.Tanh), reads=[Bi], writes=[Blb])
            S.op('act', lambda e, zt=zt, n=n: e.activation(func=AF.Copy, out=lb[0:n, 1, :], in_=zt[0:n, 2176:2304]), reads=[Bi], writes=[Blb])
            for q in range(2):
                S.op('pe', lambda e, q=q, n=n: e.transpose(out=PBF[:, q * 128:q * 128 + n], in_=lb[0:n, q, :], identity=C.ident_b[0:n, 0:n]),
                     reads=[Blb, C.B_ident], writes=[C.PB[6]])
            S.op('act', lambda e, n=n: e.activation(func=AF.Copy, out=lT[:, :, 0:n], in_=PBF[:, 0:256].rearrange("p (q t) -> p q t", q=2)[:, :, 0:n]),
                 reads=[C.PB[6]], writes=[BlT])
            for q in range(2):
                for z in range(2):
                    bank = q * 2 + z
                    S.op('pe', lambda e, q=q, z=z, bank=bank, n=n: e.matmul(P[0:n, bank, 0:384], lhsT=lT[z * 64:(z + 1) * 64, q, 0:n],
                                                                          rhs=l2[z * 64:(z + 1) * 64, q, :], start=True, stop=True),
                         reads=[BlT, B_c], writes=[C.PB[bank]])
            for z in range(2):
                S.op('dve', lambda e, z=z, n=n: e.tensor_tensor(out=wt[0:n, z, :], in0=P[0:n, z, 0:384], in1=vec[0:n, z, :], op=ALU.add),
                     reads=[C.PB[z], B_c], writes=[Bwt])
                S.op('dve', lambda e, z=z, n=n: e.tensor_tensor(out=asig[0:n, z, :], in0=P[0:n, 2 + z, 0:384], in1=vec[0:n, 2 + z, :], op=ALU.add),
                     reads=[C.PB[2 + z], B_c], writes=[Basig])
            S.op('act', lambda e, n=n: e.activation(out=wt[0:n], in_=wt[0:n], func=AF.Sigmoid), reads=[Bwt], writes=[Bwt])
            S.op('act', lambda e, n=n: e.activation(out=asig[0:n], in_=asig[0:n], func=AF.Sigmoid), reads=[Basig], writes=[Basig])
            S.op('act', lambda e, n=n, d=d: e.activation(out=d['dec'][0:n], in_=wt[0:n], func=AF.Copy, scale=-0.6065306597126334),
                 reads=[Bwt], writes=[d['Bdec']])
            S.dma('sp', dec_o[t0:t0 + n], d['dec'][0:n], reads=[d['Bdec']])
            S.op('pool', lambda e, n=n, k_=k_: e.tensor_tensor(out=kk0[0:n, :], in0=k_, in1=vec[0:n, 4, :], op=ALU.mult), reads=[d['Brkv'], B_c], writes=[Bkk0])
            S.op('pool', lambda e, n=n: e.tensor_tensor(out=ksq[0:n, :], in0=kk0[0:n, :], in1=kk0[0:n, :], op=ALU.mult), reads=[Bkk0], writes=[Bksq])
            S.op('dve', lambda e, n=n: e.tensor_reduce(out=s6[0:n, 0, :], in_=ksq[0:n, :].rearrange("p (h e) -> p h e", h=6), axis=AX.X, op=ALU.add),
                 reads=[Bksq], writes=[Bs6[0]])
            S.op('dve', lambda e, n=n: e.tensor_scalar(out=s6[0:n, 1, :], in0=s6[0:n, 0, :], scalar1=1e-24, scalar2=None, op0=ALU.max),
                 reads=[Bs6[0]], writes=[Bs6[1]])
            S.op('act', lambda e, n=n: e.activation(out=s6[0:n, 1, :], in_=s6[0:n, 1, :], func=AF.Sqrt), reads=[Bs6[1]], writes=[Bs6[1]])
            S.op('dve', lambda e, n=n: e.reciprocal(out=s6[0:n, 1, :], in_=s6[0:n, 1, :]), reads=[Bs6[1]], writes=[Bs6[1]])
            S.op('dve', lambda e, n=n: e.tensor_tensor(out=kkf[0:n, :].rearrange("p (h e) -> p h e", h=6), in0=kk0[0:n, :].rearrange("p (h e) -> p h e", h=6),
                                                      in1=s6[0:n, 1, :].unsqueeze(2).to_broadcast([n, 6, 64]), op=ALU.mult),
                 reads=[Bkk0, Bs6[1]], writes=[Bkkf])
            sc = d['scan']
            Bsc = d['Bscan']
            S.op('act', lambda e, n=n, sc=sc: e.mul(out=sc[0:n, 0, :], in_=kkf[0:n, :], mul=-1.0), reads=[Bkkf], writes=[Bsc])
            S.op('act', lambda e, n=n, sc=sc, r_=r_: e.activation(func=AF.Copy, out=sc[0:n, 1, :], in_=r_), reads=[d['Brkv']], writes=[Bsc])
            for z in range(2):
                S.op('pool', lambda e, n=n, sc=sc, z=z: e.tensor_tensor(out=sc[0:n, 2 + z, :], in0=kkf[0:n, :], in1=asig[0:n, z, :], op=ALU.mult),
                     reads=[Bkkf, Basig], writes=[Bsc])
                S.op('dve', lambda e, n=n, z=z: e.scalar_tensor_tensor(out=tk[0:n, z, :], in0=asig[0:n, z, :], scalar=-1.0, in1=vec[0:n, 5, :],
                                                                      op0=ALU.add, op1=ALU.mult), reads=[Basig, B_c], writes=[Btk])
            for z in range(2):
                S.op('dve', lambda e, n=n, z=z, sc=sc, k_=k_: e.scalar_tensor_tensor(out=sc[0:n, 4 + z, :], in0=tk[0:n, z, :], scalar=1.0, in1=k_,
                                                                                    op0=ALU.add, op1=ALU.mult), reads=[Btk, d['Brkv']], writes=[Bsc])
            S.dma('sp', scan_o[t0:t0 + n], sc[0:n], reads=[Bsc])
        S.emit()
    return nc


def rope_tables():
    rows = 8192 // 64
    t = np.arange(8192)
    row = (t // 64).astype(np.float32)
    col = (t % 64).astype(np.float32)
    inv = (10000.0 ** (-np.arange(16, dtype=np.float32) / 16)).astype(np.float32)
    ang = np.concatenate([row[:, None] * inv, col[:, None] * inv], -1).astype(np.float32)
    return np.cos(ang).astype(np.float32), np.sin(ang).astype(np.float32)


def run_prep(zl, zc, conv_w, w0, w2, a0, a2, k_k, k_a):
    nc = _get('prep', build_prep)
    cos, sin = rope_tables()
    ident = np.eye(128, dtype=np.float32)
    cwf = np.ascontiguousarray(conv_w.reshape(1, 3 * 1152))
    pvecs = np.ascontiguousarray(np.concatenate([w0[0], w0[1], a0[0], a0[1], k_k, k_a]).reshape(1, 6 * 384))
    l2 = np.ascontiguousarray(np.stack([w2.reshape(128, 384), a2.reshape(128, 384)], 1))

    def shift(zz, d):
        o = np.zeros_like(zz)
        if d == 1:
            o[1:] = zz[:-1]
        else:
            o[:-1] = zz[1:]
        return o
    maps = []
    for core in range(NCORE):
        b, q = core // 4, core % 4
        rl = zl[b][:, 896:2048]
        rc = zc[b][:, 896:2048]
        zt = np.concatenate([zl[b, q * TL:(q + 1) * TL], zc[b, q * TC:(q + 1) * TC]], 0)
        zp = np.concatenate([shift(rl, 1)[q * TL:(q + 1) * TL], shift(rc, 1)[q * TC:(q + 1) * TC]], 0)
        zn = np.concatenate([shift(rl, -1)[q * TL:(q + 1) * TL], shift(rc, -1)[q * TC:(q + 1) * TC]], 0)
        tab = np.zeros((TL + TC, 64), np.float32)
        tab[:TL, 0:32] = cos[q * TL:(q + 1) * TL]
        tab[:TL, 32:64] = sin[q * TL:(q + 1) * TL]
        tab[TL:, 0:32] = 1.0
        maps.append(dict(zt=np.ascontiguousarray(zt), zp=np.ascontiguousarray(zp), zn=np.ascontiguousarray(zn), tab=tab,
                         cw=cwf, pvecs=pvecs, l2=l2, ident=ident))
    res = run_bass_kernel_spmd(nc, maps, core_ids=list(range(NCORE)))
    out = {}
    for key, wdt in [('qk', 512), ('v', 128), ('rkv', 1152)]:
        out[key] = unshard_tokens([r[key] for r in res.results], wdt)
    out['scan'] = unshard_tokens([r['scan'].reshape(TL + TC, 6 * 384) for r in res.results], 6 * 384)
    out['dec'] = unshard_tokens([r['dec'].reshape(TL + TC, 2 * 384) for r in res.results], 2 * 384)
    return out


CH = 64
NSTEP = 8448


def build_scan2(nstep=NSTEP):
    nc = bass.Bass("TRN2", target_bir_lowering=False)
    nch = nstep // CH
    tm_d = nc.dram_tensor("tm", [3, nch, CH, 4, 64], F32, kind="ExternalInput").ap()
    fm_d = nc.dram_tensor("fm", [3, nch, 64, 4, CH], F32, kind="ExternalInput").ap()
    lw_d = nc.dram_tensor("lw", [3, nch, 64, 2, 64], F32, kind="ExternalInput").ap()
    cst_d = nc.dram_tensor("cst", [64, 8, 64], F32, kind="ExternalInput").ap()
    y_o = nc.dram_tensor("y", [3, nch, 64, CH], F32, kind="ExternalOutput").ap()
    NSET = 9
    with ExitStack() as st:
        S = Sched(nc, st)
        P = st.enter_context(nc.psum_tensor("P", [128, 8, 512], F32))
        PB = [S.buf(f'pb{i}') for i in range(8)]
        cst = S.sb("cst_sb", [64, 8, 64], F32)
        B_c = S.buf('cst')
        S.dma('sp', cst[:], cst_d, writes=[B_c])
        tri = cst[:, 0, :]
        maskM = cst[:, 1:6, :].rearrange("p a b -> p (a b)")
        ident = cst[:, 6, :]
        sets = []
        for i in range(NSET):
            d = dict(
                tm=S.sb(f"tm{i}", [64, 4, 64], F32), fm=S.sb(f"fm{i}", [64, 4, 64], F32), lw=S.sb(f"lw{i}", [64, 2, 64], F32),
                Ep=S.sb(f"Ep{i}", [64, 128], F32), En=S.sb(f"En{i}", [64, 128], F32), Ev=S.sb(f"Ev{i}", [64, 128], F32),
                AR=S.sb(f"AR{i}", [64, 128], F32), BKf=S.sb(f"BKf{i}", [64, 2, 64], F32), BKt=S.sb(f"BKt{i}", [64, 2, 64], F32),
                M=S.sb(f"M{i}", [64, 320], F32), X=[S.sb(f"X{i}_{q}", [64, 128], F32) for q in range(2)],
                NN=[S.sb(f"NN{i}_{q}", [64, 128], F32) for q in range(2)],
                Rhat=S.sb(f"Rhat{i}", [64, 64], F32), Y0=S.sb(f"Y0{i}", [64, 64], F32), G0=S.sb(f"G0{i}", [64, 64], F32),
                HT=S.sb(f"HT{i}", [64, 64], F32), ysb=S.sb(f"ysb{i}", [64, 64], F32),
            )
            for k in ['in', 'Ep', 'En', 'Ev', 'AR', 'BKf', 'BKt', 'M', 'X0', 'X1', 'NN0', 'NN1', 'Rhat', 'Y0', 'G0', 'HT', 'ysb']:
                d['B' + k] = S.buf(f'{k}{i}')
            sets.append(d)
        ST = [[S.sb(f"ST{it}_{q}", [64, 64], F32) for q in range(2)] for it in range(3)]
        B_ST = [[S.buf(f'ST{it}_{q}') for q in range(2)] for it in range(3)]
        for it in range(3):
            S.op('pool', lambda e, it=it: e.memset(ST[it][0][:], 0.0), writes=[B_ST[it][0]])

        def load(c, it, d):
            S.dma('sp', d['tm'][:], tm_d[it, c], writes=[d['Bin']])
            S.dma('sp', d['fm'][:], fm_d[it, c], writes=[d['Bin']])
            S.dma('sp', d['lw'][:], lw_d[it, c], writes=[d['Bin']])

        def mm(out, lhsT, rhs, reads, writes, start=True, stop=True):
            S.op('pe', lambda e: e.matmul(out, lhsT=lhsT, rhs=rhs, start=start, stop=stop), reads=reads, writes=writes)

        def act(out, in_, func, reads, writes, scale=None):
            if scale is None:
                S.op('act', lambda e: e.activation(out=out, in_=in_, func=func), reads=reads, writes=writes)
            else:
                S.op('act', lambda e: e.activation(out=out, in_=in_, func=func, scale=scale), reads=reads, writes=writes)

        def tt(eng, out, in0, in1, op, reads, writes):
            S.op(eng, lambda e: e.tensor_tensor(out=out, in0=in0, in1=in1, op=op), reads=reads, writes=writes)

        def mm(out, lhsT, rhs, reads, writes, start=True, stop=True):
            S.op('pe', lambda e: e.matmul(out, lhsT=lhsT, rhs=rhs, start=start, stop=stop), reads=reads, writes=writes)

        def act(out, in_, func, reads, writes, scale=None):
            if scale is None:
                S.op('act', lambda e: e.activation(out=out, in_=in_, func=func), reads=reads, writes=writes)
            else:
                S.op('act', lambda e: e.activation(out=out, in_=in_, func=func, scale=scale), reads=reads, writes=writes)

        def tt(eng, out, in0, in1, op, reads, writes):
            S.op(eng, lambda e: e.tensor_tensor(out=out, in0=in0, in1=in1, op=op), reads=reads, writes=writes)

        def item_prog(n, c, it, d):
            bk0 = n % 3
            bk1 = bk0
            bk2 = 3 + (n % 3)
            bk3 = 6 + (n % 2)
            Bin = d['Bin']
            tm, fm, lw = d['tm'], d['fm'], d['lw']
            Lps = P[0:64, bk0, 0:128]
            BE = d['BEp']
            mm(P[0:64, bk0, 0:64], tri, lw[:, 0, :], [Bin, B_c], [PB[bk0]])
            mm(P[0:64, bk0, 64:128], lw[:, 0, :], tri, [Bin, B_c], [PB[bk0]])
            yield
            act(d['Ep'][:], Lps, AF.Exp, [PB[bk0]], [BE])
            act(d['En'][:], Lps, AF.Exp, [PB[bk0]], [BE], scale=-1.0)
            tt('dve', d['Ev'][:], Lps, lw[:].rearrange("p a b -> p (a b)"), ALU.subtract, [PB[bk0], Bin], [BE])
            yield
            act(d['Ev'][:], d['Ev'][:], AF.Exp, [BE], [BE])
            yield
            X0 = d['X'][0]
            tt('pool', X0[:, 0:64], tm[:, 0, :], d['Ev'][:, 0:64], ALU.mult, [Bin, BE], [d['BX0']])
            tt('dve', d['BKt'][:], tm[:, 1:3, :], d['En'][:, 0:64].unsqueeze(1).to_broadcast([64, 2, 64]), ALU.mult, [Bin, BE], [d['BBKt']])
            tt('pool', d['AR'][:, 0:64], fm[:, 0, :], d['Ev'][:, 64:128], ALU.mult, [Bin, BE], [d['BAR']])
            tt('dve', d['AR'][:, 64:128], fm[:, 1, :], d['Ep'][:, 64:128], ALU.mult, [Bin, BE], [d['BAR']])
            tt('pool', d['BKf'][:], fm[:, 2:4, :], d['En'][:, 64:128].unsqueeze(1).to_broadcast([64, 2, 64]), ALU.mult, [Bin, BE], [d['BBKf']])
            yield
            mm(P[0:64, bk1, 192:320], d['BKf'][:, 0, :], d['AR'][:], [d['BBKf'], d['BAR']], [PB[bk1]])
            mm(P[0:64, bk1, 320:448], d['BKf'][:, 1, :], d['AR'][:], [d['BBKf'], d['BAR']], [PB[bk1]])
            mm(P[0:64, bk1, 448:512], d['AR'][:, 0:64], d['BKf'][:, 0, :], [d['BBKf'], d['BAR']], [PB[bk1]])
            yield
            tt('dve', d['M'][:], P[0:64, bk1, 192:512], maskM, ALU.mult, [PB[bk1], B_c], [d['BM']])
            yield
            M = d['M']
            Mbr, Mka, Mkr = M[:, 64:128], M[:, 128:192], M[:, 192:256]
            mm(P[0:64, bk0, 128:192], Mka, tm[:, 3, :], [d['BM'], Bin], [PB[bk0]])
            yield
            act(X0[:, 64:128], P[0:64, bk0, 128:192], AF.Copy, [PB[bk0]], [d['BX0']])
            yield
            Nk = M[:, 0:64]
            NkT = M[:, 256:320]
            BNk = d['BM']
            xi = 0
            for k in range(6):
                Xc, Xn = d['X'][xi], d['X'][1 - xi]
                BXc, BXn = d['BX%d' % xi], d['BX%d' % (1 - xi)]
                mm(P[0:64, bk2, 0:128], Nk, Xc[:], [BNk, BXc], [PB[bk2]])
                yield
                tt('dve', Xn[:], P[0:64, bk2, 0:128], Xc[:], ALU.add, [PB[bk2], BXc], [BXn])
                yield
                xi = 1 - xi
                if k < 5:
                    NNn = d['NN'][k % 2]
                    BNn = d['BNN%d' % (k % 2)]
                    mm(P[0:64, bk2, 128:192], NkT, Nk, [BNk], [PB[bk2]])
                    mm(P[0:64, bk2, 192:256], Nk, NkT, [BNk], [PB[bk2]])
                    yield
                    act(NNn[:], P[0:64, bk2, 128:256], AF.Copy, [PB[bk2]], [BNn])
                    yield
                    Nk, NkT, BNk = NNn[:, 0:64], NNn[:, 64:128], BNn
            Xf = d['X'][xi]
            BXf = d['BX%d' % xi]
            At, Wt = Xf[:, 0:64], Xf[:, 64:128]
            Bt, Kt = d['BKt'][:, 0, :], d['BKt'][:, 1, :]
            Vt = tm[:, 3, :]
            mm(P[0:64, bk3, 0:64], At, Mbr, [BXf, d['BM']], [PB[bk3]])
            yield
            tt('dve', d['Rhat'][:], P[0:64, bk3, 0:64], d['AR'][:, 64:128], ALU.add, [PB[bk3], d['BAR']], [d['BRhat']])
            yield
            mm(P[0:64, bk3, 64:128], At, Bt, [BXf, d['BBKt']], [PB[bk3]])
            yield
            tt('dve', d['G0'][:], P[0:64, bk3, 64:128], ident, ALU.add, [PB[bk3], B_c], [d['BG0']])
            yield
            mm(P[0:64, bk3, 128:192], Wt, Mbr, [BXf, d['BM']], [PB[bk3]], start=True, stop=False)
            mm(P[0:64, bk3, 128:192], Vt, Mkr, [Bin, d['BM']], [PB[bk3]], start=False, stop=True)
            yield
            act(d['Y0'][:], P[0:64, bk3, 128:192], AF.Copy, [PB[bk3]], [d['BY0']])
            yield
            mm(P[0:64, bk3, 192:256], Bt, Wt, [BXf, d['BBKt']], [PB[bk3]], start=True, stop=False)
            mm(P[0:64, bk3, 192:256], Kt, Vt, [Bin, d['BBKt']], [PB[bk3]], start=False, stop=True)
            yield
            PC = d['Ep'][:, 127:128]
            act(d['HT'][:], P[0:64, bk3, 192:256], AF.Copy, [PB[bk3], BE], [d['BHT']], scale=PC)
            yield
            Sc, Sn = ST[it][c % 2], ST[it][(c + 1) % 2]
            BSc, BSn = B_ST[it][c % 2], B_ST[it][(c + 1) % 2]
            mm(P[0:64, bk3, 256:320], Sc[:], d['Rhat'][:], [BSc, d['BRhat']], [PB[bk3]])
            mm(P[0:64, bk3, 320:384], d['G0'][:], Sc[:], [BSc, d['BG0']], [PB[bk3]])
            yield
            tt('dve', d['ysb'][:], P[0:64, bk3, 256:320], d['Y0'][:], ALU.add, [PB[bk3], d['BY0']], [d['Bysb']])
            (lambda out, in0, scalar, in1, reads, writes: S.op('dve', lambda e: e.scalar_tensor_tensor(
                out=out, in0=in0, scalar=scalar, in1=in1, op0=ALU.mult, op1=ALU.add), reads=reads, writes=writes))(
                Sn[:], P[0:64, bk3, 320:384], PC, d['HT'][:], [PB[bk3], BE, d['BHT']], [BSn])
            S.dma('sp', y_o[it, c], d['ysb'][:], reads=[d['Bysb']])

            yield
        order = [(c, it) for c in range(nch) for it in range(3)]
        import os
        W = int(os.environ.get('SCAN2_W', '6'))
        PF = NSET - W
        nload = [0]

        def ensure_loaded(upto):
            while nload[0] < min(upto, len(order)):
                m = nload[0]
                load(order[m][0], order[m][1], sets[m % NSET])
                nload[0] += 1
        ensure_loaded(W + PF)
        active = []
        nxt = 0
        STAG = int(os.environ.get('SCAN2_STAG', '7'))
        rnd = 0
        last_admit = -10 ** 9
        while nxt < len(order) or active:
            rnd += 1
            if len(active) < W and nxt < len(order) and (rnd - last_admit >= STAG or not active):
                active.append(item_prog(nxt, order[nxt][0], order[nxt][1], sets[nxt % NSET]))
                nxt += 1
                last_admit = rnd
            still = []
            for g in active:
                try:
                    next(g)
                    still.append(g)
                except StopIteration:
                    ensure_loaded(nxt + PF + 1)
            active = still
        S.emit()
    return nc


def scan2_host_inputs(prep, nstep=NSTEP):
    scan_l, scan_c = prep['scan']
    lw_l, lw_c = prep['dec']
    rkv_l, rkv_c = prep['rkv']
    nch = nstep // CH
    t = np.arange(64)
    tri = (t[:, None] <= t[None, :]).astype(np.float32)
    su = (t[:, None] < t[None, :]).astype(np.float32)
    sl = (t[:, None] > t[None, :]).astype(np.float32)
    cst = np.ascontiguousarray(np.stack([tri, su, tri, su, tri, sl, np.eye(64, dtype=np.float32), tri.T], 1))

    def seq(lat, ctx, b, z):
        s = np.concatenate([ctx[b], lat[b]], 0) if z == 0 else np.concatenate([ctx[b][::-1], lat[b][::-1]], 0)
        return s[:nstep]
    maps = []
    for core in range(NCORE):
        tm = np.zeros((3, nch, CH, 4, 64), np.float32)
        fm = np.zeros((3, nch, 64, 4, CH), np.float32)
        lw = np.zeros((3, nch, 64, 2, 64), np.float32)
        for it in range(3):
            item = core * 3 + it
            z, b, h = item // 12, (item // 6) % 2, item % 6
            hs = slice(h * 64, (h + 1) * 64)
            sc = seq(scan_l, scan_c, b, z).reshape(nstep, 6, 384).astype(np.float32)
            a_, r_, b_, k_ = sc[:, 0, hs], sc[:, 1, hs], sc[:, 2 + z, hs], sc[:, 4 + z, hs]
            v_ = seq(rkv_l, rkv_c, b, z)[:, 768 + h * 64:768 + (h + 1) * 64]
            l_ = seq(lw_l, lw_c, b, z).reshape(nstep, 2, 384)[:, z, hs]
            tm[it] = np.stack([a_, b_, k_, v_], 1).reshape(nch, CH, 4, 64)
            fm[it] = np.stack([a_, r_, b_, k_], 1).reshape(nch, CH, 4, 64).transpose(0, 3, 2, 1)
            lc = l_.reshape(nch, CH, 64)
            lw[it] = np.stack([lc, lc.transpose(0, 2, 1)], 2)
        maps.append(dict(tm=tm, fm=fm, lw=lw, cst=cst))
    return maps


def scan2_host_outputs(ys, nstep=NSTEP):
    yl = np.zeros((2, 2, 8192, 384), np.float32)
    yc = np.zeros((2, 2, 256, 384), np.float32)
    nch = nstep // CH
    for core in range(NCORE):
        for it in range(3):
            item = core * 3 + it
            z, b, h = item // 12, (item // 6) % 2, item % 6
            y = np.zeros((NSTEP, 64), np.float32)
            y[:nstep] = ys[core][it].transpose(0, 2, 1).reshape(nstep, 64)
            c_, l_ = y[:256], y[256:]
            if z == 1:
                c_, l_ = c_[::-1], l_[::-1]
            yc[z, b, :, h * 64:(h + 1) * 64] = c_
            yl[z, b, :, h * 64:(h + 1) * 64] = l_
    return yl, yc


def run_scan2(prep, nstep=NSTEP):
    nc = _get(f'scan2_{nstep}', lambda: build_scan2(nstep))
    maps = scan2_host_inputs(prep, nstep)
    res = run_bass_kernel_spmd(nc, maps, core_ids=list(range(NCORE)))
    return scan2_host_outputs([r['y'] for r in res.results], nstep)


def build_attn(nblk=16, with_ctx=True):
    nc = bass.Bass("TRN2", target_bir_lowering=False)
    NQ = nblk * 128
    QT_d = nc.dram_tensor("QT", [64, 6, NQ], BF16, kind="ExternalInput").ap()
    KT_d = nc.dram_tensor("KT", [64, 2, NQ + 256], BF16, kind="ExternalInput").ap()
    KcT_d = nc.dram_tensor("KcT", [64, 2, 256], BF16, kind="ExternalInput").ap()
    V_d = nc.dram_tensor("V", [128, nblk + 2, 2, 65], BF16, kind="ExternalInput").ap()
    Vc_d = nc.dram_tensor("Vc", [128, 2, 2, 65], BF16, kind="ExternalInput").ap()
    mask_d = nc.dram_tensor("mask", [128, 2, 128], BF16, kind="ExternalInput").ap()
    sink_d = nc.dram_tensor("sink", [1, 6], F32, kind="ExternalInput").ap()
    QcT_d = nc.dram_tensor("QcT", [64, 6, 64], BF16, kind="ExternalInput").ap()
    o_d = nc.dram_tensor("o", [NQ, 384], F32, kind="ExternalOutput").ap()
    oc_d = nc.dram_tensor("oc", [64, 384], F32, kind="ExternalOutput").ap()
    with ExitStack() as st:
        S = Sched(nc, st)
        P = st.enter_context(nc.psum_tensor("P", [128, 8, 512], F32))
        PB = [S.buf(f'pb{i}') for i in range(8)]
        QT = S.sb("QT_sb", [64, 6, NQ], BF16)
        KT = S.sb("KT_sb", [64, 2, NQ + 256], BF16)
        KcT = S.sb("KcT_sb", [64, 2, 256], BF16)
        V = S.sb("V_sb", [128, nblk + 2, 2, 65], BF16)
        Vc = S.sb("Vc_sb", [128, 2, 2, 65], BF16)
        mask = S.sb("mask_sb", [128, 2, 128], BF16)
        esink = S.sb("esink", [128, 6], F32)
        QcT = S.sb("QcT_sb", [64, 6, 64], BF16)
        B_in = S.buf('in')
        B_es = S.buf('es')
        for t, d in [(QT, QT_d), (KT, KT_d), (KcT, KcT_d), (V, V_d), (Vc, Vc_d), (mask, mask_d), (QcT, QcT_d)]:
            S.dma('sp', t[:], d, writes=[S.buf()])
        S.dma('sp', esink[:], mkap(sink_d, 0, [(0, 128), (1, 6)]), writes=[B_es])
        S.op('act', lambda e: e.activation(out=esink[:], in_=esink[:], func=AF.Exp), reads=[B_es], writes=[B_es])
        PT = [S.sb(f"PT{i}", [128, 384], BF16) for i in range(5)]
        B_PT = [S.buf(f'PT{i}') for i in range(5)]
        ot = [S.sb(f"ot{i}", [128, 384], F32) for i in range(2)]
        B_ot = [S.buf(f'ot{i}') for i in range(2)]
        den = S.sb("den", [128, 2, 3], F32)
        B_den = [S.buf('den0'), S.buf('den1')]
        it = [0]

        def attn_block(q_ap, nq, chunks, kh, otile, Bot):
            i = it[0]
            it[0] += 1
            nch = len(chunks)
            for c, (kT_ap, v_ap, mi) in enumerate(chunks):
                S.op('pe', lambda e, c=c, kT_ap=kT_ap: e.matmul(P[:, c, 0:3 * nq].rearrange("p (g q) -> p g q", g=3), lhsT=kT_ap, rhs=q_ap,
                                                                  start=True, stop=True),
                     reads=[B_in], writes=[PB[c]])
                S.op('act', lambda e, c=c: e.activation(out=PT[c][:, 0:3 * nq], in_=P[:, c, 0:3 * nq], func=AF.Exp, scale=0.125),
                     reads=[PB[c]], writes=[B_PT[c]])
                if mi is not None:
                    eng = 'dve' if mi == 0 else 'pool'
                    S.op(eng, lambda e, c=c, mi=mi: e.tensor_tensor(out=PT[c][:, 0:3 * nq].rearrange("p (g q) -> p g q", g=3),
                                                                     in0=PT[c][:, 0:3 * nq].rearrange("p (g q) -> p g q", g=3),
                                                                     in1=mask[:, mi, 0:nq].unsqueeze(1).to_broadcast([128, 3, nq]), op=ALU.mult),
                         reads=[B_in], writes=[B_PT[c]])
            ob = 5 + (i % 2)
            for g in range(3):
                for c, (kT_ap, v_ap, mi) in enumerate(chunks):
                    S.op('pe', lambda e, c=c, g=g, v_ap=v_ap: e.matmul(P[0:nq, ob, g * 65:(g + 1) * 65], lhsT=PT[c][:, g * nq:(g + 1) * nq], rhs=v_ap,
                                                                     start=(c == 0), stop=(c == nch - 1)),
                         reads=[B_PT[c], B_in], writes=[PB[ob]])
            ov = P[0:nq, ob, 0:195].rearrange("p (g e) -> p g e", g=3)
            dn = den[0:nq, i % 2, :]
            S.op('dve', lambda e: e.tensor_tensor(out=dn, in0=ov[:, :, 64], in1=esink[0:nq, kh * 3:kh * 3 + 3], op=ALU.add),
                 reads=[PB[ob], B_es], writes=[B_den[i % 2]])
            S.op('dve', lambda e: e.reciprocal(out=dn, in_=dn), reads=[B_den[i % 2]], writes=[B_den[i % 2]])
            S.op('dve', lambda e: e.tensor_tensor(out=otile[0:nq, kh * 192:(kh + 1) * 192].rearrange("p (g e) -> p g e", g=3), in0=ov[:, :, 0:64],
                                                  in1=dn.unsqueeze(2).to_broadcast([nq, 3, 64]), op=ALU.mult),
                 reads=[PB[ob], B_den[i % 2]], writes=[Bot])

        all_in = [o for o in S.ops['sp']]
        bo = S.op('pool', lambda e: e.memset(den[:].rearrange("p a b -> p (a b)"), 0.0), writes=[B_in, B_den[0], B_den[1]])
        bo.deps.extend(all_in)
        for n in range(nblk):
            otile = ot[n % 2]
            Bot = B_ot[n % 2]
            for kh in range(2):
                chunks = []
                for j, mi in [(0, 0), (1, None), (2, 1)]:
                    chunks.append((KT[:, kh, (n + j) * 128:(n + j + 1) * 128], V[:, n + j, kh, :], mi))
                for j in range(2):
                    chunks.append((KcT[:, kh, j * 128:(j + 1) * 128], Vc[:, j, kh, :], None))
                attn_block(QT[:, kh * 3:kh * 3 + 3, n * 128:(n + 1) * 128], 128, chunks, kh, otile, Bot)
            S.dma('sp', o_d[n * 128:(n + 1) * 128, :], otile[:], reads=[Bot])
        if with_ctx:
            otile = ot[nblk % 2]
            Bot = B_ot[nblk % 2]
            for kh in range(2):
                chunks = [(KcT[:, kh, j * 128:(j + 1) * 128], Vc[:, j, kh, :], None) for j in range(2)]
                attn_block(QcT[:, kh * 3:kh * 3 + 3, :], 64, chunks, kh, otile, Bot)
            S.dma('sp', oc_d, otile[0:64, :], reads=[Bot])
        S.emit()
    return nc


def run_attn(prep, sink):
    import ml_dtypes
    bf = ml_dtypes.bfloat16
    nc = _get('attn', build_attn)
    qk_l, qk_c = prep['qk']
    v_l, v_c = prep['v']
    kk = np.arange(128)[:, None]
    qq = np.arange(128)[None, :]
    mask = np.stack([(kk >= qq), (kk <= qq)], 1).astype(np.float32).astype(bf)
    maps = []
    for core in range(NCORE):
        b, q = core // 4, core % 4
        s0 = q * TL
        QT = np.ascontiguousarray(qk_l[b, s0:s0 + TL, 0:384].reshape(TL, 6, 64).transpose(2, 1, 0))
        kpad = np.zeros((8192 + 256, 128), bf)
        kpad[128:128 + 8192] = qk_l[b, :, 384:512]
        KT = np.ascontiguousarray(kpad[s0:s0 + TL + 256].reshape(TL + 256, 2, 64).transpose(2, 1, 0))
        KcT = np.ascontiguousarray(qk_c[b, :, 384:512].reshape(256, 2, 64).transpose(2, 1, 0))
        vpad = np.zeros((8192 + 256, 2, 65), bf)
        vpad[128:128 + 8192, :, 0:64] = v_l[b].reshape(8192, 2, 64)
        vpad[128:128 + 8192, :, 64] = 1.0
        V = np.ascontiguousarray(vpad[s0:s0 + TL + 256].reshape(18, 128, 2, 65).transpose(1, 0, 2, 3))
        vc = np.ones((256, 2, 65), bf)
        vc[:, :, 0:64] = v_c[b].reshape(256, 2, 64)
        Vc = np.ascontiguousarray(vc.reshape(2, 128, 2, 65).transpose(1, 0, 2, 3))
        QcT = np.ascontiguousarray(qk_c[b, q * TC:(q + 1) * TC, 0:384].reshape(TC, 6, 64).transpose(2, 1, 0))
        maps.append(dict(QT=QT, KT=KT, KcT=KcT, V=V, Vc=Vc, mask=mask, sink=np.ascontiguousarray(sink.reshape(1, 6).astype(np.float32)), QcT=QcT))
    res = run_bass_kernel_spmd(nc, maps, core_ids=list(range(NCORE)))
    al = np.zeros((2, 8192, 384), np.float32)
    ac = np.zeros((2, 256, 384), np.float32)
    for core in range(NCORE):
        b, q = core // 4, core % 4
        al[b, q * TL:(q + 1) * TL] = res.results[core]['o']
        ac[b, q * TC:(q + 1) * TC] = res.results[core]['oc']
    return al, ac


def build_fourier(with_ctx=True):
    nc = bass.Bass("TRN2", target_bir_lowering=False)
    x0_d = nc.dram_tensor("x0", [64, 64, 128], F32, kind="ExternalInput").ap()
    xc_d = nc.dram_tensor("xc0", [64, 256], F32, kind="ExternalInput").ap()
    cs64_d = nc.dram_tensor("cs64", [64, 128], F32, kind="ExternalInput").ap()
    f128_d = nc.dram_tensor("f128", [128, 3, 128], F32, kind="ExternalInput").ap()
    tw_d = nc.dram_tensor("tw", [128, 2, 64], F32, kind="ExternalInput").ap()
    f64_d = nc.dram_tensor("f64", [64, 2, 64], F32, kind="ExternalInput").ap()
    f256_d = nc.dram_tensor("f256", [128, 2, 2, 256], F32, kind="ExternalInput").ap()
    scr = nc.dram_tensor("scr", [128, 64, 128], F32, kind="ExternalOutput").ap()
    y_d = nc.dram_tensor("y", [64, 128 * 64], F32, kind="ExternalOutput").ap()
    yc_d = nc.dram_tensor("yc", [256, 64], F32, kind="ExternalOutput").ap()
    with ExitStack() as st:
        S = Sched(nc, st)
        P = st.enter_context(nc.psum_tensor("P", [128, 8, 512], F32))
        PB = [S.buf(f'pb{i}') for i in range(8)]
        X0 = S.sb("X0", [64, 64 * 128], F32)
        X1 = S.sb("X1", [128, 64, 128], F32)
        B2 = S.sb("B2", [64, 128, 128], F32)
        cs64 = S.sb("cs64_sb", [64, 128], F32)
        f128 = S.sb("f128_sb", [128, 3, 128], F32)
        tw = S.sb("tw_sb", [128, 2, 64], F32)
        f64 = S.sb("f64_sb", [64, 2, 64], F32)
        f256 = S.sb("f256_sb", [128, 2, 2, 256], F32)
        xc = S.sb("xc_sb", [64, 256], F32)
        zc = S.sb("zc_sb", [128, 2, 128], F32)
        ycs = S.sb("ycs", [128, 2, 64], F32)
        Bt = [S.sb(f"Bt{i}", [128, 8, 2, 64], F32) for i in range(2)]
        tmp = [S.sb(f"ftmp{i}", [128, 8, 64], F32) for i in range(2)]
        B_X0, B_X1, B_B2, B_c, B_scr = S.buf('X0'), S.buf('X1'), S.buf('B2'), S.buf('c'), S.buf('scr')
        B_Bt = [S.buf('Bt0'), S.buf('Bt1')]
        B_tmp = [S.buf('tmp0'), S.buf('tmp1')]
        B_xc, B_zc, B_yc = S.buf('xc'), S.buf('zc'), S.buf('yc')
        S.dma('sp', X0[:], x0_d.rearrange("c a b -> c (a b)"), writes=[B_X0])
        for t, d in [(cs64, cs64_d), (f128, f128_d), (tw, tw_d), (f64, f64_d), (f256, f256_d)]:
            S.dma('sp', t[:], d, writes=[B_c])
        S.dma('sp', xc[:], xc_d, writes=[B_xc])
        for grp in range(16):
            bank = grp % 2
            for j in range(4):
                n2 = grp * 4 + j
                S.op('pe', lambda e, n2=n2, j=j, bank=bank: e.matmul(P[:, bank, j * 128:(j + 1) * 128], lhsT=X0[:, n2 * 128:(n2 + 1) * 128], rhs=cs64[:],
                                                                    start=True, stop=True),
                     reads=[B_X0, B_c], writes=[PB[bank]])
            dst = X1[:, grp * 4:(grp + 1) * 4, :].rearrange("p a b -> p (a b)")
            if grp % 2 == 0:
                S.op('act', lambda e, dst=dst, bank=bank: e.activation(func=AF.Copy, out=dst, in_=P[:, bank, :]), reads=[PB[bank]], writes=[B_X1])
            else:
                S.op('dve', lambda e, dst=dst, bank=bank: e.tensor_copy(out=dst, in_=P[:, bank, :]), reads=[PB[bank]], writes=[B_X1])
        for ch in range(8):
            zr = X1[:, ch * 8:(ch + 1) * 8, 0:64]
            zi = X1[:, ch * 8:(ch + 1) * 8, 64:128]
            ba = 2 + (ch % 2) * 2
            ar = P[:, ba, :].rearrange("p (a b) -> p a b", a=8)
            ai = P[:, ba + 1, :].rearrange("p (a b) -> p a b", a=8)
            S.op('pe', lambda e, ar=ar, zr=zr: e.matmul(ar, lhsT=f128[:, 0, :], rhs=zr, start=True, stop=False), reads=[B_X1, B_c], writes=[PB[ba]])
            S.op('pe', lambda e, ar=ar, zi=zi: e.matmul(ar, lhsT=f128[:, 1, :], rhs=zi, start=False, stop=True), reads=[B_X1, B_c], writes=[PB[ba]])
            S.op('pe', lambda e, ai=ai, zi=zi: e.matmul(ai, lhsT=f128[:, 0, :], rhs=zi, start=True, stop=False), reads=[B_X1, B_c], writes=[PB[ba + 1]])
            S.op('pe', lambda e, ai=ai, zr=zr: e.matmul(ai, lhsT=f128[:, 2, :], rhs=zr, start=False, stop=True), reads=[B_X1, B_c], writes=[PB[ba + 1]])
            tc_ = tw[:, 0, ch * 8:(ch + 1) * 8].unsqueeze(2).to_broadcast([128, 8, 64])
            ts_ = tw[:, 1, ch * 8:(ch + 1) * 8].unsqueeze(2).to_broadcast([128, 8, 64])
            bt = Bt[ch % 2]
            Bb = B_Bt[ch % 2]
            t0_, t1_ = tmp
            S.op('dve', lambda e, bt=bt, ar=ar, tc_=tc_: e.tensor_tensor(out=bt[:, :, 0, :], in0=ar, in1=tc_, op=ALU.mult), reads=[PB[ba], B_c], writes=[Bb])
            S.op('dve', lambda e, ai=ai, ts_=ts_: e.tensor_tensor(out=t0_[:], in0=ai, in1=ts_, op=ALU.mult), reads=[PB[ba + 1], B_c], writes=[B_tmp[0]])
            S.op('dve', lambda e, bt=bt, ai=ai, tc_=tc_: e.tensor_tensor(out=bt[:, :, 1, :], in0=ai, in1=tc_, op=ALU.mult), reads=[PB[ba + 1], B_c], writes=[Bb])
            S.op('dve', lambda e, ar=ar, ts_=ts_: e.tensor_tensor(out=t1_[:], in0=ar, in1=ts_, op=ALU.mult), reads=[PB[ba], B_c], writes=[B_tmp[1]])
            S.op('pool', lambda e, bt=bt: e.tensor_tensor(out=bt[:, :, 0, :], in0=bt[:, :, 0, :], in1=t0_[:], op=ALU.add), reads=[B_tmp[0]], writes=[Bb])
            S.op('pool', lambda e, bt=bt: e.tensor_tensor(out=bt[:, :, 1, :], in0=bt[:, :, 1, :], in1=t1_[:], op=ALU.subtract), reads=[B_tmp[1]], writes=[Bb])
            S.dma('sp', scr[:, ch * 8:(ch + 1) * 8, :], bt[:].rearrange("p a b c -> p a (b c)"), reads=[Bb], writes=[B_scr], sembuf=Bb)
        for q in range(4):
            S.dma('sp', B2[:, q * 32:(q + 1) * 32, :], scr[q * 32:(q + 1) * 32, :, :].rearrange("k n c -> n k c"), reads=[B_scr], writes=[B_B2])
        Y = X0
        for ch in range(16):
            bank = 6 + (ch % 2)
            br = B2[:, ch * 8:(ch + 1) * 8, 0:64]
            bi = B2[:, ch * 8:(ch + 1) * 8, 64:128]
            ov = P[0:64, bank, :].rearrange("p (a b) -> p a b", a=8)
            S.op('pe', lambda e, ov=ov, br=br: e.matmul(ov, lhsT=f64[:, 0, :], rhs=br, start=True, stop=False), reads=[B_B2, B_c], writes=[PB[bank]])
            S.op('pe', lambda e, ov=ov, bi=bi: e.matmul(ov, lhsT=f64[:, 1, :], rhs=bi, start=False, stop=True), reads=[B_B2, B_c], writes=[PB[bank]])
            if ch % 2 == 0:
                S.op('act', lambda e, ch=ch, bank=bank: e.activation(func=AF.Copy, out=Y[:, ch * 512:(ch + 1) * 512], in_=P[0:64, bank, :]), reads=[PB[bank]], writes=[B_X0])
            else:
                S.op('dve', lambda e, ch=ch, bank=bank: e.tensor_copy(out=Y[:, ch * 512:(ch + 1) * 512], in_=P[0:64, bank, :]), reads=[PB[bank]], writes=[B_X0])
        S.dma('sp', y_d, Y[:], reads=[B_X0])
        if with_ctx:
            for nt in range(2):
                S.op('pe', lambda e, nt=nt: e.matmul(P[:, 0, nt * 128:(nt + 1) * 128], lhsT=xc[:, nt * 128:(nt + 1) * 128], rhs=cs64[:], start=True, stop=True),
                     reads=[B_xc, B_c], writes=[PB[0]])
            S.op('act', lambda e: e.activation(func=AF.Copy, out=zc[:].rearrange("p a b -> p (a b)"), in_=P[:, 0, 0:256]), reads=[PB[0]], writes=[B_zc])
            for kt in range(2):
                ov = P[:, 1, kt * 64:(kt + 1) * 64]
                first = True
                for nt in range(2):
                    S.op('pe', lambda e, ov=ov, nt=nt, kt=kt, first=first: e.matmul(ov, lhsT=f256[:, nt, 0, kt * 128:(kt + 1) * 128], rhs=zc[:, nt, 0:64],
                                                                                  start=first, stop=False), reads=[B_zc, B_c], writes=[PB[1]])
                    first = False
                    S.op('pe', lambda e, ov=ov, nt=nt, kt=kt: e.matmul(ov, lhsT=f256[:, nt, 1, kt * 128:(kt + 1) * 128], rhs=zc[:, nt, 64:128],
                                                                     start=False, stop=(nt == 1)), reads=[B_zc, B_c], writes=[PB[1]])
            S.op('act', lambda e: e.activation(func=AF.Copy, out=ycs[:].rearrange("p a b -> p (a b)"), in_=P[:, 1, 0:128]), reads=[PB[1]], writes=[B_yc])
            S.dma('sp', yc_d.rearrange("(kt p) d -> p kt d", p=128), ycs[:], reads=[B_yc])
        S.emit()
    return nc


def fourier_tables():
    f64_ = np.float64

    def cs(n):
        i = np.arange(n)
        ang = 2 * np.pi * np.outer(i, i) / n
        return np.cos(ang), np.sin(ang)
    c64, s64 = cs(64)
    c128, s128 = cs(128)
    c256, s256 = cs(256)
    cs64 = np.concatenate([c64, -s64], 1).astype(np.float32)
    f128 = np.stack([c128, s128, -s128], 1).astype(np.float32)
    k1 = np.arange(128)[:, None]
    n2 = np.arange(64)[None, :]
    ang = 2 * np.pi * k1 * n2 / 8192.0
    tw = np.stack([np.cos(ang), np.sin(ang)], 1).astype(np.float32)
    sc = 1.0 / np.sqrt(8192.0 * 64.0)
    f64t = np.stack([c64 * sc, s64 * sc], 1).astype(np.float32)
    scc = 1.0 / np.sqrt(256.0 * 64.0)
    f256 = np.stack([c256 * scc, s256 * scc], 1).astype(np.float32)
    f256 = np.ascontiguousarray(f256.reshape(2, 128, 2, 256).transpose(1, 0, 2, 3))
    return dict(cs64=cs64, f128=f128, tw=tw, f64=f64t, f256=f256)


def run_fourier(fl, fc):
    nc = _get('fourier', build_fourier)
    tabs = fourier_tables()
    maps = []
    for core in range(NCORE):
        b, g = core // 4, core % 4
        xg = fl[b, :, g * 64:(g + 1) * 64]
        x0 = np.ascontiguousarray(xg.reshape(128, 64, 64).transpose(2, 1, 0))
        xc0 = np.ascontiguousarray(fc[b, :, g * 64:(g + 1) * 64].T)
        m = dict(x0=x0, xc0=xc0)
        m.update(tabs)
        maps.append(m)
    res = run_bass_kernel_spmd(nc, maps, core_ids=list(range(NCORE)))
    ol = np.zeros((2, 8192, 256), np.float32)
    oc = np.zeros((2, 256, 256), np.float32)
    for core in range(NCORE):
        b, g = core // 4, core % 4
        ol[b, :, g * 64:(g + 1) * 64] = res.results[core]['y'].reshape(8192, 64)
        oc[b, :, g * 64:(g + 1) * 64] = res.results[core]['yc']
    return ol, oc


def kernel(x, c, ctx, c_ctx, mod_w, mod_b, norm_g, ffn1_wi, ffn1_wo, mix_w_in, mix_w_out, attn_sink,
           rwkv_conv, rwkv_w0, rwkv_w2, rwkv_a0, rwkv_a2, rwkv_g2, rwkv_k_k, rwkv_k_a, rwkv_r_k,
           rwkv_ln_g, rwkv_ln_b, ffn2_wi, ffn2_wo):
    f = lambda a: np.asarray(a, dtype=np.float32)
    x, c, ctx, c_ctx = f(x), f(c), f(ctx), f(c_ctx)
    xs = shard_tokens(x, ctx)
    for li in range(2):
        mw, mb, g = f(mod_w[li]), f(mod_b[li]), f(norm_g[li])
        xs = run_ffn(xs, c, c_ctx, mw[:, 0:3 * D], mb[0:3 * D], g[0], g[1], f(ffn1_wi[li]), f(ffn1_wo[li]))
        zs = run_inproj(xs, c, c_ctx, mw[:, 3 * D:5 * D], mb[3 * D:5 * D], g[2], f(mix_w_in[li]))
        zl, zc = unshard_tokens(zs, INW)
        prep = run_prep(zl, zc, f(rwkv_conv[li]), f(rwkv_w0[li]), f(rwkv_w2[li]), f(rwkv_a0[li]), f(rwkv_a2[li]),
                        f(rwkv_k_k[li]), f(rwkv_k_a[li]))
        al, ac = run_attn(prep, f(attn_sink[li]))
        fl, fc = run_fourier(np.ascontiguousarray(zl[..., 0:256]), np.ascontiguousarray(zc[..., 0:256]))
        yl, yc = run_scan2(prep)
        fa = shard_tokens(np.concatenate([fl, al], -1), np.concatenate([fc, ac], -1))
        yf = shard_tokens(yl[0], yc[0])
        yb = shard_tokens(yl[1], yc[1])
        rkv = shard_tokens(prep['rkv'][0], prep['rkv'][1])
        zg = shard_tokens(zl[..., 2304:2432], zc[..., 2304:2432])
        vecs = np.stack([f(rwkv_r_k[li]).reshape(384), f(rwkv_ln_g[li]), f(rwkv_ln_b[li])], 0)
        xs = run_outproj(xs, c, c_ctx, mw[:, 5 * D:6 * D], mb[5 * D:6 * D], g[3], f(mix_w_out[li]), fa, yf, yb, rkv, zg,
                         f(rwkv_g2[li]), vecs)
        xs = run_ffn(xs, c, c_ctx, mw[:, 6 * D:9 * D], mb[6 * D:9 * D], g[4], g[5], f(ffn2_wi[li]), f(ffn2_wo[li]))
    xl, _ = unshard_tokens(xs)
    return xl
```

```python
import numpy as np
import concourse.bass as bass
import concourse.mybir as mybir
from concourse.bass_utils import run_bass_kernel_spmd
from contextlib import ExitStack

F32 = mybir.dt.float32
BF16 = mybir.dt.bfloat16
AF = mybir.ActivationFunctionType
ALU = mybir.AluOpType
AX = mybir.AxisListType

D = 1024
FF = 2752
NF = 22
NCORE = 8
TL = 2048
TC = 64
EPS = 1e-6

ENGS = ['pe', 'act', 'dve', 'pool', 'sp']


class Buf:
    __slots__ = ('name', 'w', 'r', 'sem', 'semval')

    def __init__(self, name):
        self.name = name
        self.w = None
        self.r = []
        self.sem = None
        self.semval = 0


class Op:
    __slots__ = ('eng', 'fn', 'deps', 'is_dma', 'signal', 'val', 'sem', 'idx')

    def __init__(self, eng, fn, is_dma=False):
        self.eng = eng
        self.fn = fn
        self.deps = []
        self.is_dma = is_dma
        self.signal = False
        self.val = 0
        self.sem = None
        self.idx = 0


class Sched:
    def __init__(self, nc, stack):
        self.nc = nc
        self.stack = stack
        self.ops = {e: [] for e in ENGS}
        self.esem = {}
        for e in ['pe', 'act', 'dve', 'pool']:
            self.esem[e] = stack.enter_context(nc.semaphore('es_' + e))
        self.dma_bufs = []
        self.nbuf = 0

    def sb(self, name, shape, dtype):
        return self.stack.enter_context(self.nc.sbuf_tensor(name, list(shape), dtype))

    def buf(self, name=None):
        self.nbuf += 1
        return Buf(name or f'b{self.nbuf}')

    def _track(self, o, reads, writes):
        deps = []
        for b in reads:
            if b.w is not None:
                deps.append(b.w)
        for b in writes:
            if b.w is not None:
                deps.append(b.w)
            deps.extend(b.r)
        seen = set()
        for d in deps:
            if d is o or id(d) in seen:
                continue
            seen.add(id(d))
            o.deps.append(d)
        for b in reads:
            b.r.append(o)
        for b in writes:
            b.w = o
            b.r = []

    def op(self, eng, fn, reads=(), writes=()):
        o = Op(eng, fn)
        self._track(o, reads, writes)
        o.idx = len(self.ops[eng])
        self.ops[eng].append(o)
        return o

    def dma(self, eng, out, in_, reads=(), writes=(), sembuf=None, **kw):
        o = Op(eng, None, is_dma=True)
        self._track(o, reads, writes)
        sb_ = sembuf or (writes[0] if writes else reads[0])
        if sb_.sem is None:
            sb_.sem = self.stack.enter_context(self.nc.semaphore('ds_' + sb_.name))
            self.dma_bufs.append(sb_)
        sb_.semval += 16
        o.sem = sb_.sem
        o.val = sb_.semval
        o.fn = lambda e: e.dma_start(out=out, in_=in_, **kw)
        o.idx = len(self.ops[eng])
        self.ops[eng].append(o)
        return o

    def _needs_sync(self, o, d):
        if d.is_dma:
            return True
        if d.eng != o.eng:
            return True
        return d.eng != 'pe'

    def emit(self):
        for e in ENGS:
            for o in self.ops[e]:
                for d in o.deps:
                    if (not d.is_dma) and self._needs_sync(o, d):
                        d.signal = True
        EPOCH = 4000
        for e in ENGS:
            c = 0
            sems = [self.esem[e]] if e in self.esem else []
            for o in self.ops[e]:
                if o.is_dma:
                    continue
                if o.signal:
                    ep, v = divmod(c, EPOCH)
                    if ep >= len(sems):
                        sems.append(self.stack.enter_context(self.nc.semaphore(f'es_{e}_{ep}')))
                    c += 1
                    o.val = v + 1
                    o.sem = sems[ep]
        finals = [(b.sem, b.semval) for b in self.dma_bufs]

        def run(e, h):
            seen = {}
            for o in self.ops[e]:
                for d in o.deps:
                    if not self._needs_sync(o, d):
                        continue
                    k = id(d.sem)
                    if seen.get(k, 0) >= d.val:
                        continue
                    seen[k] = d.val
                    h.wait_ge(d.sem, d.val)
                ins = o.fn(h)
                if o.is_dma:
                    ins.then_inc(o.sem, 16)
                elif o.signal:
                    ins.then_inc(o.sem, 1)
            if e == 'sp':
                for s, v in finals:
                    h.wait_ge(s, v)

        with self.nc.Block() as block:
            @block.tensor
            def _(h):
                run('pe', h)

            @block.scalar
            def _(h):
                run('act', h)

            @block.vector
            def _(h):
                run('dve', h)

            @block.gpsimd
            def _(h):
                run('pool', h)

            @block.sync
            def _(h):
                run('sp', h)


def mkap(t, offset, pat):
    return bass.AP(t.tensor, offset, [list(p) for p in pat])


class Ctx:
    pass


def setup_common(S, nc):
    C = Ctx()
    C.P = S.stack.enter_context(nc.psum_tensor("P", [128, 8, 512], F32))
    C.PB = [S.buf(f'pb{i}') for i in range(8)]
    C.ident_f = S.sb("ident_f", [128, 128], F32)
    C.ident_b = S.sb("ident_b", [128, 128], BF16)
    C.B_ident = S.buf('ident')
    return C


def load_ident(S, C, ident_dram):
    S.dma('sp', C.ident_f[:], ident_dram, writes=[C.B_ident])
    S.op('dve', lambda e: e.tensor_copy(out=C.ident_b[:], in_=C.ident_f[:]), reads=[C.B_ident], writes=[C.B_ident])


def rstd_from_ss(S, ss, rstd, B_ss, B_rstd, n, eps):
    S.op('dve', lambda e: e.tensor_scalar(out=rstd, in0=ss, scalar1=1.0 / n, scalar2=eps, op0=ALU.mult, op1=ALU.add),
         reads=[B_ss], writes=[B_rstd])
    S.op('act', lambda e: e.activation(out=rstd, in_=rstd, func=AF.Sqrt), reads=[B_rstd], writes=[B_rstd])
    S.op('dve', lambda e: e.reciprocal(out=rstd, in_=rstd), reads=[B_rstd], writes=[B_rstd])


def rows_to_cols(S, C, rows_sb, B_rows, off, out_cols, B_out, bank):
    for dc in range(8):
        S.op('pe', lambda e, dc=dc: e.transpose(out=C.P[:, bank, dc * 2:dc * 2 + 2], in_=rows_sb[0:2, off + dc * 128:off + (dc + 1) * 128],
                                                identity=C.ident_f[0:2, 0:2]),
             reads=[B_rows, C.B_ident], writes=[C.PB[bank]])
    S.op('dve', lambda e: e.tensor_copy(out=out_cols[:].rearrange("p a b -> p (a b)"), in_=C.P[:, bank, 0:16]),
         reads=[C.PB[bank]], writes=[B_out])


def rows_bcast(S, C, rows_sb, B_rows, off, sel, B_sel, cond, bank0):
    for hh in range(2):
        S.op('pe', lambda e, hh=hh: e.matmul(C.P[:, bank0 + hh, :], lhsT=sel[0:2, cond, :], rhs=rows_sb[0:2, off + hh * 512:off + (hh + 1) * 512],
                                             start=True, stop=True),
             reads=[B_rows, B_sel], writes=[C.PB[bank0 + hh]])


INW = 2432
GN_EPS = 64e-5


def build_dense(mode, ntiles_l=16, has_ctx=True):
    nc = bass.Bass("TRN2", target_bir_lowering=False)
    T = ntiles_l * 128 + (TC if has_ctx else 0)
    nmod = {'ffn': 3, 'inproj': 2, 'outproj': 1}[mode]
    x = nc.dram_tensor("x", [T, D], F32, kind="ExternalInput").ap()
    condT_d = nc.dram_tensor("condT", [128, 8, 2], F32, kind="ExternalInput").ap()
    mw = nc.dram_tensor("mw", [D, nmod * D], F32, kind="ExternalInput").ap()
    mb = nc.dram_tensor("mb", [2, nmod * D], F32, kind="ExternalInput").ap()
    ident_d = nc.dram_tensor("ident", [128, 128], F32, kind="ExternalInput").ap()
    sel_d = nc.dram_tensor("sel", [2, 2, 128], F32, kind="ExternalInput").ap()
    if mode != 'outproj':
        gpreT_d = nc.dram_tensor("gpreT", [128, 8], F32, kind="ExternalInput").ap()
    if mode != 'inproj':
        gpost_d = nc.dram_tensor("gpost", [1, D], F32, kind="ExternalInput").ap()
        y = nc.dram_tensor("y", [T, D], F32, kind="ExternalOutput").ap()
    if mode == 'ffn':
        wi = nc.dram_tensor("wi", [D, 2 * FF], F32, kind="ExternalInput").ap()
        wo = nc.dram_tensor("wo", [FF, D], F32, kind="ExternalInput").ap()
    elif mode == 'inproj':
        w_d = nc.dram_tensor("w", [D, INW], F32, kind="ExternalInput").ap()
        z_d = nc.dram_tensor("z", [T, INW], F32, kind="ExternalOutput").ap()
    else:
        w_d = nc.dram_tensor("w", [D, D], F32, kind="ExternalInput").ap()
        fa_d = nc.dram_tensor("fa", [T, 640], F32, kind="ExternalInput").ap()
        yf_d = nc.dram_tensor("yf", [T, 384], F32, kind="ExternalInput").ap()
        yb_d = nc.dram_tensor("yb", [T, 384], F32, kind="ExternalInput").ap()
        rkv_d = nc.dram_tensor("rkv", [T, 1152], F32, kind="ExternalInput").ap()
        zg_d = nc.dram_tensor("zg", [T, 128], F32, kind="ExternalInput").ap()
        g2_d = nc.dram_tensor("g2", [128, 384], F32, kind="ExternalInput").ap()
        vec_d = nc.dram_tensor("vecs", [3, 384], F32, kind="ExternalInput").ap()

    with ExitStack() as st:
        S = Sched(nc, st)
        C = setup_common(S, nc)
        P = C.P
        load_ident(S, C, ident_d)
        xg = [S.sb(f"xg{i}", [128, 2, D], F32) for i in range(2)]
        B_xg = [[S.buf(f'xg{i}_{j}') for j in range(2)] for i in range(2)]
        stg_ap = [xg[0][:].rearrange("p a b -> p (a b)"), xg[1][:].rearrange("p a b -> p (a b)")]
        B_stg = [B_xg[0][0], B_xg[1][0]]
        condT = S.sb("condT_sb", [128, 8, 2], F32)
        C.B_cond = S.buf('cond')
        rows_sb = S.sb("rows_sb", [2, nmod * D], F32)
        B_rows = S.buf('rows')
        sel = S.sb("sel_sb", [2, 2, 128], F32)
        B_sel = S.buf('sel')
        B_g = S.buf('g')
        B_modT = S.buf('modT')
        B_GG = S.buf('GG')

        S.dma('sp', condT[:], condT_d, writes=[C.B_cond])
        S.op('act', lambda e: e.activation(out=condT[:], in_=condT[:], func=AF.Silu), reads=[C.B_cond], writes=[C.B_cond])
        S.dma('sp', rows_sb[:], mb, writes=[B_rows])
        S.dma('sp', sel[:], sel_d, writes=[B_sel])
        nchunk = nmod * 2
        for k in range(8):
            ncol = nmod * D
            S.dma('sp', stg_ap[0][:, 0:min(2048, ncol)], mw[k * 128:(k + 1) * 128, 0:min(2048, ncol)], writes=[B_stg[0]])
            if ncol > 2048:
                S.dma('sp', stg_ap[1][:, 0:ncol - 2048], mw[k * 128:(k + 1) * 128, 2048:ncol], writes=[B_stg[1]])
            for n in range(nchunk):
                sa = stg_ap[0] if n < 4 else stg_ap[1]
                cc = n * 512 if n < 4 else (n - 4) * 512
                S.op('pe', lambda e, sa=sa, n=n, k=k, cc=cc: e.matmul(P[0:2, n, :], lhsT=condT[:, k, :], rhs=sa[:, cc:cc + 512],
                                                                     start=(k == 0), stop=(k == 7)),
                     reads=[B_stg[0] if n < 4 else B_stg[1], C.B_cond], writes=[C.PB[n]])
        for n in range(nchunk):
            S.op('dve', lambda e, n=n: e.tensor_tensor(out=rows_sb[0:2, n * 512:(n + 1) * 512], in0=P[0:2, n, :],
                                                      in1=rows_sb[0:2, n * 512:(n + 1) * 512], op=ALU.add),
                 reads=[C.PB[n], B_rows], writes=[B_rows])
        if mode != 'outproj':
            gpreT = S.sb("gpreT_sb", [128, 8], F32)
            S.dma('sp', gpreT[:], gpreT_d, writes=[B_g])
            modT = [S.sb(f"modT{i}", [128, 8, 2], F32) for i in range(2)]
            G1T = S.sb("G1T", [128, 8, 2], F32)
            rows_to_cols(S, C, rows_sb, B_rows, 0, modT[0], B_modT, 6)
            rows_to_cols(S, C, rows_sb, B_rows, D, modT[1], B_modT, 7)
            S.op('dve', lambda e: e.tensor_scalar(out=G1T[:], in0=modT[1][:], scalar1=1.0, scalar2=None, op0=ALU.add),
                 reads=[B_modT], writes=[B_modT])
            S.op('pool', lambda e: e.tensor_tensor(out=G1T[:], in0=G1T[:], in1=gpreT[:].unsqueeze(2).to_broadcast([128, 8, 2]), op=ALU.mult),
                 reads=[B_modT, B_g], writes=[B_modT])
        if mode != 'inproj':
            gpost_bc = S.sb("gpost_bc", [128, D], F32)
            S.dma('sp', gpost_bc[:], mkap(gpost_d, 0, [(0, 128), (1, D)]), writes=[B_g])
            GG = S.sb("GG", [128, 2, D], F32)
            goff = 2 * D if mode == 'ffn' else 0
            gfac = 0.5 if mode == 'ffn' else 1.0
            for cond in range(2):
                rows_bcast(S, C, rows_sb, B_rows, goff, sel, B_sel, cond, 0 + 2 * cond)
                for hh in range(2):
                    S.op('dve', lambda e, cond=cond, hh=hh: e.scalar_tensor_tensor(
                        out=GG[:, cond, hh * 512:(hh + 1) * 512], in0=P[:, 2 * cond + hh, :], scalar=gfac,
                        in1=gpost_bc[:, hh * 512:(hh + 1) * 512], op0=ALU.mult, op1=ALU.mult),
                        reads=[C.PB[2 * cond + hh], B_g], writes=[B_GG])

        cast_engs = ['act', 'pool', 'dve']
        cnt = [0, 0]

        def cast_in(dst_ap, src_dram, rows, cols, Bdst):
            sa = stg_ap[cnt[0] % 2]
            Bs = B_stg[cnt[0] % 2]
            cnt[0] += 1
            S.dma('sp', sa[0:rows, 0:cols], src_dram, writes=[Bs])
            eng = cast_engs[cnt[1] % 3]
            cnt[1] += 1
            if eng == 'act':
                S.op('act', lambda e: e.activation(func=AF.Copy, out=dst_ap, in_=sa[0:rows, 0:cols]), reads=[Bs], writes=[Bdst])
            else:
                S.op(eng, lambda e: e.tensor_copy(out=dst_ap, in_=sa[0:rows, 0:cols]), reads=[Bs], writes=[Bdst])

        B_w = S.buf('w')
        B_wo = S.buf('wo')
        if mode == 'ffn':
            wi_sb = S.sb("wi_sb", [128, 8, 2 * FF], BF16)
            wo_sb = S.sb("wo_sb", [128, NF, D], BF16)
            for k in range(8):
                for q in range(4):
                    c0 = q * 1376
                    cast_in(wi_sb[:, k, c0:c0 + 1376], wi[k * 128:(k + 1) * 128, c0:c0 + 1376], 128, 1376, B_w)
            for fc in range(NF):
                rows = 128 if fc < NF - 1 else 64
                cast_in(wo_sb[0:rows, fc, :], wo[fc * 128:fc * 128 + rows, :], rows, D, B_wo)
        else:
            wcols = INW if mode == 'inproj' else D
            w_sb = S.sb("w_sb", [128, 8, wcols], BF16)
            for k in range(8):
                for c0 in range(0, wcols, 1216 if mode == 'inproj' else 1024):
                    cw = min(1216 if mode == 'inproj' else 1024, wcols - c0)
                    cast_in(w_sb[:, k, c0:c0 + cw], w_d[k * 128:(k + 1) * 128, c0:c0 + cw], 128, cw, B_w)
        if mode == 'outproj':
            g2_sb = S.sb("g2_sb", [128, 384], BF16)
            cast_in(g2_sb[:], g2_d, 128, 384, B_w)
            vec_bc = S.sb("vec_bc", [128, 3, 384], F32)
            B_vec = S.buf('vec')
            S.dma('sp', vec_bc[:], mkap(vec_d, 0, [(0, 128), (384, 3), (1, 384)]), writes=[B_vec])

        tiles = [(i * 128, 128, 0) for i in range(ntiles_l)]
        if has_ctx:
            tiles.append((ntiles_l * 128, TC, 1))
        groups = [tiles[i:i + 2] for i in range(0, len(tiles), 2)]
        xn = S.sb("xn", [128, 2, D], BF16)
        B_xn = [S.buf('xn0'), S.buf('xn1')]
        junk = S.sb("junk", [128, D], BF16)
        B_junk = S.buf('junk')
        stat = S.sb("stat", [128, 8], F32)
        B_stat = [S.buf('st0'), S.buf('st1'), S.buf('st2'), S.buf('st3')]
        hT = S.sb("hT", [128, 8, 256], BF16)
        B_hT = S.buf('hT')
        PBF = P[:, 6, :].bitcast(BF16)
        if mode == 'ffn':
            hid = S.sb("hid", [128, NF, 256], BF16)
            B_hid = S.buf('hid')
            sg = [S.sb(f"sg{i}", [128, 256], F32) for i in range(2)]
            B_sg = [S.buf('sg0'), S.buf('sg1')]
        if mode != 'inproj':
            tmp = S.sb("tmp", [128, D], F32)
            B_tmp = S.buf('tmp')
        if mode == 'inproj':
            zt = [S.sb(f"zt{i}", [128, INW], F32) for i in range(2)]
            B_zt = [S.buf('zt0'), S.buf('zt1')]
        if mode == 'outproj':
            IN = [dict(fa=S.sb(f"fa{i}", [128, 640], F32), yf=S.sb(f"yf{i}", [128, 384], F32), yb=S.sb(f"yb{i}", [128, 384], F32),
                       rkv=S.sb(f"rkv{i}", [128, 1152], F32), zg=S.sb(f"zg{i}", [128, 128], F32)) for i in range(2)]
            B_IN = [S.buf('in0'), S.buf('in1')]
            w1 = S.sb("w1", [128, 384], F32)
            w2 = S.sb("w2", [128, 384], F32)
            w3 = S.sb("w3", [128, 384], F32)
            B_w1, B_w2, B_w3 = S.buf('w1'), S.buf('w2'), S.buf('w3')
            s6 = S.sb("s6", [128, 4, 6], F32)
            B_s6 = [S.buf(f's6{i}') for i in range(4)]
            sgb = S.sb("sgb", [128, 128], BF16)
            sgT = S.sb("sgT", [128, 128], BF16)
            B_sgb, B_sgT = S.buf('sgb'), S.buf('sgT')
            PBF1 = P[:, 1, :].bitcast(BF16)

        tcount = [0]

        def load_group(gi):
            for j, (t0, n, cond) in enumerate(groups[gi]):
                S.dma('sp', xg[gi % 2][0:n, j, :], x[t0:t0 + n, :], writes=[B_xg[gi % 2][j]])

        def load_tile_inputs(ti):
            t0, n, cond = tiles[ti]
            d = IN[ti % 2]
            Bi = B_IN[ti % 2]
            S.dma('sp', d['fa'][0:n, :], fa_d[t0:t0 + n, :], writes=[Bi])
            S.dma('sp', d['yf'][0:n, :], yf_d[t0:t0 + n, :], writes=[Bi])
            S.dma('sp', d['yb'][0:n, :], yb_d[t0:t0 + n, :], writes=[Bi])
            S.dma('sp', d['rkv'][0:n, :], rkv_d[t0:t0 + n, :], writes=[Bi])
            S.dma('sp', d['zg'][0:n, :], zg_d[t0:t0 + n, :], writes=[Bi])

        def v6(ap, n):
            return ap[0:n, :].rearrange("p (h e) -> p h e", h=6)

        def bc6(ap6, n):
            return ap6.unsqueeze(2).to_broadcast([n, 6, 64])

        def readout_tile(ti, j):
            t0, n, cond = tiles[ti]
            d = IN[ti % 2]
            Bi = B_IN[ti % 2]
            r_ = d['rkv'][0:n, 0:384]
            k_ = d['rkv'][0:n, 384:768]
            v_ = d['rkv'][0:n, 768:1152]
            S.op('act', lambda e: e.activation(func=AF.Copy, out=xn[0:n, j, 0:640], in_=d['fa'][0:n, :]), reads=[Bi], writes=[B_xn[j]])
            S.op('act', lambda e: e.activation(out=sgb[0:n, :], in_=d['zg'][0:n, :], func=AF.Sigmoid), reads=[Bi], writes=[B_sgb])
            S.op('pe', lambda e: e.transpose(out=PBF1[:, 0:n], in_=sgb[0:n, :], identity=C.ident_b[0:n, 0:n]),
                 reads=[B_sgb, C.B_ident], writes=[C.PB[1]])
            S.op('act', lambda e: e.activation(func=AF.Copy, out=sgT[:, 0:n], in_=PBF1[:, 0:n]), reads=[C.PB[1]], writes=[B_sgT])
            S.op('pe', lambda e: e.matmul(P[0:n, 0, 0:384], lhsT=sgT[:, 0:n], rhs=g2_sb[:], start=True, stop=True),
                 reads=[B_sgT, B_w], writes=[C.PB[0]])
            S.op('pool', lambda e: e.tensor_tensor(out=w1[0:n, :], in0=d['yf'][0:n, :], in1=d['yb'][0:n, :], op=ALU.add),
                 reads=[Bi], writes=[B_w1])
            S.op('dve', lambda e: e.tensor_reduce(out=s6[0:n, 0, :], in_=v6(w1, n), axis=AX.X, op=ALU.add), reads=[B_w1], writes=[B_s6[0]])
            S.op('dve', lambda e: e.scalar_tensor_tensor(out=v6(w2, n), in0=bc6(s6[0:n, 0, :], n), scalar=-1.0 / 64, in1=v6(w1, n),
                                                         op0=ALU.mult, op1=ALU.add), reads=[B_s6[0], B_w1], writes=[B_w2])
            S.op('pool', lambda e: e.tensor_tensor(out=w3[0:n, :], in0=w2[0:n, :], in1=w2[0:n, :], op=ALU.mult), reads=[B_w2], writes=[B_w3])
            S.op('dve', lambda e: e.tensor_reduce(out=s6[0:n, 1, :], in_=v6(w3, n), axis=AX.X, op=ALU.add), reads=[B_w3], writes=[B_s6[1]])
            rstd_from_ss(S, s6[0:n, 1, :], s6[0:n, 2, :], B_s6[1], B_s6[2], 64, GN_EPS)
            S.op('dve', lambda e: e.tensor_tensor(out=v6(w2, n), in0=v6(w2, n), in1=bc6(s6[0:n, 2, :], n), op=ALU.mult),
                 reads=[B_s6[2], B_w2], writes=[B_w2])
            S.op('pool', lambda e: e.tensor_tensor(out=w2[0:n, :], in0=w2[0:n, :], in1=vec_bc[0:n, 1, :], op=ALU.mult), reads=[B_vec], writes=[B_w2])
            S.op('pool', lambda e: e.tensor_tensor(out=w3[0:n, :], in0=r_, in1=k_, op=ALU.mult), reads=[Bi, B_w2], writes=[B_w3])
            S.op('pool', lambda e: e.tensor_tensor(out=w2[0:n, :], in0=w2[0:n, :], in1=vec_bc[0:n, 2, :], op=ALU.add), reads=[B_vec, B_w3], writes=[B_w2])
            S.op('pool', lambda e: e.tensor_tensor(out=w3[0:n, :], in0=w3[0:n, :], in1=vec_bc[0:n, 0, :], op=ALU.mult), reads=[B_vec, B_w2], writes=[B_w3])
            S.op('dve', lambda e: e.tensor_reduce(out=s6[0:n, 3, :], in_=v6(w3, n), axis=AX.X, op=ALU.add), reads=[B_w3], writes=[B_s6[3]])
            S.op('dve', lambda e: e.tensor_tensor(out=v6(w1, n), in0=v_.rearrange("p (h e) -> p h e", h=6), in1=bc6(s6[0:n, 3, :], n), op=ALU.mult),
                 reads=[B_s6[3], Bi], writes=[B_w1])
            S.op('pool', lambda e: e.tensor_tensor(out=w1[0:n, :], in0=w1[0:n, :], in1=w2[0:n, :], op=ALU.add), reads=[B_w2], writes=[B_w1])
            S.op('dve', lambda e: e.tensor_tensor(out=xn[0:n, j, 640:1024], in0=P[0:n, 0, 0:384], in1=w1[0:n, :], op=ALU.mult),
                 reads=[C.PB[0], B_w1], writes=[B_xn[j]])

        load_group(0)
        if mode == 'outproj':
            load_tile_inputs(0)
        for gi, grp in enumerate(groups):
            xb = xg[gi % 2]
            Bx = B_xg[gi % 2]
            ntok = sum(t[1] for t in grp)
            if gi + 1 < len(groups):
                load_group(gi + 1)
            for j, (t0, n, cond) in enumerate(grp):
                ti = gi * 2 + j
                if mode == 'outproj':
                    if ti + 1 < len(tiles):
                        load_tile_inputs(ti + 1)
                    readout_tile(ti, j)
                else:
                    ss = stat[0:n, j:j + 1]
                    rs = stat[0:n, 2 + j:3 + j]
                    S.op('act', lambda e, n=n, j=j, ss=ss, xb=xb: e.activation(out=junk[0:n, :], in_=xb[0:n, j, :], func=AF.Square, accum_out=ss),
                         reads=[Bx[j]], writes=[B_junk, B_stat[j]])
                    rstd_from_ss(S, ss, rs, B_stat[j], B_stat[2 + j], D, EPS)
                    S.op('act', lambda e, n=n, j=j, rs=rs, xb=xb: e.activation(out=xn[0:n, j, :], in_=xb[0:n, j, :], func=AF.Copy, scale=rs),
                         reads=[Bx[j], B_stat[2 + j]], writes=[B_xn[j]])
                for kc in range(8):
                    S.op('pe', lambda e, n=n, j=j, kc=kc: e.transpose(out=PBF[:, kc * 128:kc * 128 + n], in_=xn[0:n, j, kc * 128:(kc + 1) * 128],
                                                                      identity=C.ident_b[0:n, 0:n]),
                         reads=[B_xn[j], C.B_ident], writes=[C.PB[6]])
                pv = PBF.rearrange("p (k t) -> p k t", k=8)[:, :, 0:n]
                if mode == 'outproj':
                    S.op('act', lambda e, n=n, j=j, pv=pv: e.activation(func=AF.Copy, out=hT[:, :, j * 128:j * 128 + n], in_=pv), reads=[C.PB[6]], writes=[B_hT])
                else:
                    S.op('dve', lambda e, n=n, j=j, cond=cond, pv=pv: e.tensor_tensor(
                        out=hT[:, :, j * 128:j * 128 + n], in0=pv, in1=G1T[:, :, cond:cond + 1].to_broadcast([128, 8, n]), op=ALU.mult),
                        reads=[C.PB[6], B_modT], writes=[B_hT])
                    S.op('pool', lambda e, n=n, j=j, cond=cond: e.tensor_tensor(
                        out=hT[:, :, j * 128:j * 128 + n], in0=hT[:, :, j * 128:j * 128 + n],
                        in1=modT[0][:, :, cond:cond + 1].to_broadcast([128, 8, n]), op=ALU.add),
                        reads=[B_modT], writes=[B_hT])
            if mode == 'ffn':
                for fc in range(NF):
                    fw = 128 if fc < NF - 1 else 64
                    bg = fc % 2
                    for which in range(2):
                        col0 = which * FF + fc * 128
                        bank = bg * 2 + which
                        for k in range(8):
                            S.op('pe', lambda e, fw=fw, col0=col0, bank=bank, k=k, ntok=ntok: e.matmul(
                                P[0:fw, bank, 0:ntok], lhsT=wi_sb[:, k, col0:col0 + fw], rhs=hT[:, k, 0:ntok], start=(k == 0), stop=(k == 7)),
                                reads=[B_w, B_hT], writes=[C.PB[bank]])
                    S.op('act', lambda e, fw=fw, bg=bg, ntok=ntok: e.activation(out=sg[bg][0:fw, 0:ntok], in_=P[0:fw, bg * 2, 0:ntok], func=AF.Silu),
                         reads=[C.PB[bg * 2]], writes=[B_sg[bg]])
                    S.op('dve', lambda e, fw=fw, bg=bg, fc=fc, ntok=ntok: e.tensor_tensor(out=hid[0:fw, fc, 0:ntok], in0=P[0:fw, bg * 2 + 1, 0:ntok],
                                                                                       in1=sg[bg][0:fw, 0:ntok], op=ALU.mult),
                         reads=[C.PB[bg * 2 + 1], B_sg[bg]], writes=[B_hid])
            if mode == 'inproj':
                for j, (t0, n, cond) in enumerate(grp):
                    zb = zt[tcount[0] % 2]
                    Bz = B_zt[tcount[0] % 2]
                    tcount[0] += 1
                    for ci, c0 in enumerate(range(0, INW, 512)):
                        cw = min(512, INW - c0)
                        for k in range(8):
                            S.op('pe', lambda e, n=n, j=j, ci=ci, c0=c0, cw=cw, k=k: e.matmul(
                                P[0:n, ci, 0:cw], lhsT=hT[:, k, j * 128:j * 128 + n], rhs=w_sb[:, k, c0:c0 + cw], start=(k == 0), stop=(k == 7)),
                                reads=[B_w, B_hT], writes=[C.PB[ci]])
                        if ci % 2 == 0:
                            S.op('act', lambda e, n=n, ci=ci, c0=c0, cw=cw, zb=zb: e.activation(func=AF.Copy, out=zb[0:n, c0:c0 + cw], in_=P[0:n, ci, 0:cw]),
                                 reads=[C.PB[ci]], writes=[Bz])
                        else:
                            S.op('dve', lambda e, n=n, ci=ci, c0=c0, cw=cw, zb=zb: e.tensor_copy(out=zb[0:n, c0:c0 + cw], in_=P[0:n, ci, 0:cw]),
                                 reads=[C.PB[ci]], writes=[Bz])
                    S.dma('sp', z_d[t0:t0 + n, :], zb[0:n, :], reads=[Bz])
                continue
            for j, (t0, n, cond) in enumerate(grp):
                for hh in range(2):
                    bank = 4 + hh
                    if mode == 'ffn':
                        for fc in range(NF):
                            fw = 128 if fc < NF - 1 else 64
                            S.op('pe', lambda e, fw=fw, fc=fc, bank=bank, hh=hh, j=j, n=n: e.matmul(
                                P[0:n, bank, :], lhsT=hid[0:fw, fc, j * 128:j * 128 + n], rhs=wo_sb[0:fw, fc, hh * 512:(hh + 1) * 512],
                                start=(fc == 0), stop=(fc == NF - 1)),
                                reads=[B_hid, B_wo], writes=[C.PB[bank]])
                    else:
                        for k in range(8):
                            S.op('pe', lambda e, k=k, bank=bank, hh=hh, j=j, n=n: e.matmul(
                                P[0:n, bank, :], lhsT=hT[:, k, j * 128:j * 128 + n], rhs=w_sb[:, k, hh * 512:(hh + 1) * 512],
                                start=(k == 0), stop=(k == 7)),
                                reads=[B_hT, B_w], writes=[C.PB[bank]])
                ss = stat[0:n, 4 + j:5 + j]
                rs = stat[0:n, 6 + j:7 + j]
                yv = P[0:n, 4:6, :].rearrange("p a b -> p (a b)")
                S.op('act', lambda e, n=n, ss=ss, yv=yv: e.activation(out=junk[0:n, :], in_=yv, func=AF.Square, accum_out=ss),
                     reads=[C.PB[4], C.PB[5]], writes=[B_junk, B_stat[j]])
                rstd_from_ss(S, ss, rs, B_stat[j], B_stat[2 + j], D, EPS)
                S.op('dve', lambda e, n=n, rs=rs, yv=yv, cond=cond: e.scalar_tensor_tensor(
                    out=tmp[0:n, :], in0=yv, scalar=rs, in1=GG[0:n, cond, :], op0=ALU.mult, op1=ALU.mult),
                    reads=[C.PB[4], C.PB[5], B_stat[2 + j], B_GG], writes=[B_tmp])
                S.op('pool', lambda e, n=n, j=j, xb=xb: e.tensor_tensor(out=xb[0:n, j, :], in0=xb[0:n, j, :], in1=tmp[0:n, :], op=ALU.add),
                     reads=[B_tmp], writes=[Bx[j]])
                S.dma('sp', y[t0:t0 + n, :], xb[0:n, j, :], reads=[Bx[j]])
        S.emit()
    return nc


_cache = {}


def _get(name, builder):
    if name not in _cache:
        _cache[name] = builder()
    return _cache[name]


def _common_maps(c, c_ctx, mod_w, mod_b):
    ident = np.eye(128, dtype=np.float32)
    sel = np.zeros((2, 2, 128), np.float32)
    sel[0, 0, :] = 1.0
    sel[1, 1, :] = 1.0
    maps = []
    mwc = np.ascontiguousarray(mod_w)
    mbc = np.ascontiguousarray(np.stack([mod_b, mod_b], 0))
    for core in range(NCORE):
        b = core // 4
        cond = np.stack([c[b], c_ctx], 0)
        condT = np.ascontiguousarray(cond.reshape(2, 8, 128).transpose(2, 1, 0))
        maps.append(dict(condT=condT, mw=mwc, mb=mbc, ident=ident, sel=sel))
    return maps


def shard_tokens(xl, xc):
    xs = []
    for core in range(NCORE):
        b, q = core // 4, core % 4
        xs.append(np.ascontiguousarray(np.concatenate([xl[b, q * TL:(q + 1) * TL], xc[b, q * TC:(q + 1) * TC]], 0)))
    return xs


def unshard_tokens(ys, width=D):
    xl = np.zeros((2, 8192, width), ys[0].dtype)
    xc = np.zeros((2, 256, width), ys[0].dtype)
    for core in range(NCORE):
        b, q = core // 4, core % 4
        xl[b, q * TL:(q + 1) * TL] = ys[core][:TL]
        xc[b, q * TC:(q + 1) * TC] = ys[core][TL:]
    return xl, xc


def run_ffn(xs, c, c_ctx, mod_w, mod_b, g_pre, g_post, wi, wo):
    nc = _get('ffn', lambda: build_dense('ffn'))
    maps = _common_maps(c, c_ctx, mod_w, mod_b)
    gpreT = np.ascontiguousarray(g_pre.reshape(8, 128).T)
    gpost = np.ascontiguousarray(g_post.reshape(1, D))
    wi = np.ascontiguousarray(wi)
    wo = np.ascontiguousarray(wo)
    for core in range(NCORE):
        maps[core].update(x=xs[core], gpreT=gpreT, gpost=gpost, wi=wi, wo=wo)
    res = run_bass_kernel_spmd(nc, maps, core_ids=list(range(NCORE)))
    return [r['y'] for r in res.results]


def run_inproj(xs, c, c_ctx, mod_w, mod_b, g_pre, w_in):
    nc = _get('inproj', lambda: build_dense('inproj'))
    maps = _common_maps(c, c_ctx, mod_w, mod_b)
    gpreT = np.ascontiguousarray(g_pre.reshape(8, 128).T)
    w_in = np.ascontiguousarray(w_in)
    for core in range(NCORE):
        maps[core].update(x=xs[core], gpreT=gpreT, w=w_in)
    res = run_bass_kernel_spmd(nc, maps, core_ids=list(range(NCORE)))
    return [r['z'] for r in res.results]


def run_outproj(xs, c, c_ctx, mod_w, mod_b, g_post, w_out, fa, yf, yb, rkv, zg, g2, vecs):
    nc = _get('outproj', lambda: build_dense('outproj'))
    maps = _common_maps(c, c_ctx, mod_w, mod_b)
    gpost = np.ascontiguousarray(g_post.reshape(1, D))
    w_out = np.ascontiguousarray(w_out)
    for core in range(NCORE):
        maps[core].update(x=xs[core], gpost=gpost, w=w_out, fa=fa[core], yf=yf[core], yb=yb[core], rkv=rkv[core], zg=zg[core],
                          g2=np.ascontiguousarray(g2), vecs=np.ascontiguousarray(vecs))
    res = run_bass_kernel_spmd(nc, maps, core_ids=list(range(NCORE)))
    return [r['y'] for r in res.results]


def build_prep(ntiles_l=16, has_ctx=True):
    nc = bass.Bass("TRN2", target_bir_lowering=False)
    T = ntiles_l * 128 + (TC if has_ctx else 0)
    zt_d = nc.dram_tensor("zt", [T, INW], F32, kind="ExternalInput").ap()
    zp_d = nc.dram_tensor("zp", [T, 1152], F32, kind="ExternalInput").ap()
    zn_d = nc.dram_tensor("zn", [T, 1152], F32, kind="ExternalInput").ap()
    tab_d = nc.dram_tensor("tab", [T, 64], F32, kind="ExternalInput").ap()
    cw_d = nc.dram_tensor("cw", [1, 3 * 1152], F32, kind="ExternalInput").ap()
    vec_d = nc.dram_tensor("pvecs", [1, 6 * 384], F32, kind="ExternalInput").ap()
    l2_d = nc.dram_tensor("l2", [128, 2, 384], F32, kind="ExternalInput").ap()
    ident_d = nc.dram_tensor("ident", [128, 128], F32, kind="ExternalInput").ap()
    qk_o = nc.dram_tensor("qk", [T, 512], BF16, kind="ExternalOutput").ap()
    v_o = nc.dram_tensor("v", [T, 128], BF16, kind="ExternalOutput").ap()
    rkv_o = nc.dram_tensor("rkv", [T, 1152], F32, kind="ExternalOutput").ap()
    scan_o = nc.dram_tensor("scan", [T, 6, 384], BF16, kind="ExternalOutput").ap()
    dec_o = nc.dram_tensor("dec", [T, 2, 384], F32, kind="ExternalOutput").ap()
    with ExitStack() as st:
        S = Sched(nc, st)
        C = setup_common(S, nc)
        P = C.P
        load_ident(S, C, ident_d)
        cw = S.sb("cw_sb", [128, 3, 1152], F32)
        vec = S.sb("vec_sb", [128, 6, 384], F32)
        l2f = S.sb("l2f", [128, 2, 384], F32)
        l2 = S.sb("l2b", [128, 2, 384], BF16)
        B_c = S.buf('consts')
        S.dma('sp', cw[:].rearrange("p a b -> p (a b)"), mkap(cw_d, 0, [(0, 128), (1, 3 * 1152)]), writes=[B_c])
        S.dma('sp', vec[:].rearrange("p a b -> p (a b)"), mkap(vec_d, 0, [(0, 128), (1, 6 * 384)]), writes=[B_c])
        S.dma('sp', l2f[:], l2_d, writes=[B_c])
        S.op('dve', lambda e: e.tensor_copy(out=l2[:], in_=l2f[:]), reads=[B_c], writes=[B_c])
        tiles = [(i * 128, 128) for i in range(ntiles_l)]
        if has_ctx:
            tiles.append((ntiles_l * 128, TC))
        sets = []
        for i in range(2):
            d = dict(zt=S.sb(f"zt{i}", [128, INW], F32), zp=S.sb(f"zp{i}", [128, 1152], F32), zn=S.sb(f"zn{i}", [128, 1152], F32),
                     tab=S.sb(f"tab{i}", [128, 64], F32), B=S.buf(f'in{i}'),
                     qk=S.sb(f"qk{i}", [128, 512], BF16), v=S.sb(f"v{i}", [128, 128], BF16), rkv=S.sb(f"rkv{i}", [128, 1152], F32),
                     scan=S.sb(f"scan{i}", [128, 6, 384], BF16), dec=S.sb(f"dec{i}", [128, 2, 384], F32),
                     Bqk=S.buf(f'qk{i}'), Bv=S.buf(f'v{i}'), Brkv=S.buf(f'rkv{i}'), Bscan=S.buf(f'scan{i}'), Bdec=S.buf(f'dec{i}'))
            sets.append(d)
        t1 = S.sb("t1", [128, 8, 32], F32)
        t2 = S.sb("t2", [128, 8, 32], F32)
        t3 = S.sb("t3", [128, 8, 32], F32)
        t4 = S.sb("t4", [128, 8, 32], F32)
        Bt = [S.buf(f't{i}') for i in range(4)]
        ca = S.sb("ca", [128, 1152], F32)
        cb = S.sb("cb", [128, 1152], F32)
        Bca, Bcb = S.buf('ca'), S.buf('cb')
        lb = S.sb("lb", [128, 2, 128], BF16)
        lT = S.sb("lT", [128, 2, 128], BF16)
        Blb, BlT = S.buf('lb'), S.buf('lT')
        PBF = P[:, 6, :].bitcast(BF16)
        asig = S.sb("asig", [128, 2, 384], F32)
        Basig = S.buf('asig')
        wt = S.sb("wt", [128, 2, 384], F32)
        Bwt = S.buf('wt')
        kk0 = S.sb("kk0", [128, 384], F32)
        ksq = S.sb("ksq", [128, 384], F32)
        kkf = S.sb("kkf", [128, 384], F32)
        Bkk0, Bksq, Bkkf = S.buf('kk0'), S.buf('ksq'), S.buf('kkf')
        s6 = S.sb("s6", [128, 2, 6], F32)
        Bs6 = [S.buf('s60'), S.buf('s61')]
        tk = S.sb("tk", [128, 2, 384], F32)
        Btk = S.buf('tk')

        def load(ti):
            t0, n = tiles[ti]
            d = sets[ti % 2]
            S.dma('sp', d['zt'][0:n, :], zt_d[t0:t0 + n, :], writes=[d['B']])
            S.dma('sp', d['zp'][0:n, :], zp_d[t0:t0 + n, :], writes=[d['B']])
            S.dma('sp', d['zn'][0:n, :], zn_d[t0:t0 + n, :], writes=[d['B']])
            S.dma('sp', d['tab'][0:n, :], tab_d[t0:t0 + n, :], writes=[d['B']])

        load(0)
        for ti, (t0, n) in enumerate(tiles):
            if ti + 1 < len(tiles):
                load(ti + 1)
            d = sets[ti % 2]
            Bi = d['B']
            zt = d['zt']
            qk = zt[0:n, 256:768].rearrange("p (h two e) -> p h two e", h=8, two=2)
            x1 = qk[:, :, 0, :]
            x2 = qk[:, :, 1, :]
            cosb = d['tab'][0:n, 0:32].unsqueeze(1).to_broadcast([n, 8, 32])
            sinb = d['tab'][0:n, 32:64].unsqueeze(1).to_broadcast([n, 8, 32])
            oq = d['qk'][0:n, :].rearrange("p (h two e) -> p h two e", h=8, two=2)
            S.op('dve', lambda e, x1=x1, cosb=cosb, n=n: e.tensor_tensor(out=t1[0:n], in0=x1, in1=cosb, op=ALU.mult), reads=[Bi], writes=[Bt[0]])
            S.op('pool', lambda e, x2=x2, sinb=sinb, n=n: e.tensor_tensor(out=t2[0:n], in0=x2, in1=sinb, op=ALU.mult), reads=[Bi], writes=[Bt[1]])
            S.op('dve', lambda e, x2=x2, cosb=cosb, n=n: e.tensor_tensor(out=t3[0:n], in0=x2, in1=cosb, op=ALU.mult), reads=[Bi], writes=[Bt[2]])
            S.op('pool', lambda e, x1=x1, sinb=sinb, n=n: e.tensor_tensor(out=t4[0:n], in0=x1, in1=sinb, op=ALU.mult), reads=[Bi], writes=[Bt[3]])
            S.op('dve', lambda e, oq=oq, n=n: e.tensor_tensor(out=oq[:, :, 0, :], in0=t1[0:n], in1=t2[0:n], op=ALU.subtract),
                 reads=[Bt[0], Bt[1]], writes=[d['Bqk']])
            S.op('pool', lambda e, oq=oq, n=n: e.tensor_tensor(out=oq[:, :, 1, :], in0=t3[0:n], in1=t4[0:n], op=ALU.add),
                 reads=[Bt[2], Bt[3]], writes=[d['Bqk']])
            S.dma('sp', qk_o[t0:t0 + n, :], d['qk'][0:n, :], reads=[d['Bqk']])
            S.op('act', lambda e, zt=zt, d=d, n=n: e.activation(func=AF.Copy, out=d['v'][0:n, :], in_=zt[0:n, 768:896]), reads=[Bi], writes=[d['Bv']])
            S.dma('sp', v_o[t0:t0 + n, :], d['v'][0:n, :], reads=[d['Bv']])
            rkv = d['rkv']
            S.op('pool', lambda e, d=d, n=n: e.tensor_tensor(out=ca[0:n, :], in0=d['zp'][0:n, :], in1=cw[0:n, 0, :], op=ALU.mult), reads=[Bi, B_c], writes=[Bca])
            S.op('dve', lambda e, zt=zt, n=n: e.tensor_tensor(out=cb[0:n, :], in0=zt[0:n, 896:2048], in1=cw[0:n, 1, :], op=ALU.mult), reads=[Bi, B_c], writes=[Bcb])
            S.op('pool', lambda e, d=d, n=n, rkv=rkv: e.tensor_tensor(out=rkv[0:n, :], in0=d['zn'][0:n, :], in1=cw[0:n, 2, :], op=ALU.mult), reads=[Bi, B_c], writes=[d['Brkv']])
            S.op('dve', lambda e, n=n: e.tensor_tensor(out=ca[0:n, :], in0=ca[0:n, :], in1=cb[0:n, :], op=ALU.add), reads=[Bcb], writes=[Bca])
            S.op('pool', lambda e, n=n, rkv=rkv: e.tensor_tensor(out=rkv[0:n, :], in0=rkv[0:n, :], in1=ca[0:n, :], op=ALU.add), reads=[Bca], writes=[d['Brkv']])
            S.dma('sp', rkv_o[t0:t0 + n, :], rkv[0:n, :], reads=[d['Brkv']])
            r_ = rkv[0:n, 0:384]
            k_ = rkv[0:n, 384:768]
            S.op('act', lambda e, zt=zt, n=n: e.activation(out=lb[0:n, 0, :], in_=zt[0:n, 2048:2176], func=AF.Tanh), reads=[Bi], writes=[Blb])
            S.op('act', lambda e, zt=zt, n=n: e.activation(func=AF.Copy, out=lb[0:n, 1, :], in_=zt[0:n, 2176:2304]), reads=[Bi], writes=[Blb])
            for q in range(2):
                S.op('pe', lambda e, q=q, n=n: e.transpose(out=PBF[:, q * 128:q * 128 + n], in_=lb[0:n, q, :], identity=C.ident_b[0:n, 0:n]),
                     reads=[Blb, C.B_ident], writes=[C.PB[6]])
            S.op('act', lambda e, n=n: e.activation(func=AF.Copy, out=lT[:, :, 0:n], in_=PBF[:, 0:256].rearrange("p (q t) -> p q t", q=2)[:, :, 0:n]),
                 reads=[C.PB[6]], writes=[BlT])
            for q in range(2):
                for z in range(2):
                    bank = q * 2 + z
                    S.op('pe', lambda e, q=q, z=z, bank=bank, n=n: e.matmul(P[0:n, bank, 0:384], lhsT=lT[z * 64:(z + 1) * 64, q, 0:n],
                                                                          rhs=l2[z * 64:(z + 1) * 64, q, :], start=True, stop=True),
                         reads=[BlT, B_c], writes=[C.PB[bank]])
            for z in range(2):
                S.op('dve', lambda e, z=z, n=n: e.tensor_tensor(out=wt[0:n, z, :], in0=P[0:n, z, 0:384], in1=vec[0:n, z, :], op=ALU.add),
                     reads=[C.PB[z], B_c], writes=[Bwt])
                S.op('dve', lambda e, z=z, n=n: e.tensor_tensor(out=asig[0:n, z, :], in0=P[0:n, 2 + z, 0:384], in1=vec[0:n, 2 + z, :], op=ALU.add),
                     reads=[C.PB[2 + z], B_c], writes=[Basig])
            S.op('act', lambda e, n=n: e.activation(out=wt[0:n], in_=wt[0:n], func=AF.Sigmoid), reads=[Bwt], writes=[Bwt])
            S.op('act', lambda e, n=n: e.activation(out=asig[0:n], in_=asig[0:n], func=AF.Sigmoid), reads=[Basig], writes=[Basig])
            S.op('act', lambda e, n=n, d=d: e.activation(out=d['dec'][0:n], in_=wt[0:n], func=AF.Copy, scale=-0.6065306597126334),
                 reads=[Bwt], writes=[d['Bdec']])
            S.dma('sp', dec_o[t0:t0 + n], d['dec'][0:n], reads=[d['Bdec']])
            S.op('pool', lambda e, n=n, k_=k_: e.tensor_tensor(out=kk0[0:n, :], in0=k_, in1=vec[0:n, 4, :], op=ALU.mult), reads=[d['Brkv'], B_c], writes=[Bkk0])
            S.op('pool', lambda e, n=n: e.tensor_tensor(out=ksq[0:n, :], in0=kk0[0:n, :], in1=kk0[0:n, :], op=ALU.mult), reads=[Bkk0], writes=[Bksq])
            S.op('dve', lambda e, n=n: e.tensor_reduce(out=s6[0:n, 0, :], in_=ksq[0:n, :].rearrange("p (h e) -> p h e", h=6), axis=AX.X, op=ALU.add),
                 reads=[Bksq], writes=[Bs6[0]])
            S.op('dve', lambda e, n=n: e.tensor_scalar(out=s6[0:n, 1, :], in0=s6[0:n, 0, :], scalar1=1e-24, scalar2=None, op0=ALU.max),
                 reads=[Bs6[0]], writes=[Bs6[1]])
            S.op('act', lambda e, n=n: e.activation(out=s6[0:n, 1, :], in_=s6[0:n, 1, :], func=AF.Sqrt), reads=[Bs6[1]], writes=[Bs6[1]])
            S.op('dve', lambda e, n=n: e.reciprocal(out=s6[0:n, 1, :], in_=s6[0:n, 1, :]), reads=[Bs6[1]], writes=[Bs6[1]])
            S.op('dve', lambda e, n=n: e.tensor_tensor(out=kkf[0:n, :].rearrange("p (h e) -> p h e", h=6), in0=kk0[0:n, :].rearrange("p (h e) -> p h e", h=6),
                                                      in1=s6[0:n, 1, :].unsqueeze(2).to_broadcast([n, 6, 64]), op=ALU.mult),
                 reads=[Bkk0, Bs6[1]], writes=[Bkkf])
            sc = d['scan']
            Bsc = d['Bscan']
            S.op('act', lambda e, n=n, sc=sc: e.mul(out=sc[0:n, 0, :], in_=kkf[0:n, :], mul=-1.0), reads=[Bkkf], writes=[Bsc])
            S.op('act', lambda e, n=n, sc=sc, r_=r_: e.activation(func=AF.Copy, out=sc[0:n, 1, :], in_=r_), reads=[d['Brkv']], writes=[Bsc])
            for z in range(2):
                S.op('pool', lambda e, n=n, sc=sc, z=z: e.tensor_tensor(out=sc[0:n, 2 + z, :], in0=kkf[0:n, :], in1=asig[0:n, z, :], op=ALU.mult),
                     reads=[Bkkf, Basig], writes=[Bsc])
                S.op('dve', lambda e, n=n, z=z: e.scalar_tensor_tensor(out=tk[0:n, z, :], in0=asig[0:n, z, :], scalar=-1.0, in1=vec[0:n, 5, :],
                                                                      op0=ALU.add, op1=ALU.mult), reads=[Basig, B_c], writes=[Btk])
            for z in range(2):
                S.op('dve', lambda e, n=n, z=z, sc=sc, k_=k_: e.scalar_tensor_tensor(out=sc[0:n, 4 + z, :], in0=tk[0:n, z, :], scalar=1.0, in1=k_,
                                                                                    op0=ALU.add, op1=ALU.mult), reads=[Btk, d['Brkv']], writes=[Bsc])
            S.dma('sp', scan_o[t0:t0 + n], sc[0:n], reads=[Bsc])
        S.emit()
    return nc


def rope_tables():
    rows = 8192 // 64
    t = np.arange(8192)
    row = (t // 64).astype(np.float32)
    col = (t % 64).astype(np.float32)
    inv = (10000.0 ** (-np.arange(16, dtype=np.float32) / 16)).astype(np.float32)
    ang = np.concatenate([row[:, None] * inv, col[:, None] * inv], -1).astype(np.float32)
    return np.cos(ang).astype(np.float32), np.sin(ang).astype(np.float32)


def run_prep(zl, zc, conv_w, w0, w2, a0, a2, k_k, k_a):
    nc = _get('prep', build_prep)
    cos, sin = rope_tables()
    ident = np.eye(128, dtype=np.float32)
    cwf = np.ascontiguousarray(conv_w.reshape(1, 3 * 1152))
    pvecs = np.ascontiguousarray(np.concatenate([w0[0], w0[1], a0[0], a0[1], k_k, k_a]).reshape(1, 6 * 384))
    l2 = np.ascontiguousarray(np.stack([w2.reshape(128, 384), a2.reshape(128, 384)], 1))

    def shift(zz, d):
        o = np.zeros_like(zz)
        if d == 1:
            o[1:] = zz[:-1]
        else:
            o[:-1] = zz[1:]
        return o
    maps = []
    for core in range(NCORE):
        b, q = core // 4, core % 4
        rl = zl[b][:, 896:2048]
        rc = zc[b][:, 896:2048]
        zt = np.concatenate([zl[b, q * TL:(q + 1) * TL], zc[b, q * TC:(q + 1) * TC]], 0)
        zp = np.concatenate([shift(rl, 1)[q * TL:(q + 1) * TL], shift(rc, 1)[q * TC:(q + 1) * TC]], 0)
        zn = np.concatenate([shift(rl, -1)[q * TL:(q + 1) * TL], shift(rc, -1)[q * TC:(q + 1) * TC]], 0)
        tab = np.zeros((TL + TC, 64), np.float32)
        tab[:TL, 0:32] = cos[q * TL:(q + 1) * TL]
        tab[:TL, 32:64] = sin[q * TL:(q + 1) * TL]
        tab[TL:, 0:32] = 1.0
        maps.append(dict(zt=np.ascontiguousarray(zt), zp=np.ascontiguousarray(zp), zn=np.ascontiguousarray(zn), tab=tab,
                         cw=cwf, pvecs=pvecs, l2=l2, ident=ident))
    res = run_bass_kernel_spmd(nc, maps, core_ids=list(range(NCORE)))
    out = {}
    for key, wdt in [('qk', 512), ('v', 128), ('rkv', 1152)]:
        out[key] = unshard_tokens([r[key] for r in res.results], wdt)
    out['scan'] = unshard_tokens([r['scan'].reshape(TL + TC, 6 * 384) for r in res.results], 6 * 384)
    out['dec'] = unshard_tokens([r['dec'].reshape(TL + TC, 2 * 384) for r in res.results], 2 * 384)
    return out


CH = 64
NSTEP = 8448


def build_scan2(nstep=NSTEP):
    nc = bass.Bass("TRN2", target_bir_lowering=False)
    nch = nstep // CH
    tm_d = nc.dram_tensor("tm", [3, nch, CH, 4, 64], F32, kind="ExternalInput").ap()
    fm_d = nc.dram_tensor("fm", [3, nch, 64, 4, CH], F32, kind="ExternalInput").ap()
    lw_d = nc.dram_tensor("lw", [3, nch, 64, 2, 64], F32, kind="ExternalInput").ap()
    cst_d = nc.dram_tensor("cst", [64, 8, 64], F32, kind="ExternalInput").ap()
    y_o = nc.dram_tensor("y", [3, nch, 64, CH], F32, kind="ExternalOutput").ap()
    NSET = 9
    with ExitStack() as st:
        S = Sched(nc, st)
        P = st.enter_context(nc.psum_tensor("P", [128, 8, 512], F32))
        PB = [S.buf(f'pb{i}') for i in range(8)]
        cst = S.sb("cst_sb", [64, 8, 64], F32)
        B_c = S.buf('cst')
        S.dma('sp', cst[:], cst_d, writes=[B_c])
        tri = cst[:, 0, :]
        maskM = cst[:, 1:6, :].rearrange("p a b -> p (a b)")
        ident = cst[:, 6, :]
        sets = []
        for i in range(NSET):
            d = dict(
                tm=S.sb(f"tm{i}", [64, 4, 64], F32), fm=S.sb(f"fm{i}", [64, 4, 64], F32), lw=S.sb(f"lw{i}", [64, 2, 64], F32),
                Ep=S.sb(f"Ep{i}", [64, 128], F32), En=S.sb(f"En{i}", [64, 128], F32), Ev=S.sb(f"Ev{i}", [64, 128], F32),
                AR=S.sb(f"AR{i}", [64, 128], F32), BKf=S.sb(f"BKf{i}", [64, 2, 64], F32), BKt=S.sb(f"BKt{i}", [64, 2, 64], F32),
                M=S.sb(f"M{i}", [64, 320], F32), X=[S.sb(f"X{i}_{q}", [64, 128], F32) for q in range(2)],
                NN=[S.sb(f"NN{i}_{q}", [64, 128], F32) for q in range(2)],
                Rhat=S.sb(f"Rhat{i}", [64, 64], F32), Y0=S.sb(f"Y0{i}", [64, 64], F32), G0=S.sb(f"G0{i}", [64, 64], F32),
                HT=S.sb(f"HT{i}", [64, 64], F32), ysb=S.sb(f"ysb{i}", [64, 64], F32),
            )
            for k in ['in', 'Ep', 'En', 'Ev', 'AR', 'BKf', 'BKt', 'M', 'X0', 'X1', 'NN0', 'NN1', 'Rhat', 'Y0', 'G0', 'HT', 'ysb']:
                d['B' + k] = S.buf(f'{k}{i}')
            sets.append(d)
        ST = [[S.sb(f"ST{it}_{q}", [64, 64], F32) for q in range(2)] for it in range(3)]
        B_ST = [[S.buf(f'ST{it}_{q}') for q in range(2)] for it in range(3)]
        for it in range(3):
            S.op('pool', lambda e, it=it: e.memset(ST[it][0][:], 0.0), writes=[B_ST[it][0]])

        def load(c, it, d):
            S.dma('sp', d['tm'][:], tm_d[it, c], writes=[d['Bin']])
            S.dma('sp', d['fm'][:], fm_d[it, c], writes=[d['Bin']])
            S.dma('sp', d['lw'][:], lw_d[it, c], writes=[d['Bin']])

        def mm(out, lhsT, rhs, reads, writes, start=True, stop=True):
            S.op('pe', lambda e: e.matmul(out, lhsT=lhsT, rhs=rhs, start=start, stop=stop), reads=reads, writes=writes)

        def act(out, in_, func, reads, writes, scale=None):
            if scale is None:
                S.op('act', lambda e: e.activation(out=out, in_=in_, func=func), reads=reads, writes=writes)
            else:
                S.op('act', lambda e: e.activation(out=out, in_=in_, func=func, scale=scale), reads=reads, writes=writes)

        def tt(eng, out, in0, in1, op, reads, writes):
            S.op(eng, lambda e: e.tensor_tensor(out=out, in0=in0, in1=in1, op=op), reads=reads, writes=writes)

        def mm(out, lhsT, rhs, reads, writes, start=True, stop=True):
            S.op('pe', lambda e: e.matmul(out, lhsT=lhsT, rhs=rhs, start=start, stop=stop), reads=reads, writes=writes)

        def act(out, in_, func, reads, writes, scale=None):
            if scale is None:
                S.op('act', lambda e: e.activation(out=out, in_=in_, func=func), reads=reads, writes=writes)
            else:
                S.op('act', lambda e: e.activation(out=out, in_=in_, func=func, scale=scale), reads=reads, writes=writes)

        def tt(eng, out, in0, in1, op, reads, writes):
            S.op(eng, lambda e: e.tensor_tensor(out=out, in0=in0, in1=in1, op=op), reads=reads, writes=writes)

        def item_prog(n, c, it, d):
            bk0 = n % 3
            bk1 = bk0
            bk2 = 3 + (n % 3)
            bk3 = 6 + (n % 2)
            Bin = d['Bin']
            tm, fm, lw = d['tm'], d['fm'], d['lw']
            Lps = P[0:64, bk0, 0:128]
            BE = d['BEp']
            mm(P[0:64, bk0, 0:64], tri, lw[:, 0, :], [Bin, B_c], [PB[bk0]])
            mm(P[0:64, bk0, 64:128], lw[:, 0, :], tri, [Bin, B_c], [PB[bk0]])
            yield
            act(d['Ep'][:], Lps, AF.Exp, [PB[bk0]], [BE])
            act(d['En'][:], Lps, AF.Exp, [PB[bk0]], [BE], scale=-1.0)
            tt('dve', d['Ev'][:], Lps, lw[:].rearrange("p a b -> p (a b)"), ALU.subtract, [PB[bk0], Bin], [BE])
            yield
            act(d['Ev'][:], d['Ev'][:], AF.Exp, [BE], [BE])
            yield
            X0 = d['X'][0]
            tt('pool', X0[:, 0:64], tm[:, 0, :], d['Ev'][:, 0:64], ALU.mult, [Bin, BE], [d['BX0']])
            tt('dve', d['BKt'][:], tm[:, 1:3, :], d['En'][:, 0:64].unsqueeze(1).to_broadcast([64, 2, 64]), ALU.mult, [Bin, BE], [d['BBKt']])
            tt('pool', d['AR'][:, 0:64], fm[:, 0, :], d['Ev'][:, 64:128], ALU.mult, [Bin, BE], [d['BAR']])
            tt('dve', d['AR'][:, 64:128], fm[:, 1, :], d['Ep'][:, 64:128], ALU.mult, [Bin, BE], [d['BAR']])
            tt('pool', d['BKf'][:], fm[:, 2:4, :], d['En'][:, 64:128].unsqueeze(1).to_broadcast([64, 2, 64]), ALU.mult, [Bin, BE], [d['BBKf']])
            yield
            mm(P[0:64, bk1, 192:320], d['BKf'][:, 0, :], d['AR'][:], [d['BBKf'], d['BAR']], [PB[bk1]])
            mm(P[0:64, bk1, 320:448], d['BKf'][:, 1, :], d['AR'][:], [d['BBKf'], d['BAR']], [PB[bk1]])
            mm(P[0:64, bk1, 448:512], d['AR'][:, 0:64], d['BKf'][:, 0, :], [d['BBKf'], d['BAR']], [PB[bk1]])
            yield
            tt('dve', d['M'][:], P[0:64, bk1, 192:512], maskM, ALU.mult, [PB[bk1], B_c], [d['BM']])
            yield
            M = d['M']
            Mbr, Mka, Mkr = M[:, 64:128], M[:, 128:192], M[:, 192:256]
            mm(P[0:64, bk0, 128:192], Mka, tm[:, 3, :], [d['BM'], Bin], [PB[bk0]])
            yield
            act(X0[:, 64:128], P[0:64, bk0, 128:192], AF.Copy, [PB[bk0]], [d['BX0']])
            yield
            Nk = M[:, 0:64]
            NkT = M[:, 256:320]
            BNk = d['BM']
            xi = 0
            for k in range(6):
                Xc, Xn = d['X'][xi], d['X'][1 - xi]
                BXc, BXn = d['BX%d' % xi], d['BX%d' % (1 - xi)]
                mm(P[0:64, bk2, 0:128], Nk, Xc[:], [BNk, BXc], [PB[bk2]])
                yield
                tt('dve', Xn[:], P[0:64, bk2, 0:128], Xc[:], ALU.add, [PB[bk2], BXc], [BXn])
                yield
                xi = 1 - xi
                if k < 5:
                    NNn = d['NN'][k % 2]
                    BNn = d['BNN%d' % (k % 2)]
                    mm(P[0:64, bk2, 128:192], NkT, Nk, [BNk], [PB[bk2]])
                    mm(P[0:64, bk2, 192:256], Nk, NkT, [BNk], [PB[bk2]])
                    yield
                    act(NNn[:], P[0:64, bk2, 128:256], AF.Copy, [PB[bk2]], [BNn])
                    yield
                    Nk, NkT, BNk = NNn[:, 0:64], NNn[:, 64:128], BNn
            Xf = d['X'][xi]
            BXf = d['BX%d' % xi]
            At, Wt = Xf[:, 0:64], Xf[:, 64:128]
            Bt, Kt = d['BKt'][:, 0, :], d['BKt'][:, 1, :]
            Vt = tm[:, 3, :]
            mm(P[0:64, bk3, 0:64], At, Mbr, [BXf, d['BM']], [PB[bk3]])
            yield
            tt('dve', d['Rhat'][:], P[0:64, bk3, 0:64], d['AR'][:, 64:128], ALU.add, [PB[bk3], d['BAR']], [d['BRhat']])
            yield
            mm(P[0:64, bk3, 64:128], At, Bt, [BXf, d['BBKt']], [PB[bk3]])
            yield
            tt('dve', d['G0'][:], P[0:64, bk3, 64:128], ident, ALU.add, [PB[bk3], B_c], [d['BG0']])
            yield
            mm(P[0:64, bk3, 128:192], Wt, Mbr, [BXf, d['BM']], [PB[bk3]], start=True, stop=False)
            mm(P[0:64, bk3, 128:192], Vt, Mkr, [Bin, d['BM']], [PB[bk3]], start=False, stop=True)
            yield
            act(d['Y0'][:], P[0:64, bk3, 128:192], AF.Copy, [PB[bk3]], [d['BY0']])
            yield
            mm(P[0:64, bk3, 192:256], Bt, Wt, [BXf, d['BBKt']], [PB[bk3]], start=True, stop=False)
            mm(P[0:64, bk3, 192:256], Kt, Vt, [Bin, d['BBKt']], [PB[bk3]], start=False, stop=True)
            yield
            PC = d['Ep'][:, 127:128]
            act(d['HT'][:], P[0:64, bk3, 192:256], AF.Copy, [PB[bk3], BE], [d['BHT']], scale=PC)
            yield
            Sc, Sn = ST[it][c % 2], ST[it][(c + 1) % 2]
            BSc, BSn = B_ST[it][c % 2], B_ST[it][(c + 1) % 2]
            mm(P[0:64, bk3, 256:320], Sc[:], d['Rhat'][:], [BSc, d['BRhat']], [PB[bk3]])
            mm(P[0:64, bk3, 320:384], d['G0'][:], Sc[:], [BSc, d['BG0']], [PB[bk3]])
            yield
            tt('dve', d['ysb'][:], P[0:64, bk3, 256:320], d['Y0'][:], ALU.add, [PB[bk3], d['BY0']], [d['Bysb']])
            (lambda out, in0, scalar, in1, reads, writes: S.op('dve', lambda e: e.scalar_tensor_tensor(
                out=out, in0=in0, scalar=scalar, in1=in1, op0=ALU.mult, op1=ALU.add), reads=reads, writes=writes))(
                Sn[:], P[0:64, bk3, 320:384], PC, d['HT'][:], [PB[bk3], BE, d['BHT']], [BSn])
            S.dma('sp', y_o[it, c], d['ysb'][:], reads=[d['Bysb']])

            yield
        order = [(c, it) for c in range(nch) for it in range(3)]
        import os
        W = int(os.environ.get('SCAN2_W', '6'))
        PF = NSET - W
        nload = [0]

        def ensure_loaded(upto):
            while nload[0] < min(upto, len(order)):
                m = nload[0]
                load(order[m][0], order[m][1], sets[m % NSET])
                nload[0] += 1
        ensure_loaded(W + PF)
        active = []
        nxt = 0
        STAG = int(os.environ.get('SCAN2_STAG', '7'))
        rnd = 0
        last_admit = -10 ** 9
        while nxt < len(order) or active:
            rnd += 1
            if len(active) < W and nxt < len(order) and (rnd - last_admit >= STAG or not active):
                active.append(item_prog(nxt, order[nxt][0], order[nxt][1], sets[nxt % NSET]))
                nxt += 1
                last_admit = rnd
            still = []
            for g in active:
                try:
                    next(g)
                    still.append(g)
                except StopIteration:
                    ensure_loaded(nxt + PF + 1)
            active = still
        S.emit()
    return nc


def scan2_host_inputs(prep, nstep=NSTEP):
    scan_l, scan_c = prep['scan']
    lw_l, lw_c = prep['dec']
    rkv_l, rkv_c = prep['rkv']
    nch = nstep // CH
    t = np.arange(64)
    tri = (t[:, None] <= t[None, :]).astype(np.float32)
    su = (t[:, None] < t[None, :]).astype(np.float32)
    sl = (t[:, None] > t[None, :]).astype(np.float32)
    cst = np.ascontiguousarray(np.stack([tri, su, tri, su, tri, sl, np.eye(64, dtype=np.float32), tri.T], 1))

    def seq(lat, ctx, b, z):
        s = np.concatenate([ctx[b], lat[b]], 0) if z == 0 else np.concatenate([ctx[b][::-1], lat[b][::-1]], 0)
        return s[:nstep]
    maps = []
    for core in range(NCORE):
        tm = np.zeros((3, nch, CH, 4, 64), np.float32)
        fm = np.zeros((3, nch, 64, 4, CH), np.float32)
        lw = np.zeros((3, nch, 64, 2, 64), np.float32)
        for it in range(3):
            item = core * 3 + it
            z, b, h = item // 12, (item // 6) % 2, item % 6
            hs = slice(h * 64, (h + 1) * 64)
            sc = seq(scan_l, scan_c, b, z).reshape(nstep, 6, 384).astype(np.float32)
            a_, r_, b_, k_ = sc[:, 0, hs], sc[:, 1, hs], sc[:, 2 + z, hs], sc[:, 4 + z, hs]
            v_ = seq(rkv_l, rkv_c, b, z)[:, 768 + h * 64:768 + (h + 1) * 64]
            l_ = seq(lw_l, lw_c, b, z).reshape(nstep, 2, 384)[:, z, hs]
            tm[it] = np.stack([a_, b_, k_, v_], 1).reshape(nch, CH, 4, 64)
            fm[it] = np.stack([a_, r_, b_, k_], 1).reshape(nch, CH, 4, 64).transpose(0, 3, 2, 1)
            lc = l_.reshape(nch, CH, 64)
            lw[it] = np.stack([lc, lc.transpose(0, 2, 1)], 2)
        maps.append(dict(tm=tm, fm=fm, lw=lw, cst=cst))
    return maps


def scan2_host_outputs(ys, nstep=NSTEP):
    yl = np.zeros((2, 2, 8192, 384), np.float32)
    yc = np.zeros((2, 2, 256, 384), np.float32)
    nch = nstep // CH
    for core in range(NCORE):
        for it in range(3):
            item = core * 3 + it
            z, b, h = item // 12, (item // 6) % 2, item % 6
            y = np.zeros((NSTEP, 64), np.float32)
            y[:nstep] = ys[core][it].transpose(0, 2, 1).reshape(nstep, 64)
            c_, l_ = y[:256], y[256:]
            if z == 1:
                c_, l_ = c_[::-1], l_[::-1]
            yc[z, b, :, h * 64:(h + 1) * 64] = c_
            yl[z, b, :, h * 64:(h + 1) * 64] = l_
    return yl, yc


def run_scan2(prep, nstep=NSTEP):
    nc = _get(f'scan2_{nstep}', lambda: build_scan2(nstep))
    maps = scan2_host_inputs(prep, nstep)
    res = run_bass_kernel_spmd(nc, maps, core_ids=list(range(NCORE)))
    return scan2_host_outputs([r['y'] for r in res.results], nstep)


def build_attn(nblk=16, with_ctx=True):
    nc = bass.Bass("TRN2", target_bir_lowering=False)
    NQ = nblk * 128
    QT_d = nc.dram_tensor("QT", [64, 6, NQ], BF16, kind="ExternalInput").ap()
    KT_d = nc.dram_tensor("KT", [64, 2, NQ + 256], BF16, kind="ExternalInput").ap()
    KcT_d = nc.dram_tensor("KcT", [64, 2, 256], BF16, kind="ExternalInput").ap()
    V_d = nc.dram_tensor("V", [128, nblk + 2, 2, 65], BF16, kind="ExternalInput").ap()
    Vc_d = nc.dram_tensor("Vc", [128, 2, 2, 65], BF16, kind="ExternalInput").ap()
    mask_d = nc.dram_tensor("mask", [128, 2, 128], BF16, kind="ExternalInput").ap()
    sink_d = nc.dram_tensor("sink", [1, 6], F32, kind="ExternalInput").ap()
    QcT_d = nc.dram_tensor("QcT", [64, 6, 64], BF16, kind="ExternalInput").ap()
    o_d = nc.dram_tensor("o", [NQ, 384], F32, kind="ExternalOutput").ap()
    oc_d = nc.dram_tensor("oc", [64, 384], F32, kind="ExternalOutput").ap()
    with ExitStack() as st:
        S = Sched(nc, st)
        P = st.enter_context(nc.psum_tensor("P", [128, 8, 512], F32))
        PB = [S.buf(f'pb{i}') for i in range(8)]
        QT = S.sb("QT_sb", [64, 6, NQ], BF16)
        KT = S.sb("KT_sb", [64, 2, NQ + 256], BF16)
        KcT = S.sb("KcT_sb", [64, 2, 256], BF16)
        V = S.sb("V_sb", [128, nblk + 2, 2, 65], BF16)
        Vc = S.sb("Vc_sb", [128, 2, 2, 65], BF16)
        mask = S.sb("mask_sb", [128, 2, 128], BF16)
        esink = S.sb("esink", [128, 6], F32)
        QcT = S.sb("QcT_sb", [64, 6, 64], BF16)
        B_in = S.buf('in')
        B_es = S.buf('es')
        for t, d in [(QT, QT_d), (KT, KT_d), (KcT, KcT_d), (V, V_d), (Vc, Vc_d), (mask, mask_d), (QcT, QcT_d)]:
            S.dma('sp', t[:], d, writes=[S.buf()])
        S.dma('sp', esink[:], mkap(sink_d, 0, [(0, 128), (1, 6)]), writes=[B_es])
        S.op('act', lambda e: e.activation(out=esink[:], in_=esink[:], func=AF.Exp), reads=[B_es], writes=[B_es])
        PT = [S.sb(f"PT{i}", [128, 384], BF16) for i in range(5)]
        B_PT = [S.buf(f'PT{i}') for i in range(5)]
        ot = [S.sb(f"ot{i}", [128, 384], F32) for i in range(2)]
        B_ot = [S.buf(f'ot{i}') for i in range(2)]
        den = S.sb("den", [128, 2, 3], F32)
        B_den = [S.buf('den0'), S.buf('den1')]
        it = [0]

        def attn_block(q_ap, nq, chunks, kh, otile, Bot):
            i = it[0]
            it[0] += 1
            nch = len(chunks)
            for c, (kT_ap, v_ap, mi) in enumerate(chunks):
                S.op('pe', lambda e, c=c, kT_ap=kT_ap: e.matmul(P[:, c, 0:3 * nq].rearrange("p (g q) -> p g q", g=3), lhsT=kT_ap, rhs=q_ap,
                                                                  start=True, stop=True),
                     reads=[B_in], writes=[PB[c]])
                S.op('act', lambda e, c=c: e.activation(out=PT[c][:, 0:3 * nq], in_=P[:, c, 0:3 * nq], func=AF.Exp, scale=0.125),
                     reads=[PB[c]], writes=[B_PT[c]])
                if mi is not None:
                    eng = 'dve' if mi == 0 else 'pool'
                    S.op(eng, lambda e, c=c, mi=mi: e.tensor_tensor(out=PT[c][:, 0:3 * nq].rearrange("p (g q) -> p g q", g=3),
                                                                     in0=PT[c][:, 0:3 * nq].rearrange("p (g q) -> p g q", g=3),
                                                                     in1=mask[:, mi, 0:nq].unsqueeze(1).to_broadcast([128, 3, nq]), op=ALU.mult),
                         reads=[B_in], writes=[B_PT[c]])
            ob = 5 + (i % 2)
            for g in range(3):
                for c, (kT_ap, v_ap, mi) in enumerate(chunks):
                    S.op('pe', lambda e, c=c, g=g, v_ap=v_ap: e.matmul(P[0:nq, ob, g * 65:(g + 1) * 65], lhsT=PT[c][:, g * nq:(g + 1) * nq], rhs=v_ap,
                                                                     start=(c == 0), stop=(c == nch - 1)),
                         reads=[B_PT[c], B_in], writes=[PB[ob]])
            ov = P[0:nq, ob, 0:195].rearrange("p (g e) -> p g e", g=3)
            dn = den[0:nq, i % 2, :]
            S.op('dve', lambda e: e.tensor_tensor(out=dn, in0=ov[:, :, 64], in1=esink[0:nq, kh * 3:kh * 3 + 3], op=ALU.add),
                 reads=[PB[ob], B_es], writes=[B_den[i % 2]])
            S.op('dve', lambda e: e.reciprocal(out=dn, in_=dn), reads=[B_den[i % 2]], writes=[B_den[i % 2]])
            S.op('dve', lambda e: e.tensor_tensor(out=otile[0:nq, kh * 192:(kh + 1) * 192].rearrange("p (g e) -> p g e", g=3), in0=ov[:, :, 0:64],
                                                  in1=dn.unsqueeze(2).to_broadcast([nq, 3, 64]), op=ALU.mult),
                 reads=[PB[ob], B_den[i % 2]], writes=[Bot])

        all_in = [o for o in S.ops['sp']]
        bo = S.op('pool', lambda e: e.memset(den[:].rearrange("p a b -> p (a b)"), 0.0), writes=[B_in, B_den[0], B_den[1]])
        bo.deps.extend(all_in)
        for n in range(nblk):
            otile = ot[n % 2]
            Bot = B_ot[n % 2]
            for kh in range(2):
                chunks = []
                for j, mi in [(0, 0), (1, None), (2, 1)]:
                    chunks.append((KT[:, kh, (n + j) * 128:(n + j + 1) * 128], V[:, n + j, kh, :], mi))
                for j in range(2):
                    chunks.append((KcT[:, kh, j * 128:(j + 1) * 128], Vc[:, j, kh, :], None))
                attn_block(QT[:, kh * 3:kh * 3 + 3, n * 128:(n + 1) * 128], 128, chunks, kh, otile, Bot)
            S.dma('sp', o_d[n * 128:(n + 1) * 128, :], otile[:], reads=[Bot])
        if with_ctx:
            otile = ot[nblk % 2]
            Bot = B_ot[nblk % 2]
            for kh in range(2):
                chunks = [(KcT[:, kh, j * 128:(j + 1) * 128], Vc[:, j, kh, :], None) for j in range(2)]
                attn_block(QcT[:, kh * 3:kh * 3 + 3, :], 64, chunks, kh, otile, Bot)
            S.dma('sp', oc_d, otile[0:64, :], reads=[Bot])
        S.emit()
    return nc


def run_attn(prep, sink):
    import ml_dtypes
    bf = ml_dtypes.bfloat16
    nc = _get('attn', build_attn)
    qk_l, qk_c = prep['qk']
    v_l, v_c = prep['v']
    kk = np.arange(128)[:, None]
    qq = np.arange(128)[None, :]
    mask = np.stack([(kk >= qq), (kk <= qq)], 1).astype(np.float32).astype(bf)
    maps = []
    for core in range(NCORE):
        b, q = core // 4, core % 4
        s0 = q * TL
        QT = np.ascontiguousarray(qk_l[b, s0:s0 + TL, 0:384].reshape(TL, 6, 64).transpose(2, 1, 0))
        kpad = np.zeros((8192 + 256, 128), bf)
        kpad[128:128 + 8192] = qk_l[b, :, 384:512]
        KT = np.ascontiguousarray(kpad[s0:s0 + TL + 256].reshape(TL + 256, 2, 64).transpose(2, 1, 0))
        KcT = np.ascontiguousarray(qk_c[b, :, 384:512].reshape(256, 2, 64).transpose(2, 1, 0))
        vpad = np.zeros((8192 + 256, 2, 65), bf)
        vpad[128:128 + 8192, :, 0:64] = v_l[b].reshape(8192, 2, 64)
        vpad[128:128 + 8192, :, 64] = 1.0
        V = np.ascontiguousarray(vpad[s0:s0 + TL + 256].reshape(18, 128, 2, 65).transpose(1, 0, 2, 3))
        vc = np.ones((256, 2, 65), bf)
        vc[:, :, 0:64] = v_c[b].reshape(256, 2, 64)
        Vc = np.ascontiguousarray(vc.reshape(2, 128, 2, 65).transpose(1, 0, 2, 3))
        QcT = np.ascontiguousarray(qk_c[b, q * TC:(q + 1) * TC, 0:384].reshape(TC, 6, 64).transpose(2, 1, 0))
        maps.append(dict(QT=QT, KT=KT, KcT=KcT, V=V, Vc=Vc, mask=mask, sink=np.ascontiguousarray(sink.reshape(1, 6).astype(np.float32)), QcT=QcT))
    res = run_bass_kernel_spmd(nc, maps, core_ids=list(range(NCORE)))
    al = np.zeros((2, 8192, 384), np.float32)
    ac = np.zeros((2, 256, 384), np.float32)
    for core in range(NCORE):
        b, q = core // 4, core % 4
        al[b, q * TL:(q + 1) * TL] = res.results[core]['o']
        ac[b, q * TC:(q + 1) * TC] = res.results[core]['oc']
    return al, ac


def build_fourier(with_ctx=True):
    nc = bass.Bass("TRN2", target_bir_lowering=False)
    x0_d = nc.dram_tensor("x0", [64, 64, 128], F32, kind="ExternalInput").ap()
    xc_d = nc.dram_tensor("xc0", [64, 256], F32, kind="ExternalInput").ap()
    cs64_d = nc.dram_tensor("cs64", [64, 128], F32, kind="ExternalInput").ap()
    f128_d = nc.dram_tensor("f128", [128, 3, 128], F32, kind="ExternalInput").ap()
    tw_d = nc.dram_tensor("tw", [128, 2, 64], F32, kind="ExternalInput").ap()
    f64_d = nc.dram_tensor("f64", [64, 2, 64], F32, kind="ExternalInput").ap()
    f256_d = nc.dram_tensor("f256", [128, 2, 2, 256], F32, kind="ExternalInput").ap()
    scr = nc.dram_tensor("scr", [128, 64, 128], F32, kind="ExternalOutput").ap()
    y_d = nc.dram_tensor("y", [64, 128 * 64], F32, kind="ExternalOutput").ap()
    yc_d = nc.dram_tensor("yc", [256, 64], F32, kind="ExternalOutput").ap()
    with ExitStack() as st:
        S = Sched(nc, st)
        P = st.enter_context(nc.psum_tensor("P", [128, 8, 512], F32))
        PB = [S.buf(f'pb{i}') for i in range(8)]
        X0 = S.sb("X0", [64, 64 * 128], F32)
        X1 = S.sb("X1", [128, 64, 128], F32)
        B2 = S.sb("B2", [64, 128, 128], F32)
        cs64 = S.sb("cs64_sb", [64, 128], F32)
        f128 = S.sb("f128_sb", [128, 3, 128], F32)
        tw = S.sb("tw_sb", [128, 2, 64], F32)
        f64 = S.sb("f64_sb", [64, 2, 64], F32)
        f256 = S.sb("f256_sb", [128, 2, 2, 256], F32)
        xc = S.sb("xc_sb", [64, 256], F32)
        zc = S.sb("zc_sb", [128, 2, 128], F32)
        ycs = S.sb("ycs", [128, 2, 64], F32)
        Bt = [S.sb(f"Bt{i}", [128, 8, 2, 64], F32) for i in range(2)]
        tmp = [S.sb(f"ftmp{i}", [128, 8, 64], F32) for i in range(2)]
        B_X0, B_X1, B_B2, B_c, B_scr = S.buf('X0'), S.buf('X1'), S.buf('B2'), S.buf('c'), S.buf('scr')
        B_Bt = [S.buf('Bt0'), S.buf('Bt1')]
        B_tmp = [S.buf('tmp0'), S.buf('tmp1')]
        B_xc, B_zc, B_yc = S.buf('xc'), S.buf('zc'), S.buf('yc')
        S.dma('sp', X0[:], x0_d.rearrange("c a b -> c (a b)"), writes=[B_X0])
        for t, d in [(cs64, cs64_d), (f128, f128_d), (tw, tw_d), (f64, f64_d), (f256, f256_d)]:
            S.dma('sp', t[:], d, writes=[B_c])
        S.dma('sp', xc[:], xc_d, writes=[B_xc])
        for grp in range(16):
            bank = grp % 2
            for j in range(4):
                n2 = grp * 4 + j
                S.op('pe', lambda e, n2=n2, j=j, bank=bank: e.matmul(P[:, bank, j * 128:(j + 1) * 128], lhsT=X0[:, n2 * 128:(n2 + 1) * 128], rhs=cs64[:],
                                                                    start=True, stop=True),
                     reads=[B_X0, B_c], writes=[PB[bank]])
            dst = X1[:, grp * 4:(grp + 1) * 4, :].rearrange("p a b -> p (a b)")
            if grp % 2 == 0:
                S.op('act', lambda e, dst=dst, bank=bank: e.activation(func=AF.Copy, out=dst, in_=P[:, bank, :]), reads=[PB[bank]], writes=[B_X1])
            else:
                S.op('dve', lambda e, dst=dst, bank=bank: e.tensor_copy(out=dst, in_=P[:, bank, :]), reads=[PB[bank]], writes=[B_X1])
        for ch in range(8):
            zr = X1[:, ch * 8:(ch + 1) * 8, 0:64]
            zi = X1[:, ch * 8:(ch + 1) * 8, 64:128]
            ba = 2 + (ch % 2) * 2
            ar = P[:, ba, :].rearrange("p (a b) -> p a b", a=8)
            ai = P[:, ba + 1, :].rearrange("p (a b) -> p a b", a=8)
            S.op('pe', lambda e, ar=ar, zr=zr: e.matmul(ar, lhsT=f128[:, 0, :], rhs=zr, start=True, stop=False), reads=[B_X1, B_c], writes=[PB[ba]])
            S.op('pe', lambda e, ar=ar, zi=zi: e.matmul(ar, lhsT=f128[:, 1, :], rhs=zi, start=False, stop=True), reads=[B_X1, B_c], writes=[PB[ba]])
            S.op('pe', lambda e, ai=ai, zi=zi: e.matmul(ai, lhsT=f128[:, 0, :], rhs=zi, start=True, stop=False), reads=[B_X1, B_c], writes=[PB[ba + 1]])
            S.op('pe', lambda e, ai=ai, zr=zr: e.matmul(ai, lhsT=f128[:, 2, :], rhs=zr, start=False, stop=True), reads=[B_X1, B_c], writes=[PB[ba + 1]])
            tc_ = tw[:, 0, ch * 8:(ch + 1) * 8].unsqueeze(2).to_broadcast([128, 8, 64])
            ts_ = tw[:, 1, ch * 8:(ch + 1) * 8].unsqueeze(2).to_broadcast([128, 8, 64])
            bt = Bt[ch % 2]
            Bb = B_Bt[ch % 2]
            t0_, t1_ = tmp
            S.op('dve', lambda e, bt=bt, ar=ar, tc_=tc_: e.tensor_tensor(out=bt[:, :, 0, :], in0=ar, in1=tc_, op=ALU.mult), reads=[PB[ba], B_c], writes=[Bb])
            S.op('dve', lambda e, ai=ai, ts_=ts_: e.tensor_tensor(out=t0_[:], in0=ai, in1=ts_, op=ALU.mult), reads=[PB[ba + 1], B_c], writes=[B_tmp[0]])
            S.op('dve', lambda e, bt=bt, ai=ai, tc_=tc_: e.tensor_tensor(out=bt[:, :, 1, :], in0=ai, in1=tc_, op=ALU.mult), reads=[PB[ba + 1], B_c], writes=[Bb])
            S.op('dve', lambda e, ar=ar, ts_=ts_: e.tensor_tensor(out=t1_[:], in0=ar, in1=ts_, op=ALU.mult), reads=[PB[ba], B_c], writes=[B_tmp[1]])
            S.op('pool', lambda e, bt=bt: e.tensor_tensor(out=bt[:, :, 0, :], in0=bt[:, :, 0, :], in1=t0_[:], op=ALU.add), reads=[B_tmp[0]], writes=[Bb])
            S.op('pool', lambda e, bt=bt: e.tensor_tensor(out=bt[:, :, 1, :], in0=bt[:, :, 1, :], in1=t1_[:], op=ALU.subtract), reads=[B_tmp[1]], writes=[Bb])
            S.dma('sp', scr[:, ch * 8:(ch + 1) * 8, :], bt[:].rearrange("p a b c -> p a (b c)"), reads=[Bb], writes=[B_scr], sembuf=Bb)
        for q in range(4):
            S.dma('sp', B2[:, q * 32:(q + 1) * 32, :], scr[q * 32:(q + 1) * 32, :, :].rearrange("k n c -> n k c"), reads=[B_scr], writes=[B_B2])
        Y = X0
        for ch in range(16):
            bank = 6 + (ch % 2)
            br = B2[:, ch * 8:(ch + 1) * 8, 0:64]
            bi = B2[:, ch * 8:(ch + 1) * 8, 64:128]
            ov = P[0:64, bank, :].rearrange("p (a b) -> p a b", a=8)
            S.op('pe', lambda e, ov=ov, br=br: e.matmul(ov, lhsT=f64[:, 0, :], rhs=br, start=True, stop=False), reads=[B_B2, B_c], writes=[PB[bank]])
            S.op('pe', lambda e, ov=ov, bi=bi: e.matmul(ov, lhsT=f64[:, 1, :], rhs=bi, start=False, stop=True), reads=[B_B2, B_c], writes=[PB[bank]])
            if ch % 2 == 0:
                S.op('act', lambda e, ch=ch, bank=bank: e.activation(func=AF.Copy, out=Y[:, ch * 512:(ch + 1) * 512], in_=P[0:64, bank, :]), reads=[PB[bank]], writes=[B_X0])
            else:
                S.op('dve', lambda e, ch=ch, bank=bank: e.tensor_copy(out=Y[:, ch * 512:(ch + 1) * 512], in_=P[0:64, bank, :]), reads=[PB[bank]], writes=[B_X0])
        S.dma('sp', y_d, Y[:], reads=[B_X0])
        if with_ctx:
            for nt in range(2):
                S.op('pe', lambda e, nt=nt: e.matmul(P[:, 0, nt * 128:(nt + 1) * 128], lhsT=xc[:, nt * 128:(nt + 1) * 128], rhs=cs64[:], start=True, stop=True),
                     reads=[B_xc, B_c], writes=[PB[0]])
            S.op('act', lambda e: e.activation(func=AF.Copy, out=zc[:].rearrange("p a b -> p (a b)"), in_=P[:, 0, 0:256]), reads=[PB[0]], writes=[B_zc])
            for kt in range(2):
                ov = P[:, 1, kt * 64:(kt + 1) * 64]
                first = True
                for nt in range(2):
                    S.op('pe', lambda e, ov=ov, nt=nt, kt=kt, first=first: e.matmul(ov, lhsT=f256[:, nt, 0, kt * 128:(kt + 1) * 128], rhs=zc[:, nt, 0:64],
                                                                                  start=first, stop=False), reads=[B_zc, B_c], writes=[PB[1]])
                    first = False
                    S.op('pe', lambda e, ov=ov, nt=nt, kt=kt: e.matmul(ov, lhsT=f256[:, nt, 1, kt * 128:(kt + 1) * 128], rhs=zc[:, nt, 64:128],
                                                                     start=False, stop=(nt == 1)), reads=[B_zc, B_c], writes=[PB[1]])
            S.op('act', lambda e: e.activation(func=AF.Copy, out=ycs[:].rearrange("p a b -> p (a b)"), in_=P[:, 1, 0:128]), reads=[PB[1]], writes=[B_yc])
            S.dma('sp', yc_d.rearrange("(kt p) d -> p kt d", p=128), ycs[:], reads=[B_yc])
        S.emit()
    return nc


def fourier_tables():
    f64_ = np.float64

    def cs(n):
        i = np.arange(n)
        ang = 2 * np.pi * np.outer(i, i) / n
        return np.cos(ang), np.sin(ang)
    c64, s64 = cs(64)
    c128, s128 = cs(128)
    c256, s256 = cs(256)
    cs64 = np.concatenate([c64, -s64], 1).astype(np.float32)
    f128 = np.stack([c128, s128, -s128], 1).astype(np.float32)
    k1 = np.arange(128)[:, None]
    n2 = np.arange(64)[None, :]
    ang = 2 * np.pi * k1 * n2 / 8192.0
    tw = np.stack([np.cos(ang), np.sin(ang)], 1).astype(np.float32)
    sc = 1.0 / np.sqrt(8192.0 * 64.0)
    f64t = np.stack([c64 * sc, s64 * sc], 1).astype(np.float32)
    scc = 1.0 / np.sqrt(256.0 * 64.0)
    f256 = np.stack([c256 * scc, s256 * scc], 1).astype(np.float32)
    f256 = np.ascontiguousarray(f256.reshape(2, 128, 2, 256).transpose(1, 0, 2, 3))
    return dict(cs64=cs64, f128=f128, tw=tw, f64=f64t, f256=f256)


def run_fourier(fl, fc):
    nc = _get('fourier', build_fourier)
    tabs = fourier_tables()
    maps = []
    for core in range(NCORE):
        b, g = core // 4, core % 4
        xg = fl[b, :, g * 64:(g + 1) * 64]
        x0 = np.ascontiguousarray(xg.reshape(128, 64, 64).transpose(2, 1, 0))
        xc0 = np.ascontiguousarray(fc[b, :, g * 64:(g + 1) * 64].T)
        m = dict(x0=x0, xc0=xc0)
        m.update(tabs)
        maps.append(m)
    res = run_bass_kernel_spmd(nc, maps, core_ids=list(range(NCORE)))
    ol = np.zeros((2, 8192, 256), np.float32)
    oc = np.zeros((2, 256, 256), np.float32)
    for core in range(NCORE):
        b, g = core // 4, core % 4
        ol[b, :, g * 64:(g + 1) * 64] = res.results[core]['y'].reshape(8192, 64)
        oc[b, :, g * 64:(g + 1) * 64] = res.results[core]['yc']
    return ol, oc


def kernel(x, c, ctx, c_ctx, mod_w, mod_b, norm_g, ffn1_wi, ffn1_wo, mix_w_in, mix_w_out, attn_sink,
           rwkv_conv, rwkv_w0, rwkv_w2, rwkv_a0, rwkv_a2, rwkv_g2, rwkv_k_k, rwkv_k_a, rwkv_r_k,
           rwkv_ln_g, rwkv_ln_b, ffn2_wi, ffn2_wo):
    f = lambda a: np.asarray(a, dtype=np.float32)
    x, c, ctx, c_ctx = f(x), f(c), f(ctx), f(c_ctx)
    xs = shard_tokens(x, ctx)
    for li in range(2):
        mw, mb, g = f(mod_w[li]), f(mod_b[li]), f(norm_g[li])
        xs = run_ffn(xs, c, c_ctx, mw[:, 0:3 * D], mb[0:3 * D], g[0], g[1], f(ffn1_wi[li]), f(ffn1_wo[li]))
        zs = run_inproj(xs, c, c_ctx, mw[:, 3 * D:5 * D], mb[3 * D:5 * D], g[2], f(mix_w_in[li]))
        zl, zc = unshard_tokens(zs, INW)
        prep = run_prep(zl, zc, f(rwkv_conv[li]), f(rwkv_w0[li]), f(rwkv_w2[li]), f(rwkv_a0[li]), f(rwkv_a2[li]),
                        f(rwkv_k_k[li]), f(rwkv_k_a[li]))
        al, ac = run_attn(prep, f(attn_sink[li]))
        fl, fc = run_fourier(np.ascontiguousarray(zl[..., 0:256]), np.ascontiguousarray(zc[..., 0:256]))
        yl, yc = run_scan2(prep)
        fa = shard_tokens(np.concatenate([fl, al], -1), np.concatenate([fc, ac], -1))
        yf = shard_tokens(yl[0], yc[0])
        yb = shard_tokens(yl[1], yc[1])
        rkv = shard_tokens(prep['rkv'][0], prep['rkv'][1])
        zg = shard_tokens(zl[..., 2304:2432], zc[..., 2304:2432])
        vecs = np.stack([f(rwkv_r_k[li]).reshape(384), f(rwkv_ln_g[li]), f(rwkv_ln_b[li])], 0)
        xs = run_outproj(xs, c, c_ctx, mw[:, 5 * D:6 * D], mb[5 * D:6 * D], g[3], f(mix_w_out[li]), fa, yf, yb, rkv, zg,
                         f(rwkv_g2[li]), vecs)
        xs = run_ffn(xs, c, c_ctx, mw[:, 6 * D:9 * D], mb[6 * D:9 * D], g[4], g[5], f(ffn2_wi[li]), f(ffn2_wo[li]))
    xl, _ = unshard_tokens(xs)
    return xl
```
